# Optimizing a Trainium2 kernel written in Bass

```python
import math
import jax, jax.numpy as jnp
from jax import lax
import numpy as np


D_MODEL = 1024
BATCH = 4
SEQ = 8192
DEPTH = 1

HEAD_DIM = 64
N_HEADS_A = 8
N_HEADS_B = 8
DIL_PATTERNS = ((128, 1), (512, 4), (2048, 16))
BAND_BLOCK = 128
ROT_DIM = HEAD_DIM // 4
ROPE_THETA = 500000.0
NSA_GROUPS = 2
NSA_HPG = N_HEADS_B // NSA_GROUPS
CMP_LEN = 32
CMP_STRIDE = 16
CMP_HIDDEN = 4 * HEAD_DIM
SLC_BLOCK = 64
SLC_TOPK = 16
WIN = 512
Q_CHUNK = 128
D_FF = 2816
CONV_W = 3
ALPHA = (2 * DEPTH) ** 0.25
BETA = (8 * DEPTH) ** -0.25
LN_EPS = 1e-5
NEG = -1e30
MIX_WIDTH = (N_HEADS_A + N_HEADS_B) * HEAD_DIM
IN_SPLITS = (N_HEADS_A * HEAD_DIM,) * 3 + (N_HEADS_B * HEAD_DIM,) + (NSA_GROUPS * HEAD_DIM,) * 6 + (N_HEADS_B * 3,)
IN_WIDTH = sum(IN_SPLITS)

kernel_name = 'hybrid_dilated_nsa_convffn'


def layer_norm(x, g=None, b=None):
    xf = x.astype(jnp.float32)
    mu = jnp.mean(xf, -1, keepdims=True)
    var = jnp.mean(jnp.square(xf - mu), -1, keepdims=True)
    y = (xf - mu) * lax.rsqrt(var + LN_EPS)
    if g is not None:
        y = y * g.astype(jnp.float32) + b.astype(jnp.float32)
    return y.astype(x.dtype)


def partial_rope(t, pos):
    inv = 1.0 / (ROPE_THETA ** (jnp.arange(0, ROT_DIM, 2, dtype=jnp.float32) / ROT_DIM))
    ang = pos.astype(jnp.float32)[:, None] * inv[None, :]
    cos, sin = jnp.cos(ang), jnp.sin(ang)
    half = ROT_DIM // 2
    t1 = t[..., :half].astype(jnp.float32)
    t2 = t[..., half:ROT_DIM].astype(jnp.float32)
    rot = jnp.concatenate([t1 * cos - t2 * sin, t1 * sin + t2 * cos], -1).astype(t.dtype)
    return jnp.concatenate([rot, t[..., ROT_DIM:]], -1)


def masked_exp(s, mask, axis):
    s = jnp.where(mask, s, NEG)
    m = jnp.max(s, axis=axis, keepdims=True)
    p = jnp.where(mask, jnp.exp(s - m), 0.0)
    return p, m, jnp.sum(p, axis=axis, keepdims=True)


def band_attention(q, k, v, reach):
    L, hd = q.shape[-2], q.shape[-1]
    n = -(-L // BAND_BLOCK)
    lead = q.shape[:-2]
    padq = [(0, 0)] * len(lead) + [(0, n * BAND_BLOCK - L), (0, 0)]
    padk = [(0, 0)] * len(lead) + [(BAND_BLOCK, n * BAND_BLOCK - L), (0, 0)]
    qb = jnp.pad(q, padq).reshape(*lead, n, BAND_BLOCK, hd)

    def kv_blocks(t):
        t = jnp.pad(t, padk).reshape(*lead, n + 1, BAND_BLOCK, hd)
        return jnp.concatenate([t[..., :-1, :, :], t[..., 1:, :, :]], axis=-2)

    kb, vb = kv_blocks(k), kv_blocks(v)
    s = jnp.einsum('...nqd,...nkd->...nqk', qb, kb).astype(jnp.float32) * (hd ** -0.5)
    blk = jnp.arange(n)[:, None, None] * BAND_BLOCK
    qpos = blk + jnp.arange(BAND_BLOCK)[None, :, None]
    kpos = blk - BAND_BLOCK + jnp.arange(2 * BAND_BLOCK)[None, None, :]
    dist = qpos - kpos
    mask = (dist >= 0) & (dist <= reach) & (kpos >= 0)
    p, m, den = masked_exp(s, mask, -1)
    o = jnp.einsum('...nqk,...nkd->...nqd', p, vb) / den
    lse = (m + jnp.log(den))[..., 0]
    o = o.reshape(*lead, n * BAND_BLOCK, hd)[..., :L, :]
    lse = lse.reshape(*lead, n * BAND_BLOCK)[..., :L]
    return o, lse


def dilated_attention(q, k, v):
    B, H, S, hd = q.shape
    outs, lses = [], []
    for window, dil in DIL_PATTERNS:
        L = -(-S // dil)

        def by_residue(t):
            t = jnp.pad(t, ((0, 0), (0, 0), (0, L * dil - S), (0, 0)))
            return t.reshape(B, H, L, dil, hd).transpose(0, 1, 3, 2, 4)

        o, lse = band_attention(by_residue(q), by_residue(k), by_residue(v), window // dil)
        outs.append(o.transpose(0, 1, 3, 2, 4).reshape(B, H, L * dil, hd)[:, :, :S])
        lses.append(lse.transpose(0, 1, 3, 2).reshape(B, H, L * dil)[:, :, :S])
    w = jax.nn.softmax(jnp.stack(lses, 0), axis=0)[..., None]
    return jnp.sum(w * jnp.stack(outs, 0), axis=0).astype(q.dtype)


def compress(t, pe, w1, w2):
    B, G, S, hd = t.shape
    n = S // CMP_STRIDE
    r = CMP_LEN // CMP_STRIDE
    chunks = t.reshape(B, G, n, CMP_STRIDE, hd)
    blocks = jnp.concatenate([chunks[:, :, j:n - r + 1 + j] for j in range(r)], axis=3)
    blocks = (blocks + pe).reshape(B, G, n - r + 1, CMP_LEN * hd)
    return jax.nn.gelu(blocks @ w1) @ w2


def nsa_attention(q, kc, vc, ks, vs, kw, vw, gate_logits, pe, w_ck1, w_ck2, w_cv1, w_cv2):
    B, Hb, S, hd = q.shape
    G, HG = NSA_GROUPS, NSA_HPG
    scale = hd ** -0.5
    qg = q.reshape(B, G, HG, S, hd)
    gates = jax.nn.sigmoid(gate_logits.astype(jnp.float32)).reshape(B, S, G, HG, 3).transpose(0, 2, 3, 1, 4)
    kcc = compress(kc, pe, w_ck1, w_ck2)
    vcc = compress(vc, pe, w_cv1, w_cv2)
    NC = kcc.shape[2]
    NS = S // SLC_BLOCK
    topk = min(SLC_TOPK, NS)
    cmp_start = jnp.arange(NC) * CMP_STRIDE
    cmp_end = cmp_start + CMP_LEN - 1
    sel_start = jnp.arange(NS) * SLC_BLOCK
    overlap = jnp.clip(jnp.minimum(cmp_start[:, None] + CMP_LEN, sel_start[None, :] + SLC_BLOCK)
                       - jnp.maximum(cmp_start[:, None], sel_start[None, :]), 0, None).astype(jnp.float32) / CMP_LEN
    ksb = ks.reshape(B, G, NS, SLC_BLOCK, hd)
    vsb = vs.reshape(B, G, NS, SLC_BLOCK, hd)
    kw_pad = jnp.pad(kw, ((0, 0), (0, 0), (WIN, 0), (0, 0)))
    vw_pad = jnp.pad(vw, ((0, 0), (0, 0), (WIN, 0), (0, 0)))
    bi = jnp.arange(B)[:, None, None, None]
    gi = jnp.arange(G)[None, :, None, None]
    blk_ids = jnp.arange(NS)
    win_off = jnp.arange(Q_CHUNK + WIN) - WIN

    def chunk(c0):
        t = c0 + jnp.arange(Q_CHUNK)
        qc = lax.dynamic_slice_in_dim(qg, c0, Q_CHUNK, axis=3)
        gc = lax.dynamic_slice_in_dim(gates, c0, Q_CHUNK, axis=3)
        s = jnp.einsum('bghqd,bgnd->bghqn', qc, kcc).astype(jnp.float32) * scale
        p, _, den = masked_exp(s, cmp_end[None, :] <= t[:, None], -1)
        p = p / jnp.where(den > 0, den, 1.0)
        o_cmp = jnp.einsum('bghqn,bgnd->bghqd', p, vcc)
        imp = jnp.einsum('bghqn,ns->bgqs', p, overlap)
        cur = (t // SLC_BLOCK)[:, None]
        forced = (blk_ids[None] == cur) | (blk_ids[None] == cur - 1) | (blk_ids[None] == 0)
        valid = blk_ids[None] * SLC_BLOCK <= t[:, None]
        score = jnp.where(forced, 1e9, jnp.where(valid, imp, -1e9))
        _, idx = lax.top_k(score, topk)
        kg = ksb[bi, gi, idx]
        vg = vsb[bi, gi, idx]
        kpos = idx[..., None] * SLC_BLOCK + jnp.arange(SLC_BLOCK)
        smask = (kpos <= t[None, None, :, None, None])[:, :, None]
        s = jnp.einsum('bghqd,bgqkld->bghqkl', qc, kg).astype(jnp.float32) * scale
        p, _, den = masked_exp(s, smask, (-2, -1))
        o_slc = jnp.einsum('bghqkl,bgqkld->bghqd', p, vg) / den[..., 0]
        kwc = lax.dynamic_slice_in_dim(kw_pad, c0, Q_CHUNK + WIN, axis=2)
        vwc = lax.dynamic_slice_in_dim(vw_pad, c0, Q_CHUNK + WIN, axis=2)
        kpos_w = c0 + win_off
        dist = t[:, None] - kpos_w[None, :]
        wmask = (dist >= 0) & (dist < WIN) & (kpos_w[None, :] >= 0)
        s = jnp.einsum('bghqd,bgkd->bghqk', qc, kwc).astype(jnp.float32) * scale
        p, _, den = masked_exp(s, wmask, -1)
        o_win = jnp.einsum('bghqk,bgkd->bghqd', p, vwc) / den
        o = gc[..., 0:1] * o_cmp + gc[..., 1:2] * o_slc + gc[..., 2:3] * o_win
        return o.astype(q.dtype)

    outs = lax.map(chunk, jnp.arange(S // Q_CHUNK) * Q_CHUNK)
    return outs.transpose(1, 0, 4, 2, 3, 5).reshape(B, S, Hb * hd)


def causal_dwconv(a, w, b):
    F = a.shape[-1]
    y = lax.conv_general_dilated(a, w[:, None, :], window_strides=(1,), padding=[(CONV_W - 1, 0)],
                                 dimension_numbers=('NWC', 'WIO', 'NWC'), feature_group_count=F)
    return y + b


def setup_inputs(seed: int = 0) -> dict:
    key = jax.random.key(seed)
    ks = jax.random.split(key, 20)
    L, D, hd, F = DEPTH, D_MODEL, HEAD_DIM, D_FF

    def n(k, shape, s):
        return jax.random.normal(k, shape, jnp.float32) * s

    return {
        'x': n(ks[0], (BATCH, SEQ, D), 1.0),
        'c': n(ks[1], (BATCH, D), 1.0),
        'w_ada': n(ks[2], (L, D, 6 * D), 0.5 * D ** -0.5),
        'b_ada': n(ks[3], (L, 6 * D), 0.01),
        'w_in': n(ks[4], (L, D, IN_WIDTH), D ** -0.5),
        'pe_cmp': n(ks[5], (L, CMP_LEN, hd), 0.02),
        'w_ck1': n(ks[6], (L, CMP_LEN * hd, CMP_HIDDEN), (CMP_LEN * hd) ** -0.5),
        'w_ck2': n(ks[7], (L, CMP_HIDDEN, hd), CMP_HIDDEN ** -0.5),
        'w_cv1': n(ks[8], (L, CMP_LEN * hd, CMP_HIDDEN), (CMP_LEN * hd) ** -0.5),
        'w_cv2': n(ks[9], (L, CMP_HIDDEN, hd), CMP_HIDDEN ** -0.5),
        'w_o': n(ks[10], (L, MIX_WIDTH, D), BETA * MIX_WIDTH ** -0.5),
        'ln1_g': 1.0 + n(ks[11], (L, D), 0.01),
        'ln1_b': n(ks[12], (L, D), 0.01),
        'w_up': n(ks[13], (L, D, 2 * F), D ** -0.5),
        'conv_w': n(ks[14], (L, CONV_W, F), CONV_W ** -0.5),
        'conv_b': n(ks[15], (L, F), 0.01),
        'w_down': n(ks[16], (L, F, D), BETA * F ** -0.5),
        'ln2_g': 1.0 + n(ks[17], (L, D), 0.01),
        'ln2_b': n(ks[18], (L, D), 0.01),
    }


def reference(x, c, w_ada, b_ada, w_in, pe_cmp, w_ck1, w_ck2, w_cv1, w_cv2, w_o, ln1_g, ln1_b,
              w_up, conv_w, conv_b, w_down, ln2_g, ln2_b):
    B, S, D = x.shape
    hd = HEAD_DIM
    pos = jnp.arange(S)
    split_at = np.cumsum(IN_SPLITS)[:-1].tolist()

    def heads(t, nh):
        return t.reshape(B, S, nh, hd).transpose(0, 2, 1, 3)

    for l in range(DEPTH):
        mod = jax.nn.silu(c) @ w_ada[l] + b_ada[l]
        sh1, sc1, g1, sh2, sc2, g2 = [m[:, None, :] for m in jnp.split(mod, 6, axis=-1)]
        u = layer_norm(x) * (1 + sc1) + sh1
        qa, ka, va, qb, kc, vc, ksl, vsl, kw, vw, gl = jnp.split(u @ w_in[l], split_at, axis=-1)
        o_a = dilated_attention(partial_rope(heads(qa, N_HEADS_A), pos),
                                partial_rope(heads(ka, N_HEADS_A), pos),
                                heads(va, N_HEADS_A))
        o_a = o_a.transpose(0, 2, 1, 3).reshape(B, S, N_HEADS_A * hd)
        o_b = nsa_attention(partial_rope(heads(qb, N_HEADS_B), pos),
                            partial_rope(heads(kc, NSA_GROUPS), pos), heads(vc, NSA_GROUPS),
                            partial_rope(heads(ksl, NSA_GROUPS), pos), heads(vsl, NSA_GROUPS),
                            partial_rope(heads(kw, NSA_GROUPS), pos), heads(vw, NSA_GROUPS),
                            gl, pe_cmp[l], w_ck1[l], w_ck2[l], w_cv1[l], w_cv2[l])
        y = jnp.concatenate([o_a, o_b], axis=-1) @ w_o[l]
        x = layer_norm(ALPHA * x + g1 * y, ln1_g[l], ln1_b[l])
        u = layer_norm(x) * (1 + sc2) + sh2
        a_gate, a_val = jnp.split(u @ w_up[l], 2, axis=-1)
        h = jax.nn.gelu(causal_dwconv(a_gate, conv_w[l], conv_b[l])) * a_val
        x = layer_norm(ALPHA * x + g2 * (h @ w_down[l]), ln2_g[l], ln2_b[l])
    return x
```

```python
import contextlib
import os
SKIP = set(os.environ.get('KSKIP', '').split(','))
STAGE = int(os.environ.get('KSTAGE', '99'))
import numpy as np
import ml_dtypes
import concourse.bass as bass
import concourse.mybir as mybir
from concourse.bass_utils import run_bass_kernel_spmd

F32 = mybir.dt.float32
BF16 = mybir.dt.bfloat16
AF = mybir.ActivationFunctionType
ALU = mybir.AluOpType
AX = mybir.AxisListType

D = 1024
S = 8192
NT = 64
Q0 = 31
NQ = 33
QTOK = NQ * 128
HD = 64
INW = 2840
DFF = 2816
NFC = 22
ALPHA = 2.0 ** 0.25
EPS = 1e-5
MASKV = -30000.0
VW = 66
FB_QA, FB_KA, FB_QB, FB_KC, FB_VC, FB_KSL, FB_KW = 0, 4, 8, 12, 13, 14, 15
NFB = 16


def bcast(ap, axis, n):
    dims = [list(d) for d in ap.ap]
    dims.insert(axis, [0, n])
    return bass.AP(ap.tensor, ap.offset, dims)


class Buf:
    __slots__ = ("name", "w", "r", "psum")

    def __init__(self, name, psum=False):
        self.name = name
        self.w = {}
        self.r = {}
        self.psum = psum


class Ctx:
    def __init__(self, nc, n_dma_sems=40):
        self.nc = nc
        self.es = contextlib.ExitStack()
        self.eng = {"pe": nc.tensor, "dve": nc.vector, "act": nc.scalar, "pool": nc.gpsimd, "sp": nc.sync}
        self.sem = {}
        self.cnt = {}
        self.semobj = {}
        for e in self.eng:
            s = self.es.enter_context(nc.semaphore("sem_" + e))
            self.sem[e] = s
            self.cnt[e] = 0
        self.dma_sems = [self.es.enter_context(nc.semaphore("dsem%d" % i)) for i in range(n_dma_sems)]
        self.dma_cnt = [0] * n_dma_sems
        self.dma_rr = 0
        self.waited = {}
        self.nbuf = 0
        self.ninst = 0

    def buf(self, name=None):
        self.nbuf += 1
        return Buf(name or ("b%d" % self.nbuf))

    def _key(self, s):
        return id(s)

    def _wait(self, e, deps):
        for k, (s, v) in deps.items():
            if e == "pe" and s is self.sem["pe"]:
                continue
            if self.waited.get((e, k), 0) < v:
                self.eng[e].wait_ge(s, v)
                self.waited[(e, k)] = v

    def _deps(self, reads, writes):
        deps = {}

        def add(dd):
            for k, (s, v) in dd.items():
                if k not in deps or deps[k][1] < v:
                    deps[k] = (s, v)
        for b in reads:
            add(b.w)
        for b in writes:
            add(b.w)
            add(b.r)
        return deps

    def _commit(self, ev, reads, writes):
        k = self._key(ev[0])
        for b in writes:
            b.w = {k: ev}
            b.r = {}
        for b in reads:
            if k not in b.r or b.r[k][1] < ev[1]:
                b.r[k] = ev

    def op(self, e, reads, writes, fn, sig=True):
        if e != "pe":
            px = [b for b in reads if b.psum]
            if px:
                reads = [b for b in reads if not b.psum]
                writes = list(writes) + px
        deps = self._deps(reads, writes)
        self._wait(e, deps)
        inst = fn(self.eng[e])
        self.ninst += 1
        if sig:
            self.cnt[e] += 1
            inst.then_inc(self.sem[e], 1)
            ev = (self.sem[e], self.cnt[e])
        else:
            ev = (self.sem[e], self.cnt[e] + 1)
        self._commit(ev, reads, writes)
        return inst

    def dma(self, reads, writes, out, in_, q="sp", **kw):
        deps = self._deps(reads, writes)
        i = self.dma_rr
        self.dma_rr = (self.dma_rr + 1) % len(self.dma_sems)
        s = self.dma_sems[i]
        if self.dma_cnt[i] > 0:
            deps[self._key(s)] = (s, self.dma_cnt[i])
        self._wait(q, deps)
        inst = self.eng[q].dma_start(out=out, in_=in_, **kw)
        self.ninst += 1
        self.dma_cnt[i] += 16
        inst.then_inc(s, 16)
        ev = (s, self.dma_cnt[i])
        self._commit(ev, reads, writes)
        return ev

    def barrier(self):
        for e in self.eng:
            deps = {}
            for e2 in self.eng:
                if e2 != e and self.cnt[e2] > 0:
                    deps[self._key(self.sem[e2])] = (self.sem[e2], self.cnt[e2])
            for i, sm in enumerate(self.dma_sems):
                if self.dma_cnt[i] > 0:
                    deps[self._key(sm)] = (sm, self.dma_cnt[i])
            self._wait(e, deps)

    def finish(self, bufs):
        deps = self._deps(bufs, [])
        self._wait("sp", deps)

    def close(self):
        self.es.close()


def host_tables(half):
    off = 0 if half == 1 else 32
    t = {}
    slots = np.arange(NT * 128)
    gpos = slots - off * 128
    real = gpos >= 0
    inv = 1.0 / (500000.0 ** (np.arange(0, 16, 2, dtype=np.float32) / 16.0))
    ang = np.where(real, gpos, 0).astype(np.float32)[:, None] * inv[None, :]
    cs = np.concatenate([np.cos(ang), np.cos(ang), -np.sin(ang), np.sin(ang)], -1).astype(np.float32)
    t["ropetab"] = np.ascontiguousarray(cs.reshape(NT, 128, 32).transpose(1, 0, 2))
    kb = np.where(real, 0.0, MASKV).astype(np.float32)
    t["keybias"] = np.ascontiguousarray(kb.reshape(NT, 128).T)
    kl = np.arange(128)[:, None]
    ql = np.arange(128)[None, :]
    m = np.zeros((3, 128, 128), np.float32)
    m[0] = np.where(kl <= ql, 0.0, MASKV)
    m[1] = np.where(kl >= ql, 0.0, MASKV)
    m[2] = np.where(kl > ql, 0.0, MASKV)
    t["masks"] = np.ascontiguousarray(m.transpose(1, 0, 2)).astype(ml_dtypes.bfloat16)
    t["ident"] = np.eye(128, dtype=np.float32).astype(ml_dtypes.bfloat16)
    E = np.zeros((128, NT * 128), np.float32)
    E[(np.arange(NT * 128) // 64), np.arange(NT * 128)] = 1.0
    t["eall"] = E.astype(ml_dtypes.bfloat16)
    cst = np.arange(512) * 16
    sst = np.arange(128) * 64
    ov = np.clip(np.minimum(cst[:, None] + 32, sst[None, :] + 64) - np.maximum(cst[:, None], sst[None, :]), 0, None) / 32.0
    ov[511] = 0.0
    t["overlap"] = np.ascontiguousarray(ov.reshape(4, 128, 128).transpose(1, 0, 2)).astype(ml_dtypes.bfloat16)
    selb = np.zeros((NQ, 128, 128), np.float32)
    cmpm = np.zeros((NQ, 512, 128), np.float32)
    for c in range(NQ):
        sl = Q0 + c
        tq = (sl - off) * 128 + np.arange(128)
        blk = np.arange(128) - off * 2
        cur = tq // 64
        sb = np.zeros((128, 128), np.float32)
        valid = (blk[None, :] >= 0) & (blk[None, :] * 64 <= tq[:, None])
        sb[:] = np.where(valid, 0.0, -1e9 - 1e4 * np.arange(128)[None, :])
        for j, bid in enumerate([cur, cur - 1, np.zeros_like(cur)]):
            hit = (blk[None, :] == bid[:, None]) & (blk[None, :] >= 0)
            sb = np.where(hit, (1.0 + j) * 1e9, sb)
        selb[c] = sb
        ci = np.arange(512)
        cstart_g = ci * 16 - off * 128
        cvalid = (cstart_g[:, None] >= 0) & (cstart_g[:, None] + 31 <= tq[None, :]) & (ci[:, None] < 511)
        cmpm[c] = np.where(cvalid, 1.0, 0.0)
    t["selb"] = np.ascontiguousarray(selb.transpose(1, 0, 2))
    t["cmpm"] = np.ascontiguousarray(cmpm.reshape(NQ, 4, 128, 128).transpose(2, 0, 1, 3)).astype(ml_dtypes.bfloat16)
    t["halo"] = np.full((128, 1), 1.0 if half == 1 else 0.0, np.float32)
    t["realb"] = np.ascontiguousarray(np.broadcast_to(((np.arange(128) - off * 2) >= 0).astype(np.float32)[None, :], (128, 128)))
    return t


TABLE_SPECS = [
    ("ropetab", [128, NT, 32], F32), ("keybias", [128, NT], F32), ("masks", [128, 3, 128], BF16),
    ("ident", [128, 128], BF16), ("eall", [128, NT * 128], BF16), ("overlap", [128, 4, 128], BF16),
    ("selb", [128, NQ, 128], F32), ("cmpm", [128, NQ, 4, 128], BF16), ("halo", [128, 1], F32), ("realb", [128, 128], F32),
]

WEIGHT_SPECS = [
    ("c", [1, D]), ("w_ada", [D, 6 * D]), ("b_ada", [1, 6 * D]), ("w_in", [D, INW]), ("pe_cmp", [32, 64]),
    ("w_ck1", [2048, 256]), ("w_ck2", [256, 64]), ("w_cv1", [2048, 256]), ("w_cv2", [256, 64]),
    ("w_o", [D, D]), ("ln1_g", [1, D]), ("ln1_b", [1, D]), ("w_up", [D, 2 * DFF]), ("conv_w", [3, DFF]),
    ("conv_b", [1, DFF]), ("w_down", [DFF, D]), ("ln2_g", [1, D]), ("ln2_b", [1, D]),
]


class K:
    def __init__(self, debug=(), scratch_in=()):
        self.debug = set(debug)
        self.scratch_in = set(scratch_in)
        nc = bass.Bass("TRN2", target_bir_lowering=False)
        self.nc = nc
        self.cx = Ctx(nc)
        self.inp = {}
        self.inp["xk"] = nc.dram_tensor("xk", [S, D], F32, kind="ExternalInput").ap()
        for name, shape in WEIGHT_SPECS:
            self.inp[name] = nc.dram_tensor(name, shape, F32, kind="ExternalInput").ap()
        for name, shape, dt in TABLE_SPECS:
            self.inp[name] = nc.dram_tensor(name, shape, dt, kind="ExternalInput").ap()
        self.out = nc.dram_tensor("out", [32 * 128, D], F32, kind="ExternalOutput").ap()
        self.outbuf = self.cx.buf("out")
        def kd(n):
            if n in self.scratch_in:
                return "ExternalInput"
            return "ExternalOutput" if n in self.debug else "Internal"
        self.ft = nc.dram_tensor("ft", [NFB, 128, S], BF16, kind=kd("ft")).ap()
        self.ftb = self.cx.buf("ft")
        self.va = nc.dram_tensor("va", [S, 8, VW], BF16, kind=kd("va")).ap()
        self.vab = self.cx.buf("va")
        self.vsl = nc.dram_tensor("vsl", [S, 2, VW], BF16, kind=kd("vsl")).ap()
        self.vslb = self.cx.buf("vsl")
        self.vwn = nc.dram_tensor("vwn", [S, 2, VW], BF16, kind=kd("vwn")).ap()
        self.vwnb = self.cx.buf("vwn")
        self.ot = nc.dram_tensor("ot", [16, 64, QTOK], BF16, kind=kd("ot")).ap()
        self.otb = self.cx.buf("ot")
        self.x1d = nc.dram_tensor("x1d", [QTOK, D], F32, kind=kd("x1d")).ap()
        self.wupd = nc.dram_tensor("wupd", [8, 128, 2 * DFF], BF16, kind="Internal").ap()
        self.wupdb = self.cx.buf("wupd")
        self.wdnd = nc.dram_tensor("wdnd", [128, NFC, D], BF16, kind="Internal").ap()
        self.wdndb = self.cx.buf("wdnd")
        self.wcast_done = False
        self.x1b = self.cx.buf("x1d")
        self.dbg = {}

    def dbg_out(self, name, shape, dt):
        ap = self.nc.dram_tensor("dbg_" + name, shape, dt, kind="ExternalOutput").ap()
        self.dbg[name] = (ap, self.cx.buf("dbg_" + name))
        return self.dbg[name]

    def sb(self, es, name, shape, dt):
        t = es.enter_context(self.nc.sbuf_tensor("s_" + name, shape, dt))
        return t, self.cx.buf(name)

    def ps(self, es, name, shape, dt):
        t = es.enter_context(self.nc.psum_tensor("p_" + name, shape, dt))
        b = self.cx.buf(name)
        b.psum = True
        return t, b

    def consts(self, es):
        cx = self.cx
        self.ident, self.identb = self.sb(es, "ident", [128, 128], BF16)
        cx.dma([], [self.identb], self.ident[:], self.inp["ident"][:, :])
        self.masks, self.masksb = self.sb(es, "masks", [128, 3, 128], BF16)
        cx.dma([], [self.masksb], self.masks[:], self.inp["masks"][:, :, :])
        self.keybias, self.keybiasb = self.sb(es, "keybias", [128, NT], F32)
        cx.dma([], [self.keybiasb], self.keybias[:], self.inp["keybias"][:, :])
        self.modT, self.modTb = self.sb(es, "modT", [128, 48], F32)
        self.gbc, self.gbcb = self.sb(es, "gbc", [128, 2, D], F32)
        self.ones, self.onesb = self.sb(es, "ones", [128, 128], F32)
        cx.op("pool", [], [self.onesb], lambda e: e.memset(self.ones[:], 1.0))
        self.identf, self.identfb = self.sb(es, "identf", [128, 128], F32)
        cx.op("dve", [self.identb], [self.identfb], lambda e: e.tensor_copy(out=self.identf[:], in_=self.ident[:]))
        self.onesbf, self.onesbfb = self.sb(es, "onesbf", [128, 128], BF16)
        cx.op("pool", [], [self.onesbfb], lambda e: e.memset(self.onesbf[:], 1.0))

    def phase_mod(self):
        cx, nc = self.cx, self.nc
        with contextlib.ExitStack() as es:
            cT, cTb = self.sb(es, "cT", [128, 8], F32)
            cx.dma([], [cTb], cT[:], self.inp["c"].rearrange("o (k p) -> p (o k)", p=128), allow_slow_non_contiguous=True)
            bT, bTb = self.sb(es, "bT", [128, 48], F32)
            cx.dma([], [bTb], bT[:], self.inp["b_ada"].rearrange("o (j p) -> p (o j)", p=128), allow_slow_non_contiguous=True)
            brow, browb = self.sb(es, "brow", [1, 2, D], F32)
            cx.dma([], [browb], brow[:, 0, :], self.inp["b_ada"][:, 2 * D:3 * D])
            cx.dma([], [browb], brow[:, 1, :], self.inp["b_ada"][:, 5 * D:6 * D])
            sc, scb = self.sb(es, "silc", [128, 8], BF16)
            cx.op("act", [cTb], [scb], lambda e: e.activation(out=sc[:], in_=cT[:], func=AF.Silu))
            wst = [self.sb(es, "wada_st%d" % i, [128, 6 * D], F32) for i in range(2)]
            wbf = [self.sb(es, "wada_bf%d" % i, [128, 6 * D], BF16) for i in range(2)]
            pp, ppb = self.ps(es, "modpart", [128, 48, 8], F32)
            prow = [self.ps(es, "modrow%d" % i, [1, 512], F32) for i in range(4)]
            wada = self.inp["w_ada"].rearrange("(k p) n -> k p n", p=128)
            for kc in range(8):
                st, stb = wst[kc % 2]
                wb, wbb = wbf[kc % 2]
                cx.dma([], [stb], st[:, 0:3 * D], wada[kc, :, 0:3 * D])
                cx.dma([], [stb], st[:, 3 * D:6 * D], wada[kc, :, 3 * D:6 * D])
                cx.op("dve", [stb], [wbb], lambda e: e.tensor_copy(out=wb[:, 0:5 * 512], in_=st[:, 0:5 * 512]))
                cx.op("act", [stb], [wbb], lambda e: e.copy(out=wb[:, 5 * 512:10 * 512], in_=st[:, 5 * 512:10 * 512]))
                cx.op("pool", [stb], [wbb], lambda e: e.tensor_copy(out=wb[:, 10 * 512:6 * D], in_=st[:, 10 * 512:6 * D]))
                for j in range(48):
                    cx.op("pe", [wbb, scb], [ppb], lambda e: e.matmul(pp[:, j, kc:kc + 1], lhsT=wb[:, j * 128:(j + 1) * 128],
                                                                       rhs=sc[:, kc:kc + 1], start=True, stop=True), sig=(j == 47))
                for r in range(4):
                    col = (2 * D if r < 2 else 5 * D) + (r % 2) * 512
                    pr, prb = prow[r]
                    cx.op("pe", [wbb, scb], [prb], lambda e: e.matmul(pr[:, :], lhsT=sc[:, kc:kc + 1], rhs=wb[:, col:col + 512],
                                                                       start=(kc == 0), stop=(kc == 7)))
            tmp, tmpb = self.sb(es, "modtmp", [128, 48], F32)
            cx.op("dve", [ppb], [tmpb], lambda e: e.tensor_reduce(out=tmp[:], in_=pp[:], axis=AX.X, op=ALU.add))
            cx.op("dve", [tmpb, bTb], [self.modTb], lambda e: e.tensor_tensor(out=self.modT[:], in0=tmp[:], in1=bT[:], op=ALU.add))
            cx.op("dve", [self.modTb], [self.modTb], lambda e: e.tensor_scalar_add(out=self.modT[:, 8:16], in0=self.modT[:, 8:16], scalar1=1.0))
            cx.op("dve", [self.modTb], [self.modTb], lambda e: e.tensor_scalar_add(out=self.modT[:, 32:40], in0=self.modT[:, 32:40], scalar1=1.0))
            grow, growb = self.sb(es, "grow", [1, 2, D], F32)
            for r in range(4):
                pr, prb = prow[r]
                cx.op("dve", [prb, browb], [growb], lambda e: e.tensor_tensor(out=grow[:, r // 2, (r % 2) * 512:(r % 2) * 512 + 512],
                                                                               in0=pr[:, :], in1=brow[:, r // 2, (r % 2) * 512:(r % 2) * 512 + 512], op=ALU.add))
            pb, pbb = self.ps(es, "gbps", [128, 512], F32)
            for r in range(4):
                c0 = (r % 2) * 512
                cx.op("pe", [growb, self.onesb], [pbb], lambda e: e.matmul(pb[:, :], lhsT=self.ones[0:1, :], rhs=grow[:, r // 2, c0:c0 + 512], start=True, stop=True))
                cx.op("act", [pbb], [self.gbcb], lambda e: e.copy(out=self.gbc[:, r // 2, c0:c0 + 512], in_=pb[:, :]))
            cx.barrier()
            if "mod" in self.debug:
                ap, b = self.dbg_out("modT", [128, 48], F32)
                cx.dma([self.modTb], [b], ap[:, :], self.modT[:])
                ap, b = self.dbg_out("gbc", [128, 2, D], F32)
                cx.dma([self.gbcb], [b], ap[:, :, :], self.gbc[:])

    def ln_stats(self, es, tag, nbufs=2):
        sets = []
        for i in range(nbufs):
            st, stb = self.sb(es, "%s_st%d" % (tag, i), [128, 12], F32)
            mv, mvb = self.sb(es, "%s_mv%d" % (tag, i), [128, 4], F32)
            sets.append((st, stb, mv, mvb))
        return sets

    def emit_ln_stats(self, lnset, x, xb):
        cx = self.cx
        st, stb, mv, mvb = lnset
        cx.op("dve", [xb], [stb], lambda e: e.bn_stats(out=st[:, 0:6], in_=x[:, 0:512]))
        cx.op("dve", [xb, stb], [stb], lambda e: e.bn_stats(out=st[:, 6:12], in_=x[:, 512:1024]))
        cx.op("dve", [stb], [mvb], lambda e: e.bn_aggr(out=mv[:, 0:2], in_=st[:, :]))
        cx.op("dve", [mvb], [mvb], lambda e: e.tensor_scalar_add(out=mv[:, 2:3], in0=mv[:, 1:2], scalar1=EPS))
        cx.op("act", [mvb], [mvb], lambda e: e.activation(out=mv[:, 2:3], in_=mv[:, 2:3], func=AF.Sqrt))
        cx.op("dve", [mvb], [mvb], lambda e: e.reciprocal(out=mv[:, 2:3], in_=mv[:, 2:3]))
        cx.op("dve", [mvb], [mvb], lambda e: e.tensor_scalar(out=mv[:, 3:4], in0=mv[:, 0:1], scalar1=mv[:, 2:3], scalar2=-1.0, op0=ALU.mult, op1=ALU.mult))
        return mv, mvb

    def emit_modT(self, xn, xnb, tp, tpb, uT, uTb, sc_col, sh_col, n=128):
        cx = self.cx
        for kc in range(8):
            cx.op("pe", [xnb, self.identb], [tpb[kc]], lambda e: e.transpose(out=tp[:, kc, :], in_=xn[:, kc * 128:(kc + 1) * 128], identity=self.ident[:]), sig=(kc == 7))
        for kc in range(8):
            use_act = (kc % 2 == 0)
            if "allact" in SKIP:
                use_act = True
            if "alldve" in SKIP:
                use_act = False
            if "plaincopy" in SKIP:
                cx.op("dve", [tpb[kc]], [uTb[kc]], lambda e: e.tensor_copy(out=uT[:, kc, :], in_=tp[:, kc, :]))
            elif use_act:
                cx.op("act", [tpb[kc], self.modTb], [uTb[kc]], lambda e: e.activation(out=uT[:, kc, :], in_=tp[:, kc, :], func=AF.Identity,
                                                                                      bias=self.modT[:, sh_col + kc:sh_col + kc + 1], scale=self.modT[:, sc_col + kc:sc_col + kc + 1]))
            else:
                cx.op("dve", [tpb[kc], self.modTb], [uTb[kc]], lambda e: e.tensor_scalar(out=uT[:, kc, :], in0=tp[:, kc, :], scalar1=self.modT[:, sc_col + kc:sc_col + kc + 1],
                                                                                         scalar2=self.modT[:, sh_col + kc:sh_col + kc + 1], op0=ALU.mult, op1=ALU.add))

    def phase_proj(self, es_persist, tiles=None):
        cx, nc = self.cx, self.nc
        tiles = list(range(NT)) if tiles is None else tiles
        self.gates, self.gatesb = self.sb(es_persist, "gates", [128, NQ, 24], F32)
        with contextlib.ExitStack() as es:
            wbf, wbfb = self.sb(es, "win_bf", [128, 8, INW], BF16)
            wbfk = [cx.buf("win_bf%d" % k) for k in range(8)]
            wst = [self.sb(es, "win_st%d" % i, [128, INW], F32) for i in range(2)]
            win = self.inp["w_in"].rearrange("(k p) n -> k p n", p=128)
            for kc in range(8):
                st, stb = wst[kc % 2]
                cx.dma([], [stb], st[:, :], win[kc, :, :])
                cx.op("dve", [stb], [wbfk[kc]], lambda e: e.tensor_copy(out=wbf[:, kc, 0:1420], in_=st[:, 0:1420]))
                cx.op("pool", [stb], [wbfk[kc]], lambda e: e.tensor_copy(out=wbf[:, kc, 1420:INW], in_=st[:, 1420:INW]))
            rope, ropeb = self.sb(es, "ropetab", [128, NT, 32], F32)
            cx.dma([], [ropeb], rope[:], self.inp["ropetab"][:, :, :])
            xt = [self.sb(es, "xt%d" % i, [128, D], F32) for i in range(2)]
            xn = [self.sb(es, "xn%d" % i, [128, D], BF16) for i in range(2)]
            uT = [self.sb(es, "uT%d" % i, [128, 8, 128], BF16) for i in range(2)]
            uTk = [[cx.buf() for k in range(8)] for i in range(2)]
            lns = self.ln_stats(es, "lnp")
            tp, _ = self.ps(es, "tp", [128, 8, 128], BF16)
            tpk1 = cx.buf()
            tpk1.psum = True
            tpk = [tpk1] * 8
            pg = [self.ps(es, "pg%d" % i, [128, 512], F32) for i in range(3)]
            tpo = [self.ps(es, "tpo%d" % i, [128, 8, 128], BF16) for i in range(2)]
            tokq = [self.sb(es, "tokq%d" % i, [128, NFB * 128], BF16) for i in range(2)]
            rsc = [self.sb(es, "rsc%d" % i, [128, 2, 8, 16], F32) for i in range(2)]
            ftst = [self.sb(es, "ftst%d" % i, [128, NFB, 512], BF16) for i in range(2)]
            vst = [self.sb(es, "vst%d" % i, [128, 8, VW], BF16) for i in range(2)]
            vsst = [self.sb(es, "vsst%d" % i, [128, 2, VW], BF16) for i in range(2)]
            vwst = [self.sb(es, "vwst%d" % i, [128, 2, VW], BF16) for i in range(2)]
            for i in range(2):
                for (t_, b_) in (vst[i], vsst[i], vwst[i]):
                    cx.op("pool", [], [b_], lambda e: e.memset(t_[:], 1.0))
                cx.op("pool", [], [ftst[i][1]], lambda e: e.memset(ftst[i][0][:], 0.0))
            xk = self.inp["xk"].rearrange("(t p) d -> t p d", p=128)
            ngroup = 0
            cst = [self.sb(es, "wc_st%d" % i, [128, DFF], F32) for i in range(2)]
            cbf = [self.sb(es, "wc_bf%d" % i, [128, DFF], BF16) for i in range(2)]
            wupv_ = self.inp["w_up"].rearrange("(k p) n -> k p n", p=128)
            wdnv_ = self.inp["w_down"].rearrange("(f p) n -> p f n", p=128)
            jobs = [("up", kc, hf) for kc in range(8) for hf in range(2)] + [("dn", q, 0) for q in range(11)]
            jobn = [0]

            pend_store = [None]

            def flush_store():
                if pend_store[0] is not None:
                    (bfb, dbuf, dst, src) = pend_store[0]
                    cx.dma([bfb], [dbuf], dst, src)
                    pend_store[0] = None

            def cast_job():
                flush_store()
                if jobn[0] >= len(jobs) or len(tiles) < 40:
                    return
                kind, a_, b_ = jobs[jobn[0]]
                st, stb = cst[jobn[0] % 2]
                bf_, bfb = cbf[jobn[0] % 2]
                jobn[0] += 1
                if kind == "up":
                    cx.dma([], [stb], st[:, :], wupv_[a_, :, b_ * DFF:(b_ + 1) * DFF])
                    cx.op("pool", [stb], [bfb], lambda e: e.tensor_copy(out=bf_[:, :], in_=st[:, :]))
                    pend_store[0] = (bfb, self.wupdb, self.wupd[a_, :, b_ * DFF:(b_ + 1) * DFF], bf_[:, :])
                else:
                    sv = st[:, 0:2 * D].rearrange("p (f n) -> p f n", n=D)
                    bv = bf_[:, 0:2 * D].rearrange("p (f n) -> p f n", n=D)
                    cx.dma([], [stb], sv, wdnv_[:, 2 * a_:2 * a_ + 2, :])
                    cx.op("pool", [stb], [bfb], lambda e: e.tensor_copy(out=bv, in_=sv))
                    pend_store[0] = (bfb, self.wdndb, self.wdnd[:, 2 * a_:2 * a_ + 2, :], bv)
                if jobn[0] == len(jobs):
                    self.wcast_done = True

            def load_x(i, t):
                cx.dma([], [xt[i][1]], xt[i][0][:, :], xk[t, :, :])

            load_x(0, tiles[0])
            pgi = 0
            def stage_a(ti):
                i = ti % 2
                x_, xb_ = xt[i]
                if ti + 1 < len(tiles):
                    load_x(1 - i, tiles[ti + 1])
                mv, mvb = self.emit_ln_stats(lns[i], x_, xb_)
                xn_, xnb_ = xn[i]
                cx.op("act", [xb_, mvb], [xnb_], lambda e: e.activation(out=xn_[:, :], in_=x_[:, :], func=AF.Identity, bias=mv[:, 3:4], scale=mv[:, 2:3]))
                self.emit_modT(xn_, xnb_, tp, tpk, uT[i][0], uTk[i], 8, 0)

            stage_a(0)
            for ti, t in enumerate(tiles):
                i = ti % 2
                isq = t >= Q0
                if ti + 1 < len(tiles):
                    stage_a(ti + 1)
                if ti >= 2 and ti % 2 == 0:
                    cast_job()
                uT_, _ = uT[i]
                tq, tqb = tokq[i]
                rs, rsb = rsc[i]
                g4 = ti // 4
                fst, fstb = ftst[g4 % 2]
                sub = ti % 4
                cs32 = rope[:, t, :]

                def proj(c0, c1):
                    nonlocal pgi
                    p_, pb_ = pg[pgi % 3]
                    pgi += 1
                    for kc in range(8):
                        cx.op("pe", [uTk[i][kc], wbfk[kc]], [pb_], lambda e: e.matmul(p_[:, 0:c1 - c0], lhsT=uT_[:, kc, :], rhs=wbf[:, kc, c0:c1],
                                                                                   start=(kc == 0), stop=(kc == 7)), sig=(kc == 7))
                    return p_, pb_

                def rope_fix(p_, pb_, src_view, dst_view, shp):
                    a, b = shp
                    A = rs[:, 0, 0:a * b, :].rearrange("p (a b) d -> p a b d", a=a)
                    B = rs[:, 1, 0:a * b, :].rearrange("p (a b) d -> p a b d", a=a)

                    def tb(lo, hi):
                        return bcast(bcast(cs32[:, lo:hi], 1, b), 1, a)
                    cx.op("dve", [pb_, ropeb], [rsb], lambda e: e.tensor_tensor(out=A, in0=src_view, in1=tb(0, 16), op=ALU.mult))
                    cx.op("dve", [pb_, ropeb, rsb], [rsb], lambda e: e.tensor_tensor(out=B[:, :, :, 0:8], in0=src_view[:, :, :, 8:16], in1=tb(16, 24), op=ALU.mult))
                    cx.op("dve", [pb_, ropeb, rsb], [rsb], lambda e: e.tensor_tensor(out=B[:, :, :, 8:16], in0=src_view[:, :, :, 0:8], in1=tb(24, 32), op=ALU.mult))
                    cx.op("dve", [rsb, tqb], [tqb], lambda e: e.tensor_tensor(out=dst_view, in0=A, in1=B, op=ALU.add))

                def hv(ap2d, a, b, astride_cols):
                    base = ap2d
                    dims = [list(base.ap[0]), [astride_cols, a], [64, b], [1, 16]]
                    return bass.AP(base.tensor, base.offset, dims)

                for (c0, dst0, need) in ((0, FB_QA * 128, isq), (512, FB_KA * 128, True), (1536, FB_QB * 128, isq)):
                    if not need:
                        continue
                    p_, pb_ = proj(c0, c0 + 512)
                    cx.op("act", [pb_], [tqb], lambda e: e.copy(out=tq[:, dst0:dst0 + 512], in_=p_[:, :]))
                    if STAGE >= 4:
                        rope_fix(p_, pb_, hv(p_[:, 0:512], 1, 8, 0), hv(tq[:, dst0:dst0 + 512], 1, 8, 0), (1, 8))
                if STAGE < 5:
                    continue
                p_, pb_ = proj(1024, 1536)
                v_, vb_ = vst[i]
                cx.op("act", [pb_], [vb_], lambda e: e.copy(out=v_[:, :, 0:64], in_=p_[:, :].rearrange("p (h d) -> p h d", d=64)))
                if "va" not in SKIP:
                    cx.dma([vb_], [self.vab], self.va[t * 128:(t + 1) * 128, :, :], v_[:])
                p_, pb_ = proj(2048, 2560)
                d0 = FB_KC * 128
                cx.op("act", [pb_], [tqb], lambda e: e.copy(out=tq[:, d0:d0 + 384], in_=p_[:, 0:384]))
                rope_fix(p_, pb_, hv(p_[:, 0:384], 2, 2, 256), hv(tq[:, d0:d0 + 384], 2, 2, 256), (2, 2))
                v_, vb_ = vsst[i]
                cx.op("act", [pb_], [vb_], lambda e: e.copy(out=v_[:, :, 0:64], in_=p_[:, 384:512].rearrange("p (h d) -> p h d", d=64)))
                cx.dma([vb_], [self.vslb], self.vsl[t * 128:(t + 1) * 128, :, :], v_[:])
                p_, pb_ = proj(2560, INW)
                d0 = FB_KW * 128
                cx.op("act", [pb_], [tqb], lambda e: e.copy(out=tq[:, d0:d0 + 128], in_=p_[:, 0:128]))
                rope_fix(p_, pb_, hv(p_[:, 0:128], 1, 2, 0), hv(tq[:, d0:d0 + 128], 1, 2, 0), (1, 2))
                v_, vb_ = vwst[i]
                cx.op("act", [pb_], [vb_], lambda e: e.copy(out=v_[:, :, 0:64], in_=p_[:, 128:256].rearrange("p (h d) -> p h d", d=64)))
                cx.dma([vb_], [self.vwnb], self.vwn[t * 128:(t + 1) * 128, :, :], v_[:])
                if isq:
                    cx.op("act", [pb_], [self.gatesb], lambda e: e.activation(out=self.gates[:, t - Q0, :], in_=p_[:, 256:280], func=AF.Sigmoid))
                blocks = list(range(NFB)) if isq else [b for b in range(NFB) if not (FB_QA <= b < FB_KA or FB_QB <= b < FB_KC)]
                for j0 in range(0, len(blocks), 8):
                    bl = blocks[j0:j0 + 8]
                    to_, tob_ = tpo[(j0 // 8) % 2]
                    for jj, b in enumerate(bl):
                        cx.op("pe", [tqb, self.identb], [tob_], lambda e: e.transpose(out=to_[:, jj, :], in_=tq[:, b * 128:(b + 1) * 128], identity=self.ident[:]), sig=(jj == len(bl) - 1))
                    runs = []
                    for jj, b in enumerate(bl):
                        if runs and runs[-1][1] + runs[-1][2] == b:
                            runs[-1][2] += 1
                        else:
                            runs.append([jj, b, 1])
                    for ri, (jj, b, n) in enumerate(runs):
                        if (j0 // 8) % 2 == 0:
                            cx.op("dve", [tob_], [fstb], lambda e: e.tensor_copy(out=fst[:, b:b + n, sub * 128:(sub + 1) * 128], in_=to_[:, jj:jj + n, :]))
                        else:
                            cx.op("act", [tob_], [fstb], lambda e: e.copy(out=fst[:, b:b + n, sub * 128:(sub + 1) * 128], in_=to_[:, jj:jj + n, :]))
                if sub == 3 or ti == len(tiles) - 1:
                    t0 = (t - sub) * 128
                    ntok = (sub + 1) * 128
                    if "ft" not in SKIP:
                        cx.dma([fstb], [self.ftb], self.ft[:, :, t0:t0 + ntok].rearrange("b p t -> p b t"), fst[:, :, 0:ntok])
            flush_store()
            cx.barrier()


def build(debug=(), phases=("mod", "proj", "cmp", "attnA", "attnB", "post1", "post2"), proj_tiles=None, a_hps=(0, 1, 2, 3), a_filter=None, b_chunks=None, p_chunks=None, scratch_in=()):
    k = K(debug=debug, scratch_in=scratch_in)
    cx = k.cx
    with contextlib.ExitStack() as es:
        k.consts(es)
        if "mod" in phases:
            k.phase_mod()
        with contextlib.ExitStack() as es_mid:
            if "proj" in phases:
                k.phase_proj(es_mid, tiles=proj_tiles)
            if "cmp" in phases:
                k.phase_compress(es_mid)
            if "attnA" in phases:
                k.phase_attnA(hps=a_hps, qfilter=a_filter)
            if "attnB" in phases:
                k.phase_attnB(chunks=b_chunks)
            cx.barrier()
        if "post1" in phases:
            k.phase_post1(chunks=p_chunks)
        if "post2" in phases:
            k.phase_post2(chunks=p_chunks)
        allb = [k.outbuf, k.ftb, k.vab, k.vslb, k.vwnb, k.otb, k.x1b] + [b for (_, b) in k.dbg.values()]
        cx.finish(allb)
    return k


def make_in_maps(inputs):
    x = np.asarray(inputs["x"], np.float32)
    maps = []
    tabs = {h: host_tables(h) for h in (0, 1)}
    for core in range(8):
        b, half = core // 2, core % 2
        m = {}
        if half == 1:
            m["xk"] = np.ascontiguousarray(x[b])
        else:
            m["xk"] = np.concatenate([np.zeros((4096, D), np.float32), x[b, 0:4096]], 0)
        for name, shape in WEIGHT_SPECS:
            a = np.asarray(inputs[name], np.float32)
            if name == "c":
                a = a[b:b + 1]
            else:
                a = a[0]
            m[name] = np.ascontiguousarray(a.reshape(shape))
        m.update(tabs[half])
        maps.append(m)
    return maps


def kernel(**inputs):
    k = build()
    maps = make_in_maps(inputs)
    res = run_bass_kernel_spmd(k.nc, maps, core_ids=list(range(8)))
    out = np.zeros((4, S, D), np.float32)
    for core in range(8):
        b, half = core // 2, core % 2
        out[b, half * 4096:(half + 1) * 4096] = res.results[core]["out"]
    return out


def _attnA(self, hps=(0, 1, 2, 3), qfilter=None):
    cx, nc = self.cx, self.nc
    QBASE = Q0 * 128
    pats = []
    for d, mlo, mhi in ((1, 31, 63), (4, 7, 15), (16, 1, 3)):
        for r in range(d):
            for m in range(mlo, mhi + 1):
                j0 = max(0, -(-(QBASE - r) // d) - 128 * m)
                if j0 >= 128:
                    continue
                pats.append((d, r, m, j0))
    if qfilter is not None:
        pats = [p for p in pats if qfilter(p)]
    vtiles = {1: (30, 34), 4: (6, 10), 16: (0, 4)}
    with contextlib.ExitStack() as es:
        self.m01, self.m01b = self.sb(es, "m01", [128, 3, 128], BF16)
        cx.op("dve", [self.masksb], [self.m01b], lambda e: e.tensor_scalar(out=self.m01[:], in0=self.masks[:], scalar1=0.0, scalar2=None, op0=ALU.is_equal))
        KTs = [self.sb(es, "a_KT%d" % i, [128, S], BF16) for i in range(2)]
        QTs = [self.sb(es, "a_QT%d" % i, [128, QTOK], BF16) for i in range(2)]
        Vps = [{d: self.sb(es, "a_V%d_%d" % (d, i), [128, d * vtiles[d][1], 2, VW], BF16) for d in (1, 4, 16)} for i in range(2)]
        acc, accb = self.sb(es, "a_acc", [65, 2, QTOK], F32)
        otst, otstb = self.sb(es, "a_otst", [64, 2, QTOK], BF16)
        PT = [self.sb(es, "a_PT%d" % i, [128, 128], BF16) for i in range(8)]
        STp = [self.ps(es, "a_ST%d" % i, [128, 512], F32) for i in range(3)]
        OTp = [self.ps(es, "a_OT%d" % i, [65, 512], F32) for i in range(4)]
        BCp = [self.ps(es, "a_BC%d" % i, [64, 512], F32) for i in range(1)] * 2
        cnt = 0
        ocnt = 0

        def load(hi):
            hp = hps[hi]
            KT, KTb = KTs[hi % 2]
            QT, QTb = QTs[hi % 2]
            cx.dma([self.ftb], [KTb], KT[:, :], self.ft[FB_KA + hp, :, :])
            cx.dma([self.ftb], [QTb], QT[:, :], self.ft[FB_QA + hp, :, QBASE:S])
            for d in (1, 4, 16):
                k0, nk = vtiles[d]
                V_, Vb_ = Vps[hi % 2][d]
                rows = self.va[d * 128 * k0:d * 128 * (k0 + nk), 2 * hp:2 * hp + 2, :].rearrange("(kk j dd) h w -> dd j kk h w", j=128, dd=d)
                for r in range(d):
                    cx.dma([self.vab], [Vb_], V_[:, r * nk:(r + 1) * nk, :, :], rows[r])

        load(0)
        for hi, hp in enumerate(hps):
            if hi + 1 < len(hps):
                load(hi + 1)
            KT, KTb = KTs[hi % 2]
            QT, QTb = QTs[hi % 2]
            Vp = Vps[hi % 2]
            cx.op("pool", [], [accb], lambda e: e.memset(acc[:], 0.0))
            aq = []

            def pv_emit(pend):
                (P_, Pb_, O_, Ob_, V_, Vb_, vidx, h, n, ki, av) = pend
                cx.op("pe", [Pb_, Vb_], [Ob_], lambda e: e.matmul(O_[:, 0:n], lhsT=V_[:, vidx, h, 0:65], rhs=P_[:, 0:n], start=(ki == 0), stop=(ki == 1)), sig=(ki == 1))
                if ki == 1:
                    cx.op("dve", [Ob_, accb], [accb], lambda e: e.tensor_tensor(out=av, in0=av, in1=O_[:, 0:n], op=ALU.add))

            for (d, r, m, j0) in pats:
                n = 128 - j0
                k0, nk = vtiles[d]
                V_, Vb_ = Vp[d]
                qs = d * (128 * m + j0) + r - QBASE
                for h in range(2):
                    pb = 64 * h
                    qv = QT[pb:pb + 64, qs:qs + d * (n - 1) + 1:d]
                    O_, Ob_ = OTp[ocnt % 4]
                    ocnt += 1
                    av = acc[:, h, qs:qs + d * (n - 1) + 1:d]
                    for ki, (kt, mi) in enumerate(((m - 1, 1), (m, 0))):
                        S_, Sb_ = STp[cnt % 2]
                        P_, Pb_ = PT[cnt % 8]
                        cnt += 1
                        ks = d * 128 * kt + r
                        kv = KT[pb:pb + 64, ks:ks + d * 127 + 1:d]
                        cx.op("pe", [KTb, QTb], [Sb_], lambda e: e.matmul(S_[:, 0:n], lhsT=kv, rhs=qv, start=True, stop=True))
                        cx.op("act", [Sb_, self.keybiasb], [Pb_], lambda e: e.activation(out=P_[:, 0:n], in_=S_[:, 0:n], func=AF.Exp, bias=self.keybias[:, d * kt:d * kt + 1], scale=0.125))
                        cx.op("pool" if cnt % 2 == 0 else "dve", [Pb_, self.m01b], [Pb_], lambda e: e.tensor_tensor(out=P_[:, 0:n], in0=P_[:, 0:n], in1=self.m01[:, mi, j0:128], op=ALU.mult))
                        aq.append((P_, Pb_, O_, Ob_, V_, Vb_, r * nk + (kt - k0), h, n, ki, av))
                        while len(aq) > 4:
                            pv_emit(aq.pop(0))
            while aq:
                pv_emit(aq.pop(0))
            for h in range(2):
                cx.op("dve", [accb], [accb], lambda e: e.tensor_scalar_max(out=acc[64:65, h, :], in0=acc[64:65, h, :], scalar1=1e-30))
                cx.op("dve", [accb], [accb], lambda e: e.reciprocal(out=acc[64:65, h, :], in_=acc[64:65, h, :]))
                for c0 in range(0, QTOK, 512):
                    w = min(512, QTOK - c0)
                    B_, Bb_ = BCp[(c0 // 512) % 2]
                    cx.op("pe", [accb, self.onesb], [Bb_], lambda e: e.matmul(B_[:, 0:w], lhsT=self.ones[64:65, 0:64], rhs=acc[64:65, h, c0:c0 + w], start=True, stop=True))
                    cx.op("dve", [accb, Bb_], [otstb], lambda e: e.tensor_tensor(out=otst[:, h, c0:c0 + w], in0=acc[0:64, h, c0:c0 + w], in1=B_[:, 0:w], op=ALU.mult))
            cx.dma([otstb], [self.otb], self.ot[2 * hp:2 * hp + 2, :, :].rearrange("h d t -> d h t"), otst[:, :, :])
        cx.barrier()


K.phase_attnA = _attnA


def _compress(self, es_persist):
    cx, nc = self.cx, self.nc
    self.kccT, self.kccTb = self.sb(es_persist, "kccT", [128, 512], BF16)
    self.vcc1, self.vcc1b = self.sb(es_persist, "vcc1", [128, 4, 2, VW], BF16)
    cx.op("pool", [], [self.kccTb], lambda e: e.memset(self.kccT[:], 0.0))
    cx.op("pool", [], [self.vcc1b], lambda e: e.memset(self.vcc1[:], 0.0))
    cx.op("pool", [self.vcc1b], [self.vcc1b], lambda e: e.memset(self.vcc1[:, :, :, 64:65], 1.0))
    with contextlib.ExitStack() as es:
        w1st, w1stb = self.sb(es, "c_w1st", [128, 16, 256], F32)
        w1 = [self.sb(es, "c_w1_%d" % i, [128, 16, 256], BF16) for i in range(2)]
        w2st, w2stb = self.sb(es, "c_w2st", [128, 2, 64], F32)
        w2 = [self.sb(es, "c_w2_%d" % i, [128, 2, 64], BF16) for i in range(2)]
        pest, pestb = self.sb(es, "c_pest", [128, 16], F32)
        pebf, pebfb = self.sb(es, "c_pebf", [128, 16], BF16)
        pebias, pebiasb = self.sb(es, "c_pebias", [128, 2, 2], F32)
        X2, X2b = self.sb(es, "c_X2", [128, S], BF16)
        hT, hTb = self.sb(es, "c_hT", [128, 2, 512], BF16)
        HP = [self.ps(es, "c_HP%d" % i, [128, 512], F32) for i in range(2)]
        OP, OPb = self.ps(es, "c_OP", [128, 512], F32)
        BP, BPb = self.ps(es, "c_BP", [128, 4], F32)
        cx.dma([], [pestb], pest[0:64, :], self.inp["pe_cmp"].rearrange("(c j) d -> j d c", j=2)[0], allow_slow_non_contiguous=True)
        cx.dma([], [pestb], pest[64:128, :], self.inp["pe_cmp"].rearrange("(c j) d -> j d c", j=2)[1], allow_slow_non_contiguous=True)
        cx.op("dve", [pestb], [pebfb], lambda e: e.tensor_copy(out=pebf[:], in_=pest[:]))
        for kv, (n1, n2) in enumerate((("w_ck1", "w_ck2"), ("w_cv1", "w_cv2"))):
            cx.dma([], [w1stb], w1st[:, :, :], self.inp[n1].rearrange("(c p) h -> p c h", p=128))
            cx.op("dve", [w1stb], [w1[kv][1]], lambda e: e.tensor_copy(out=w1[kv][0][:, 0:8, :], in_=w1st[:, 0:8, :]))
            cx.op("pool", [w1stb], [w1[kv][1]], lambda e: e.tensor_copy(out=w1[kv][0][:, 8:16, :], in_=w1st[:, 8:16, :]))
            cx.dma([], [w2stb], w2st[:, :, :], self.inp[n2].rearrange("(c p) h -> p c h", p=128))
            cx.op("dve", [w2stb], [w2[kv][1]], lambda e: e.tensor_copy(out=w2[kv][0][:], in_=w2st[:]))
            for hh in range(2):
                for c in range(16):
                    cx.op("pe", [w1[kv][1], pebfb], [BPb], lambda e: e.matmul(BP[:, 2 * kv + hh:2 * kv + hh + 1], lhsT=w1[kv][0][:, c, hh * 128:(hh + 1) * 128], rhs=pebf[:, c:c + 1],
                                                                          start=(c == 0), stop=(c == 15)), sig=(c == 15))
        cx.op("dve", [BPb], [pebiasb], lambda e: e.tensor_copy(out=pebias[:].rearrange("p a b -> p (a b)"), in_=BP[:, :]))
        for kv in range(2):
            blk = FB_KC if kv == 0 else FB_VC
            for g in range(2):
                cx.dma([self.ftb], [X2b], X2[0:64, :], self.ft[blk, g * 64:(g + 1) * 64, :])
                cx.dma([self.ftb], [X2b], X2[64:128, 0:S - 1], self.ft[blk, g * 64:(g + 1) * 64, 1:S])
                if kv == 0 and g == 0:
                    cx.op("pool", [X2b], [X2b], lambda e: e.memset(X2[64:128, S - 1:S], 0.0))
                for hh in range(2):
                    H_, Hb_ = HP[hh]
                    for c in range(16):
                        cx.op("pe", [w1[kv][1], X2b], [Hb_], lambda e: e.matmul(H_[:, 0:511], lhsT=w1[kv][0][:, c, hh * 128:(hh + 1) * 128], rhs=X2[:, 2 * c:2 * c + 16 * 510 + 1:16],
                                                                              start=(c == 0), stop=(c == 15)), sig=(c == 15))
                    cx.op("act", [Hb_, pebiasb], [hTb], lambda e: e.activation(out=hT[:, hh, 0:511], in_=H_[:, 0:511], func=AF.Gelu_apprx_tanh, bias=pebias[:, kv, hh:hh + 1]))
                if kv == 0:
                    for hh in range(2):
                        cx.op("pe", [hTb, w2[0][1]], [OPb], lambda e: e.matmul(OP[64 * g:64 * g + 64, 0:511], lhsT=w2[0][0][:, hh, :], rhs=hT[:, hh, 0:511], start=(hh == 0), stop=(hh == 1)), sig=(hh == 1))
                    cx.op("dve", [OPb], [self.kccTb], lambda e: e.tensor_copy(out=self.kccT[64 * g:64 * g + 64, 0:511], in_=OP[64 * g:64 * g + 64, 0:511]))
                else:
                    for nt in range(4):
                        w = 128 if nt < 3 else 127
                        for hh in range(2):
                            cx.op("pe", [hTb, w2[1][1]], [OPb], lambda e: e.matmul(OP[0:w, nt * 64:(nt + 1) * 64], lhsT=hT[:, hh, nt * 128:nt * 128 + w], rhs=w2[1][0][:, hh, :],
                                                                               start=(hh == 0), stop=(hh == 1)), sig=(hh == 1 and nt == 3))
                    for nt in range(4):
                        w = 128 if nt < 3 else 127
                        cx.op("dve", [OPb], [self.vcc1b], lambda e: e.tensor_copy(out=self.vcc1[0:w, nt, g, 0:64], in_=OP[0:w, nt * 64:(nt + 1) * 64]))
        cx.barrier()
        if "cmp" in self.debug:
            ap, b = self.dbg_out("kccT", [128, 512], BF16)
            cx.dma([self.kccTb], [b], ap[:, :], self.kccT[:])
            ap, b = self.dbg_out("vcc1", [128, 4, 2, VW], BF16)
            cx.dma([self.vcc1b], [b], ap[:, :, :, :], self.vcc1[:])


K.phase_compress = _compress


def _attnB(self, chunks=None):
    cx, nc = self.cx, self.nc
    chunks = list(range(NQ)) if chunks is None else chunks
    QBASE = Q0 * 128
    with contextlib.ExitStack() as es:
        if not hasattr(self, "gates"):
            gin = nc.dram_tensor("gates_in", [128, NQ, 24], F32, kind="ExternalInput").ap()
            self.gates, self.gatesb = self.sb(es, "gates", [128, NQ, 24], F32)
            cx.dma([], [self.gatesb], self.gates[:], gin[:, :, :])
        kslT2 = [self.sb(es, "b_kslT%d" % g, [128, NT // 2, 128], BF16) for g in range(2)]
        kwT, kwTb = self.sb(es, "b_kwT", [128, S], BF16)
        vsl1, vsl1b = self.sb(es, "b_vsl1", [128, NT, 2, VW], BF16)
        vw1, vw1b = self.sb(es, "b_vw1", [128, NT, 2, VW], BF16)
        QB2 = [self.sb(es, "b_QB%d" % g, [128, 4, QTOK], BF16) for g in range(2)]
        eall, eallb = self.sb(es, "b_eall", [128, NT * 128], BF16)
        ovl, ovlb = self.sb(es, "b_ovl", [128, 4, 128], BF16)
        realb, realbb = self.sb(es, "b_realb", [128, 128], F32)
        cmpm = [self.sb(es, "b_cmpm%d" % i, [128, 4, 128], BF16) for i in range(2)]
        selb = [self.sb(es, "b_selb%d" % i, [128, 128], F32) for i in range(2)]
        PT = [self.sb(es, "b_PT%d" % i, [128, 2, 512], BF16) for i in range(8)]
        small = [self.sb(es, "b_small%d" % i, [128, 64], F32) for i in range(2)]
        score = [self.sb(es, "b_score%d" % i, [128, 2, 128], F32) for i in range(2)]
        selbias = [self.sb(es, "b_selbias%d" % i, [128, 128], F32) for i in range(2)]
        selT = [self.sb(es, "b_selT%d" % i, [128, 128], BF16) for i in range(2)]
        OCs = [self.sb(es, "b_OCs%d" % i, [128, 4, VW], F32) for i in range(2)]
        tmpo = [self.sb(es, "b_tmpo%d" % i, [128, 3, 4, 64], F32) for i in range(2)]
        ob = [self.sb(es, "b_ob%d" % i, [128, 256], F32) for i in range(2)]
        obT = [self.sb(es, "b_obT%d" % i, [128, 2, 128], BF16) for i in range(2)]
        STq = es.enter_context(nc.psum_tensor("p_b_STq", [128, 4, 512], F32))
        stb = [cx.buf("stq0"), cx.buf("stq1")]
        for b_ in stb:
            b_.psum = True
        Mp = [self.ps(es, "b_M%d" % i, [128, 4, 128], F32) for i in range(2)]
        OS, OSb = self.ps(es, "b_OS", [128, 4, VW], F32)
        OCW, OCWb = self.ps(es, "b_OCW", [128, 4, VW], F32)

        for g in range(2):
            src = self.ft[FB_KSL, g * 64:(g + 1) * 64, :].rearrange("d (pr par k) -> par d pr k", par=2, k=128)
            for par in range(2):
                cx.dma([self.ftb], [kslT2[g][1]], kslT2[g][0][par * 64:(par + 1) * 64, :, :], src[par])
        cx.dma([self.ftb], [kwTb], kwT[:, :], self.ft[FB_KW, :, :])
        for q4 in range(4):
            r0, r1 = q4 * 16 * 128, (q4 + 1) * 16 * 128
            cx.dma([self.vslb], [vsl1b], vsl1[:, q4 * 16:(q4 + 1) * 16, :, :], self.vsl[r0:r1, :, :].rearrange("(t p) g w -> p t g w", p=128))
            cx.dma([self.vwnb], [vw1b], vw1[:, q4 * 16:(q4 + 1) * 16, :, :], self.vwn[r0:r1, :, :].rearrange("(t p) g w -> p t g w", p=128))
        for g in range(2):
            for hq in range(4):
                blk = FB_QB + (g * 4 + hq) // 2
                prow = ((g * 4 + hq) % 2) * 64
                for half in range(2):
                    cx.dma([self.ftb], [QB2[g][1]], QB2[g][0][64 * half:64 * half + 64, hq, :], self.ft[blk, prow:prow + 64, QBASE:S])
        cx.dma([], [eallb], eall[:, :], self.inp["eall"][:, :])
        cx.dma([], [ovlb], ovl[:, :, :], self.inp["overlap"][:, :, :])
        cx.dma([], [realbb], realb[:, :], self.inp["realb"][:, :])
        Msb = [self.sb(es, "b_Msb%d" % i, [128, 4, 128], BF16) for i in range(3)]
        m01, m01b = self.sb(es, "b_m01", [128, 3, 128], BF16)
        cx.op("dve", [self.masksb], [m01b], lambda e: e.tensor_scalar(out=m01[:], in0=self.masks[:], scalar1=0.0, scalar2=None, op0=ALU.is_equal))
        cnt = 0
        mcnt = 0
        mscnt = 0
        units = [(ci, c, g) for ci, c in enumerate(chunks) for g in range(2)]
        state = {}

        def nxt():
            nonlocal cnt
            i = cnt % 2
            P_, Pb_ = PT[cnt % 8]
            cnt += 1
            return STq[:, 2 * i:2 * i + 2, :], stb[i], P_, Pb_

        def load_tables(ci):
            c = chunks[ci]
            cx.dma([], [cmpm[ci % 2][1]], cmpm[ci % 2][0][:, :, :], self.inp["cmpm"][:, c, :, :])
            cx.dma([], [selb[ci % 2][1]], selb[ci % 2][0][:, :], self.inp["selb"][:, c, :])

        def cmp_stage(idx):
            nonlocal mcnt
            ci, c, g = units[idx]
            sc = Q0 + c
            if g == 0 and ci + 1 < len(chunks):
                load_tables(ci + 1)
            cm_, cmb_ = cmpm[ci % 2]
            pb = 64 * g
            QB, QBb = QB2[g]
            qv = QB[pb:pb + 64, :, c * 128:(c + 1) * 128]
            nts = (8 * sc + 6) // 128 + 1
            pts = []
            for nt in range(nts):
                S2, Sb_, P_, Pb_ = nxt()
                cx.op("pe", [self.kccTb, QBb], [Sb_], lambda e: e.matmul(S2[:, 0, :], lhsT=self.kccT[pb:pb + 64, nt * 128:(nt + 1) * 128], rhs=qv, start=True, stop=True))
                cx.op("act", [Sb_], [Pb_], lambda e: e.activation(out=P_[:, 0, :], in_=S2[:, 0, :], func=AF.Exp, scale=0.125))
                pv4 = P_[:, 0, :].rearrange("p (h q) -> p h q", h=4)
                cx.op("dve", [Pb_, cmb_], [Pb_], lambda e: e.tensor_tensor(out=pv4, in0=pv4, in1=bcast(cm_[:, nt, :], 1, 4), op=ALU.mult))
                pts.append((P_, Pb_))
            IMP, IMPb = Mp[mcnt % 2]
            mcnt += 1
            for h in range(4):
                for nt in range(nts):
                    P_, Pb_ = pts[nt]
                    cx.op("pe", [Pb_, self.vcc1b], [OCWb], lambda e: e.matmul(OCW[:, h, 0:65], lhsT=P_[:, 0, h * 128:(h + 1) * 128], rhs=self.vcc1[:, nt, g, 0:65], start=(nt == 0 and h == 0), stop=(nt == nts - 1 and h == 3)), sig=(nt == nts - 1 and h == 3))
            for h in range(4):
                for nt in range(nts):
                    P_, Pb_ = pts[nt]
                    cx.op("pe", [Pb_, ovlb], [IMPb], lambda e: e.matmul(IMP[:, h, :], lhsT=P_[:, 0, h * 128:(h + 1) * 128], rhs=ovl[:, nt, :], start=(nt == 0 and h == 0), stop=(nt == nts - 1 and h == 3)), sig=(h == 3 and nt == nts - 1))
            state[idx] = (IMP, IMPb)

        def main_stage(idx):
            nonlocal mcnt, mscnt
            ci, c, g = units[idx]
            sc = Q0 + c
            sb_, sbb_ = selb[ci % 2]
            u = idx % 2
            pb = 64 * g
            QB, QBb = QB2[g]
            kT, kTb = kslT2[g]
            qv = QB[pb:pb + 64, :, c * 128:(c + 1) * 128]
            qlo = QB[0:64, :, c * 128:(c + 1) * 128]
            qhi = QB[64:128, :, c * 128:(c + 1) * 128]
            sm_, smb_ = small[u]
            IMP, IMPb = state.pop(idx)
            oc_, ocb_ = OCs[u]
            cx.op("dve", [OCWb], [ocb_], lambda e: e.tensor_copy(out=oc_[:, :, 0:65], in_=OCW[:, :, 0:65]))
            cx.op("dve", [ocb_], [smb_], lambda e: e.tensor_scalar_max(out=sm_[:, 0:4], in0=oc_[:, :, 64], scalar1=1e-30))
            cx.op("dve", [smb_], [smb_], lambda e: e.reciprocal(out=sm_[:, 0:4], in_=sm_[:, 0:4]))
            sco, scob = score[u]
            cx.op("dve", [IMPb, smb_], [scob], lambda e: e.tensor_scalar(out=sco[:, 0, :], in0=IMP[:, 0, :], scalar1=sm_[:, 0:1], scalar2=None, op0=ALU.mult))
            for h in range(1, 4):
                cx.op("dve", [IMPb, smb_, scob], [scob], lambda e: e.scalar_tensor_tensor(out=sco[:, 0, :], in0=IMP[:, h, :], scalar=sm_[:, h:h + 1], in1=sco[:, 0, :], op0=ALU.mult, op1=ALU.add))
            cx.op("dve", [scob, sbb_], [scob], lambda e: e.tensor_tensor(out=sco[:, 0, :], in0=sco[:, 0, :], in1=sb_[:, :], op=ALU.add))
            cx.op("dve", [scob], [smb_], lambda e: e.max(out=sm_[:, 32:40], in_=sco[:, 0, :]))
            cx.op("dve", [scob, smb_], [scob], lambda e: e.match_replace(out=sco[:, 1, :], in_to_replace=sm_[:, 32:40], in_values=sco[:, 0, :], imm_value=-3e38))
            cx.op("dve", [scob], [smb_], lambda e: e.max(out=sm_[:, 40:48], in_=sco[:, 1, :]))
            cx.op("dve", [scob, smb_], [scob], lambda e: e.tensor_scalar(out=sco[:, 1, :], in0=sco[:, 0, :], scalar1=sm_[:, 47:48], scalar2=None, op0=ALU.is_ge))
            sbi, sbib = selbias[u]
            cx.op("dve", [scob, realbb], [sbib], lambda e: e.tensor_tensor(out=sbi[:, :], in0=sco[:, 1, :], in1=realb[:, :], op=ALU.mult))
            queue = []

            def pv_emit(pend):
                (P_, Pb_, j, O_, Ob_, V_, Vb_, kt, first, last) = pend
                for h in range(4):
                    cx.op("pe", [Pb_, Vb_], [Ob_], lambda e: e.matmul(O_[:, h, 0:65], lhsT=P_[:, j, h * 128:(h + 1) * 128], rhs=V_[:, kt, g, 0:65], start=(first and h == 0), stop=(last and h == 3)), sig=(h == 3))

            def push(pend, lag=2):
                queue.append(pend)
                while len(queue) > lag:
                    pv_emit(queue.pop(0))

            kts = list(range(sc - 4, sc + 1))
            for ki, kt in enumerate(kts):
                S2, Sb_, P_, Pb_ = nxt()
                mi = 2 if ki == 0 else (0 if ki == 4 else None)
                cx.op("pe", [kwTb, QBb], [Sb_], lambda e: e.matmul(S2[:, 0, :], lhsT=kwT[pb:pb + 64, kt * 128:(kt + 1) * 128], rhs=qv, start=True, stop=True))
                cx.op("act", [Sb_, self.keybiasb], [Pb_], lambda e: e.activation(out=P_[:, 0, :], in_=S2[:, 0, :], func=AF.Exp, bias=self.keybias[:, kt:kt + 1], scale=0.125))
                if mi is not None:
                    pw4 = P_[:, 0, :].rearrange("p (h q) -> p h q", h=4)
                    cx.op("pool", [Pb_, m01b], [Pb_], lambda e: e.tensor_tensor(out=pw4, in0=pw4, in1=bcast(m01[:, mi, :], 1, 4), op=ALU.mult))
                push((P_, Pb_, 0, OCW, OCWb, vw1, vw1b, kt, ki == 0, ki == 4))
            Mt, Mtb = Mp[mcnt % 2]
            mcnt += 1
            cx.op("pe", [sbib, self.identfb], [Mtb], lambda e: e.transpose(out=Mt[:, 0, :], in_=sbi[:, :], identity=self.identf[:]))
            sT, sTb = selT[u]
            cx.op("dve", [Mtb], [sTb], lambda e: e.tensor_copy(out=sT[:, :], in_=Mt[:, 0, :]))
            npairs = (sc + 2) // 2
            for p in range(npairs):
                if p % 2 == 0:
                    M_, Mb_ = Mp[mcnt % 2]
                    mcnt += 1
                    Ms_, Msb_ = Msb[mscnt % 3]
                    mscnt += 1
                    nk4 = min(4, sc + 1 - 2 * p)
                    for j in range(nk4):
                        cx.op("pe", [eallb, sTb], [Mb_], lambda e: e.matmul(M_[:, j, :], lhsT=eall[:, (2 * p + j) * 128:(2 * p + j + 1) * 128], rhs=sT[:, :], start=True, stop=True), sig=(j == nk4 - 1))
                    cx.op("act", [Mb_], [Msb_], lambda e: e.copy(out=Ms_[:, 0:nk4, :], in_=M_[:, 0:nk4, :]))
                S2, Sb_, P_, Pb_ = nxt()
                nk = 2 if 2 * p + 1 <= sc else 1
                for j in range(nk):
                    kt = 2 * p + j
                    diag = (kt == sc)
                    cx.op("pe", [kTb, QBb], [Sb_], lambda e: e.matmul(S2[:, j, :], lhsT=kT[64 * j:64 * j + 64, p, :], rhs=(qlo if j == 0 else qhi), start=True, stop=True), sig=(j == nk - 1))
                cx.op("act", [Sb_], [Pb_], lambda e: e.activation(out=P_[:, 0:nk, :], in_=S2[:, 0:nk, :], func=AF.Exp, scale=0.125))
                if 2 * p + nk - 1 == sc:
                    pd4 = P_[:, nk - 1, :].rearrange("p (h q) -> p h q", h=4)
                    cx.op("pool", [Pb_, m01b], [Pb_], lambda e: e.tensor_tensor(out=pd4, in0=pd4, in1=bcast(m01[:, 0, :], 1, 4), op=ALU.mult))
                pv5 = P_[:, 0:nk, :].rearrange("p k (h q) -> p k h q", h=4)
                m0 = (2 * p) % 4
                cx.op("dve", [Pb_, Msb_], [Pb_], lambda e: e.tensor_tensor(out=pv5, in0=pv5, in1=bcast(Ms_[:, m0:m0 + nk, :], 2, 4), op=ALU.mult))
                for j in range(nk):
                    kt = 2 * p + j
                    push((P_, Pb_, j, OS, OSb, vsl1, vsl1b, kt, kt == 0, kt == sc), lag=8)
            while queue:
                pv_emit(queue.pop(0))
            cx.op("dve", [OSb], [smb_], lambda e: e.tensor_scalar_max(out=sm_[:, 4:8], in0=OS[:, :, 64], scalar1=1e-30))
            cx.op("dve", [OCWb, smb_], [smb_], lambda e: e.tensor_scalar_max(out=sm_[:, 8:12], in0=OCW[:, :, 64], scalar1=1e-30))
            cx.op("dve", [smb_], [smb_], lambda e: e.reciprocal(out=sm_[:, 4:12], in_=sm_[:, 4:12]))
            for br in range(3):
                gv = self.gates[:, c, g * 12 + br:g * 12 + br + 10:3]
                cx.op("dve", [smb_, self.gatesb], [smb_], lambda e: e.tensor_tensor(out=sm_[:, 16 + 4 * br:20 + 4 * br], in0=sm_[:, 4 * br:4 * br + 4], in1=gv, op=ALU.mult))
            tm, tmb = tmpo[u]
            for br, (O_, Ob_) in enumerate(((oc_, ocb_), (OS, OSb), (OCW, OCWb))):
                cx.op("dve", [Ob_, smb_], [tmb], lambda e: e.tensor_tensor(out=tm[:, br, :, :], in0=O_[:, :, 0:64], in1=bcast(sm_[:, 16 + 4 * br:20 + 4 * br], 2, 64), op=ALU.mult))

        def tail_stage(idx):
            nonlocal mcnt
            ci, c, g = units[idx]
            u = idx % 2
            tm, tmb = tmpo[u]
            cx.op("pool", [tmb], [tmb], lambda e: e.tensor_tensor(out=tm[:, 0, :, :], in0=tm[:, 0, :, :], in1=tm[:, 1, :, :], op=ALU.add))
            o_, ob_ = ob[u]
            cx.op("pool", [tmb], [ob_], lambda e: e.tensor_tensor(out=o_[:, :].rearrange("p (h d) -> p h d", d=64), in0=tm[:, 0, :, :], in1=tm[:, 2, :, :], op=ALU.add))
            Mt, Mtb = Mp[mcnt % 2]
            mcnt += 1
            for j in range(2):
                cx.op("pe", [ob_, self.identfb], [Mtb], lambda e: e.transpose(out=Mt[:, j, :], in_=o_[:, j * 128:(j + 1) * 128], identity=self.identf[:]), sig=(j == 1))
            oT, oTb = obT[u]
            cx.op("act", [Mtb], [oTb], lambda e: e.copy(out=oT[:, :, :], in_=Mt[:, 0:2, :]))
            r0 = (8 + 4 * g) * 64
            otv = self.ot.rearrange("h d t -> (h d) t")
            for j in range(2):
                cx.dma([oTb], [self.otb], otv[r0 + j * 128:r0 + (j + 1) * 128, c * 128:(c + 1) * 128], oT[:, j, :])

        load_tables(0)
        cmp_stage(0)
        for idx in range(len(units)):
            main_stage(idx)
            if idx + 1 < len(units):
                cmp_stage(idx + 1)
            tail_stage(idx)
        cx.barrier()


K.phase_attnB = _attnB


def _bc_load(self, es, name, src_row):
    t, b = self.sb(es, name, [128, D], F32)
    src = bass.AP(src_row.tensor, src_row.offset, [[0, 128], [1, D]])
    self.cx.dma([], [b], t[:, :], src)
    return t, b


K.bc_load = _bc_load


def _post1(self, chunks=None):
    cx, nc = self.cx, self.nc
    chunks = list(range(NQ)) if chunks is None else chunks
    with contextlib.ExitStack() as es:
        wo, wob = self.sb(es, "p_wo", [128, 8, D], BF16)
        wst = [self.sb(es, "p_wost%d" % i, [128, 2, D], F32) for i in range(2)]
        wov = self.inp["w_o"].rearrange("(hp p) n -> p hp n", p=128)
        for q in range(4):
            st, stb = wst[q % 2]
            cx.dma([], [stb], st[:, :, :], wov[:, q * 2:(q + 1) * 2, :])
            cx.op("dve" if q % 2 == 0 else "pool", [stb], [wob], lambda e: e.tensor_copy(out=wo[:, q * 2:(q + 1) * 2, :], in_=st[:, :, :]))
        g1, g1b = self.bc_load(es, "p_ln1g", self.inp["ln1_g"])
        b1, b1b = self.bc_load(es, "p_ln1b", self.inp["ln1_b"])
        OTt = [self.sb(es, "p_OT%d" % i, [128, 8, 128], BF16) for i in range(2)]
        xt = [self.sb(es, "p_x%d" % i, [128, D], F32) for i in range(2)]
        tt = [self.sb(es, "p_t%d" % i, [128, D], F32) for i in range(2)]
        x1 = [self.sb(es, "p_x1%d" % i, [128, D], F32) for i in range(2)]
        lns = self.ln_stats(es, "lnq")
        Y = [self.ps(es, "p_Y%d" % i, [128, 512], F32) for i in range(4)]
        xk = self.inp["xk"].rearrange("(t p) d -> t p d", p=128)
        otv = self.ot.rearrange("(hp two) d t -> two d hp t", two=2)

        def load(ci):
            c = chunks[ci]
            i = ci % 2
            for two in range(2):
                cx.dma([self.otb], [OTt[i][1]], OTt[i][0][64 * two:64 * two + 64, :, :], otv[two][:, :, c * 128:(c + 1) * 128])
            cx.dma([], [xt[i][1]], xt[i][0][:, :], xk[Q0 + c, :, :])

        load(0)
        for ci, c in enumerate(chunks):
            i = ci % 2
            if ci + 1 < len(chunks):
                load(ci + 1)
            O_, Ob_ = OTt[i]
            x_, xb_ = xt[i]
            t_, tb_ = tt[i]
            for half in range(2):
                Y_, Yb_ = Y[(ci * 2 + half) % 4]
                for hp in range(8):
                    cx.op("pe", [Ob_, wob], [Yb_], lambda e: e.matmul(Y_[:, :], lhsT=O_[:, hp, :], rhs=wo[:, hp, half * 512:(half + 1) * 512], start=(hp == 0), stop=(hp == 7)), sig=(hp == 7))
                cx.op("dve", [Yb_, self.gbcb], [tb_], lambda e: e.tensor_tensor(out=t_[:, half * 512:(half + 1) * 512], in0=Y_[:, :], in1=self.gbc[:, 0, half * 512:(half + 1) * 512], op=ALU.mult))
            cx.op("dve", [xb_, tb_], [tb_], lambda e: e.scalar_tensor_tensor(out=t_[:, :], in0=x_[:, :], scalar=ALPHA, in1=t_[:, :], op0=ALU.mult, op1=ALU.add))
            mv, mvb = self.emit_ln_stats(lns[i], t_, tb_)
            x1_, x1b_ = x1[i]
            cx.op("act", [tb_, mvb], [x1b_], lambda e: e.activation(out=x1_[:, :], in_=t_[:, :], func=AF.Identity, bias=mv[:, 3:4], scale=mv[:, 2:3]))
            cx.op("dve", [x1b_, g1b], [x1b_], lambda e: e.tensor_tensor(out=x1_[:, :], in0=x1_[:, :], in1=g1[:, :], op=ALU.mult))
            cx.op("pool", [x1b_, b1b], [x1b_], lambda e: e.tensor_tensor(out=x1_[:, :], in0=x1_[:, :], in1=b1[:, :], op=ALU.add))
            cx.dma([x1b_], [self.x1b], self.x1d[c * 128:(c + 1) * 128, :], x1_[:, :])
        cx.barrier()


K.phase_post1 = _post1


def _post2(self, chunks=None):
    cx, nc = self.cx, self.nc
    chunks = list(range(NQ)) if chunks is None else chunks
    assert chunks[0] == 0
    body = chunks[1:]
    groups = [body[i:i + 2] for i in range(0, len(body), 2)]
    with contextlib.ExitStack() as es:
        wup, _ = self.sb(es, "f_wup", [128, 8, 2 * DFF], BF16)
        wupk = [cx.buf() for k in range(8)]
        wdn, wdnb = self.sb(es, "f_wdn", [128, NFC, D], BF16)
        if self.wcast_done:
            for kc in range(8):
                cx.dma([self.wupdb], [wupk[kc]], wup[:, kc, :], self.wupd[kc, :, :])
            for q in range(2):
                cx.dma([self.wdndb], [wdnb], wdn[:, q * 11:(q + 1) * 11, :], self.wdnd[:, q * 11:(q + 1) * 11, :])
        else:
            with contextlib.ExitStack() as es2:
                wst = [self.sb(es2, "f_wst%d" % i, [128, DFF], F32) for i in range(2)]
                wupv = self.inp["w_up"].rearrange("(k p) n -> k p n", p=128)
                n = 0
                for kc in range(8):
                    for hf in range(2):
                        st, stb = wst[n % 2]
                        cx.dma([], [stb], st[:, :], wupv[kc, :, hf * DFF:(hf + 1) * DFF])
                        cx.op("dve" if n % 2 == 0 else "pool", [stb], [wupk[kc]], lambda e: e.tensor_copy(out=wup[:, kc, hf * DFF:(hf + 1) * DFF], in_=st[:, :]))
                        n += 1
                wdnv = self.inp["w_down"].rearrange("(f p) n -> p f n", p=128)
                for q in range(11):
                    st, stb = wst[n % 2]
                    sv = st[:, 0:2 * D].rearrange("p (f n) -> p f n", n=D)
                    cx.dma([], [stb], sv, wdnv[:, 2 * q:2 * q + 2, :])
                    cx.op("dve" if n % 2 == 0 else "pool", [stb], [wdnb], lambda e: e.tensor_copy(out=wdn[:, 2 * q:2 * q + 2, :], in_=sv))
                    n += 1
                cx.barrier()
        g2, g2b = self.bc_load(es, "f_ln2g", self.inp["ln2_g"])
        b2, b2b = self.bc_load(es, "f_ln2b", self.inp["ln2_b"])
        cw, cwb = self.sb(es, "f_cw", [128, 3, NFC], F32)
        cb, cbb = self.sb(es, "f_cb", [128, NFC], F32)
        cx.dma([], [cwb], cw[:, :, :], self.inp["conv_w"].rearrange("k (f p) -> p k f", p=128), allow_slow_non_contiguous=True)
        cx.dma([], [cbb], cb[:, :], self.inp["conv_b"].rearrange("o (f p) -> p (o f)", p=128), allow_slow_non_contiguous=True)
        halo, halob = self.sb(es, "f_halo", [128, 1], F32)
        cx.dma([], [halob], halo[:, :], self.inp["halo"][:, :])
        Hs, _ = self.sb(es, "f_Hs", [128, NFC, 2], F32)
        Hk = [cx.buf() for f in range(NFC)]
        cx.op("pool", [], Hk, lambda e: e.memset(Hs[:], 0.0))
        NG = 2
        W = 128 * NG
        x1t = [self.sb(es, "f_x1%d" % i, [128, NG, D], F32) for i in range(2)]
        xn = [self.sb(es, "f_xn%d" % i, [128, D], BF16) for i in range(2)]
        uT = [self.sb(es, "f_uT%d" % i, [128, 8, W], BF16) for i in range(2)]
        uTk = [[cx.buf() for k in range(8)] for i in range(2)]
        Gf = [self.sb(es, "f_G%d" % i, [128, W + 2], F32) for i in range(3)]
        cv = [self.sb(es, "f_cv%d" % i, [128, 2, W], F32) for i in range(2)]
        gl = [self.sb(es, "f_gl%d" % i, [128, W], F32) for i in range(2)]
        hT, _ = self.sb(es, "f_hT", [128, NFC, W], BF16)
        hTk = [cx.buf() for f in range(NFC)]
        tt = [self.sb(es, "f_t%d" % i, [128, D], F32) for i in range(2)]
        lns = self.ln_stats(es, "lnf")
        lns2 = self.ln_stats(es, "lng")
        tp, _ = self.ps(es, "f_tp", [128, 8, 128], BF16)
        tpk1 = cx.buf()
        tpk1.psum = True
        tpk = [tpk1] * 8
        GV = [self.ps(es, "f_GV%d" % i, [128, 2, W], F32) for i in range(3)]
        Y = [self.ps(es, "f_Y%d" % i, [128, 512], F32) for i in range(4)]
        lcnt = [0]

        def stage_a(gi, cl):
            i = gi % 2
            x_, xb_ = x1t[i]
            for j, c in enumerate(cl):
                cx.dma([self.x1b], [xb_], x_[:, j, :], self.x1d[c * 128:(c + 1) * 128, :])
            for j, c in enumerate(cl):
                k = lcnt[0] % 2
                lcnt[0] += 1
                mv, mvb = self.emit_ln_stats(lns[k], x_[:, j, :], xb_)
                xn_, xnb_ = xn[k]
                cx.op("act", [xb_, mvb], [xnb_], lambda e: e.activation(out=xn_[:, :], in_=x_[:, j, :], func=AF.Identity, bias=mv[:, 3:4], scale=mv[:, 2:3]))
                self.emit_modT(xn_, xnb_, tp, tpk, uT[i][0][:, :, j * 128:(j + 1) * 128], uTk[i], 32, 24)

        def up_mm(gi, f, w, P_, Pb_, both):
            i = gi % 2
            uT_ = uT[i][0]
            for kc in range(8):
                cx.op("pe", [uTk[i][kc], wupk[kc]], [Pb_], lambda e: e.matmul(P_[:, 0, 0:w], lhsT=wup[:, kc, f * 128:(f + 1) * 128], rhs=uT_[:, kc, 0:w], start=(kc == 0), stop=(kc == 7)), sig=(kc == 7 and not both))
            if both:
                for kc in range(8):
                    cx.op("pe", [uTk[i][kc], wupk[kc]], [Pb_], lambda e: e.matmul(P_[:, 1, 0:w], lhsT=wup[:, kc, DFF + f * 128:DFF + (f + 1) * 128], rhs=uT_[:, kc, 0:w], start=(kc == 0), stop=(kc == 7)), sig=(kc == 7))

        stage_a(0, [0])
        if groups:
            stage_a(1, groups[0])
        cnt = 0
        for f in range(NFC):
            P_, Pb_ = GV[cnt % 3]
            G_, Gb_ = Gf[cnt % 3]
            cnt += 1
            up_mm(0, f, 128, P_, Pb_, False)
            cx.op("act", [Pb_], [Gb_], lambda e: e.copy(out=G_[:, 2:130], in_=P_[:, 0, 0:128]))
            cx.op("dve", [Gb_, halob], [Hk[f]], lambda e: e.tensor_scalar(out=Hs[:, f, :], in0=G_[:, 128:130], scalar1=halo[:, 0:1], scalar2=None, op0=ALU.mult))
        ecnt = 0
        prev = None

        def down_mm(pg, f):
            (pcl, pi) = pg
            for j, c in enumerate(pcl):
                for half in range(2):
                    Y_, Yb_ = Y[j * 2 + half]
                    cx.op("pe", [hTk[f], wdnb], [Yb_], lambda e: e.matmul(Y_[:, :], lhsT=hT[:, f, j * 128:(j + 1) * 128], rhs=wdn[:, f, half * 512:(half + 1) * 512], start=(f == 0), stop=(f == NFC - 1)), sig=(f == NFC - 1))

        def down_tail(pg, gidx):
            (pcl, pi) = pg
            x_, xb_ = x1t[pi]
            for j, c in enumerate(pcl):
                t_, tb_ = tt[j % 2]
                for half in range(2):
                    Y_, Yb_ = Y[j * 2 + half]
                    cx.op("dve", [Yb_, self.gbcb], [tb_], lambda e: e.tensor_tensor(out=t_[:, half * 512:(half + 1) * 512], in0=Y_[:, :], in1=self.gbc[:, 1, half * 512:(half + 1) * 512], op=ALU.mult))
                cx.op("dve", [xb_, tb_], [tb_], lambda e: e.scalar_tensor_tensor(out=t_[:, :], in0=x_[:, j, :], scalar=ALPHA, in1=t_[:, :], op0=ALU.mult, op1=ALU.add))
                mv2, mv2b = self.emit_ln_stats(lns2[j % 2], t_, tb_)
                cx.op("act", [tb_, mv2b], [tb_], lambda e: e.activation(out=t_[:, :], in_=t_[:, :], func=AF.Identity, bias=mv2[:, 3:4], scale=mv2[:, 2:3]))
                cx.op("pool", [tb_, g2b], [tb_], lambda e: e.tensor_tensor(out=t_[:, :], in0=t_[:, :], in1=g2[:, :], op=ALU.mult))
                cx.op("pool", [tb_, b2b], [tb_], lambda e: e.tensor_tensor(out=t_[:, :], in0=t_[:, :], in1=b2[:, :], op=ALU.add))
                cx.dma([tb_], [self.outbuf], self.out[(c - 1) * 128:c * 128, :], t_[:, :])

        for gi0, cl in enumerate(groups):
            gi = gi0 + 1
            i = gi % 2
            w = 128 * len(cl)
            pend = None

            def elem_a(pd):
                nonlocal ecnt
                (f, P_, Pb_, G_, Gb_) = pd
                cv_, cvb_ = cv[ecnt % 2]
                gl_, glb_ = gl[ecnt % 2]
                ecnt += 1
                cx.op("act", [Pb_], [Gb_], lambda e: e.copy(out=G_[:, 2:2 + w], in_=P_[:, 0, 0:w]))
                cx.op("act", [Hk[f]], [Gb_], lambda e: e.copy(out=G_[:, 0:2], in_=Hs[:, f, :]))
                cx.op("dve", [Gb_, cwb, cbb], [cvb_], lambda e: e.tensor_scalar(out=cv_[:, 0, 0:w], in0=G_[:, 2:2 + w], scalar1=cw[:, 2, f:f + 1], scalar2=cb[:, f:f + 1], op0=ALU.mult, op1=ALU.add))
                cx.op("dve", [Gb_, cwb, cvb_], [cvb_], lambda e: e.scalar_tensor_tensor(out=cv_[:, 1, 0:w], in0=G_[:, 1:1 + w], scalar=cw[:, 1, f:f + 1], in1=cv_[:, 0, 0:w], op0=ALU.mult, op1=ALU.add))
                cx.op("dve", [Gb_, cwb, cvb_], [cvb_], lambda e: e.scalar_tensor_tensor(out=cv_[:, 0, 0:w], in0=G_[:, 0:w], scalar=cw[:, 0, f:f + 1], in1=cv_[:, 1, 0:w], op0=ALU.mult, op1=ALU.add))
                cx.op("act", [Gb_], [Hk[f]], lambda e: e.copy(out=Hs[:, f, :], in_=G_[:, w:w + 2]))
                cx.op("act", [cvb_], [glb_], lambda e: e.activation(out=gl_[:, 0:w], in_=cv_[:, 0, 0:w], func=AF.Gelu_apprx_tanh))
                return (f, P_, Pb_, gl_, glb_)

            def elem_b(pd):
                (f, P_, Pb_, gl_, glb_) = pd
                cx.op("dve", [glb_, Pb_], [hTk[f]], lambda e: e.tensor_tensor(out=hT[:, f, 0:w], in0=gl_[:, 0:w], in1=P_[:, 1, 0:w], op=ALU.mult))

            pend_b = None
            for f in range(NFC):
                P_, Pb_ = GV[cnt % 3]
                G_, Gb_ = Gf[cnt % 3]
                cnt += 1
                up_mm(gi, f, w, P_, Pb_, True)
                if prev is not None:
                    down_mm(prev, f)
                nb = elem_a(pend) if pend is not None else None
                if pend_b is not None:
                    elem_b(pend_b)
                pend_b = nb
                pend = (f, P_, Pb_, G_, Gb_)
            if prev is not None:
                down_tail(prev, gi0 - 1)
            nb = elem_a(pend)
            if pend_b is not None:
                elem_b(pend_b)
            elem_b(nb)
            if gi0 + 1 < len(groups):
                stage_a(gi + 1, groups[gi0 + 1])
            prev = (cl, i)
        if prev is not None:
            for f in range(NFC):
                down_mm(prev, f)
            down_tail(prev, len(groups) - 1)
        cx.barrier()


K.phase_post2 = _post2
```

```python
import contextlib
import os
SKIP = set(os.environ.get('KSKIP', '').split(','))
STAGE = int(os.environ.get('KSTAGE', '99'))
import numpy as np
import ml_dtypes
import concourse.bass as bass
import concourse.mybir as mybir
from concourse.bass_utils import run_bass_kernel_spmd

F32 = mybir.dt.float32
BF16 = mybir.dt.bfloat16
AF = mybir.ActivationFunctionType
ALU = mybir.AluOpType
AX = mybir.AxisListType

D = 1024
S = 8192
NT = 64
Q0 = 31
NQ = 33
QTOK = NQ * 128
HD = 64
INW = 2840
DFF = 2816
NFC = 22
ALPHA = 2.0 ** 0.25
EPS = 1e-5
MASKV = -30000.0
VW = 66
FB_QA, FB_KA, FB_QB, FB_KC, FB_VC, FB_KSL, FB_KW = 0, 4, 8, 12, 13, 14, 15
NFB = 16


def bcast(ap, axis, n):
    dims = [list(d) for d in ap.ap]
    dims.insert(axis, [0, n])
    return bass.AP(ap.tensor, ap.offset, dims)


class Buf:
    __slots__ = ("name", "w", "r", "psum")

    def __init__(self, name, psum=False):
        self.name = name
        self.w = {}
        self.r = {}
        self.psum = psum


class Ctx:
    def __init__(self, nc, n_dma_sems=40):
        self.nc = nc
        self.es = contextlib.ExitStack()
        self.eng = {"pe": nc.tensor, "dve": nc.vector, "act": nc.scalar, "pool": nc.gpsimd, "sp": nc.sync}
        self.sem = {}
        self.cnt = {}
        self.semobj = {}
        for e in self.eng:
            s = self.es.enter_context(nc.semaphore("sem_" + e))
            self.sem[e] = s
            self.cnt[e] = 0
        self.dma_sems = [self.es.enter_context(nc.semaphore("dsem%d" % i)) for i in range(n_dma_sems)]
        self.dma_cnt = [0] * n_dma_sems
        self.dma_rr = 0
        self.waited = {}
        self.nbuf = 0
        self.ninst = 0

    def buf(self, name=None):
        self.nbuf += 1
        return Buf(name or ("b%d" % self.nbuf))

    def _key(self, s):
        return id(s)

    def _wait(self, e, deps):
        for k, (s, v) in deps.items():
            if e == "pe" and s is self.sem["pe"]:
                continue
            if self.waited.get((e, k), 0) < v:
                self.eng[e].wait_ge(s, v)
                self.waited[(e, k)] = v

    def _deps(self, reads, writes):
        deps = {}

        def add(dd):
            for k, (s, v) in dd.items():
                if k not in deps or deps[k][1] < v:
                    deps[k] = (s, v)
        for b in reads:
            add(b.w)
        for b in writes:
            add(b.w)
            add(b.r)
        return deps

    def _commit(self, ev, reads, writes):
        k = self._key(ev[0])
        for b in writes:
            b.w = {k: ev}
            b.r = {}
        for b in reads:
            if k not in b.r or b.r[k][1] < ev[1]:
                b.r[k] = ev

    def op(self, e, reads, writes, fn, sig=True):
        if e != "pe":
            px = [b for b in reads if b.psum]
            if px:
                reads = [b for b in reads if not b.psum]
                writes = list(writes) + px
        deps = self._deps(reads, writes)
        self._wait(e, deps)
        inst = fn(self.eng[e])
        self.ninst += 1
        if sig:
            self.cnt[e] += 1
            inst.then_inc(self.sem[e], 1)
            ev = (self.sem[e], self.cnt[e])
        else:
            ev = (self.sem[e], self.cnt[e] + 1)
        self._commit(ev, reads, writes)
        return inst

    def dma(self, reads, writes, out, in_, q="sp", **kw):
        deps = self._deps(reads, writes)
        i = self.dma_rr
        self.dma_rr = (self.dma_rr + 1) % len(self.dma_sems)
        s = self.dma_sems[i]
        if self.dma_cnt[i] > 0:
            deps[self._key(s)] = (s, self.dma_cnt[i])
        self._wait(q, deps)
        inst = self.eng[q].dma_start(out=out, in_=in_, **kw)
        self.ninst += 1
        self.dma_cnt[i] += 16
        inst.then_inc(s, 16)
        ev = (s, self.dma_cnt[i])
        self._commit(ev, reads, writes)
        return ev

    def barrier(self):
        for e in self.eng:
            deps = {}
            for e2 in self.eng:
                if e2 != e and self.cnt[e2] > 0:
                    deps[self._key(self.sem[e2])] = (self.sem[e2], self.cnt[e2])
            for i, sm in enumerate(self.dma_sems):
                if self.dma_cnt[i] > 0:
                    deps[self._key(sm)] = (sm, self.dma_cnt[i])
            self._wait(e, deps)

    def finish(self, bufs):
        deps = self._deps(bufs, [])
        self._wait("sp", deps)

    def close(self):
        self.es.close()


def host_tables(half):
    off = 0 if half == 1 else 32
    t = {}
    slots = np.arange(NT * 128)
    gpos = slots - off * 128
    real = gpos >= 0
    inv = 1.0 / (500000.0 ** (np.arange(0, 16, 2, dtype=np.float32) / 16.0))
    ang = np.where(real, gpos, 0).astype(np.float32)[:, None] * inv[None, :]
    cs = np.concatenate([np.cos(ang), np.cos(ang), -np.sin(ang), np.sin(ang)], -1).astype(np.float32)
    t["ropetab"] = np.ascontiguousarray(cs.reshape(NT, 128, 32).transpose(1, 0, 2))
    kb = np.where(real, 0.0, MASKV).astype(np.float32)
    t["keybias"] = np.ascontiguousarray(kb.reshape(NT, 128).T)
    kl = np.arange(128)[:, None]
    ql = np.arange(128)[None, :]
    m = np.zeros((3, 128, 128), np.float32)
    m[0] = np.where(kl <= ql, 0.0, MASKV)
    m[1] = np.where(kl >= ql, 0.0, MASKV)
    m[2] = np.where(kl > ql, 0.0, MASKV)
    t["masks"] = np.ascontiguousarray(m.transpose(1, 0, 2)).astype(ml_dtypes.bfloat16)
    t["ident"] = np.eye(128, dtype=np.float32).astype(ml_dtypes.bfloat16)
    E = np.zeros((128, NT * 128), np.float32)
    E[(np.arange(NT * 128) // 64), np.arange(NT * 128)] = 1.0
    t["eall"] = E.astype(ml_dtypes.bfloat16)
    cst = np.arange(512) * 16
    sst = np.arange(128) * 64
    ov = np.clip(np.minimum(cst[:, None] + 32, sst[None, :] + 64) - np.maximum(cst[:, None], sst[None, :]), 0, None) / 32.0
    ov[511] = 0.0
    t["overlap"] = np.ascontiguousarray(ov.reshape(4, 128, 128).transpose(1, 0, 2)).astype(ml_dtypes.bfloat16)
    selb = np.zeros((NQ, 128, 128), np.float32)
    cmpm = np.zeros((NQ, 512, 128), np.float32)
    for c in range(NQ):
        sl = Q0 + c
        tq = (sl - off) * 128 + np.arange(128)
        blk = np.arange(128) - off * 2
        cur = tq // 64
        sb = np.zeros((128, 128), np.float32)
        valid = (blk[None, :] >= 0) & (blk[None, :] * 64 <= tq[:, None])
        sb[:] = np.where(valid, 0.0, -1e9 - 1e4 * np.arange(128)[None, :])
        for j, bid in enumerate([cur, cur - 1, np.zeros_like(cur)]):
            hit = (blk[None, :] == bid[:, None]) & (blk[None, :] >= 0)
            sb = np.where(hit, (1.0 + j) * 1e9, sb)
        selb[c] = sb
        ci = np.arange(512)
        cstart_g = ci * 16 - off * 128
        cvalid = (cstart_g[:, None] >= 0) & (cstart_g[:, None] + 31 <= tq[None, :]) & (ci[:, None] < 511)
        cmpm[c] = np.where(cvalid, 1.0, 0.0)
    t["selb"] = np.ascontiguousarray(selb.transpose(1, 0, 2))
    t["cmpm"] = np.ascontiguousarray(cmpm.reshape(NQ, 4, 128, 128).transpose(2, 0, 1, 3)).astype(ml_dtypes.bfloat16)
    t["halo"] = np.full((128, 1), 1.0 if half == 1 else 0.0, np.float32)
    t["realb"] = np.ascontiguousarray(np.broadcast_to(((np.arange(128) - off * 2) >= 0).astype(np.float32)[None, :], (128, 128)))
    return t


TABLE_SPECS = [
    ("ropetab", [128, NT, 32], F32), ("keybias", [128, NT], F32), ("masks", [128, 3, 128], BF16),
    ("ident", [128, 128], BF16), ("eall", [128, NT * 128], BF16), ("overlap", [128, 4, 128], BF16),
    ("selb", [128, NQ, 128], F32), ("cmpm", [128, NQ, 4, 128], BF16), ("halo", [128, 1], F32), ("realb", [128, 128], F32),
]

WEIGHT_SPECS = [
    ("c", [1, D]), ("w_ada", [D, 6 * D]), ("b_ada", [1, 6 * D]), ("w_in", [D, INW]), ("pe_cmp", [32, 64]),
    ("w_ck1", [2048, 256]), ("w_ck2", [256, 64]), ("w_cv1", [2048, 256]), ("w_cv2", [256, 64]),
    ("w_o", [D, D]), ("ln1_g", [1, D]), ("ln1_b", [1, D]), ("w_up", [D, 2 * DFF]), ("conv_w", [3, DFF]),
    ("conv_b", [1, DFF]), ("w_down", [DFF, D]), ("ln2_g", [1, D]), ("ln2_b", [1, D]),
]


class K:
    def __init__(self, debug=(), scratch_in=()):
        self.debug = set(debug)
        self.scratch_in = set(scratch_in)
        nc = bass.Bass("TRN2", target_bir_lowering=False)
        self.nc = nc
        self.cx = Ctx(nc)
        self.inp = {}
        self.inp["xk"] = nc.dram_tensor("xk", [S, D], F32, kind="ExternalInput").ap()
        for name, shape in WEIGHT_SPECS:
            self.inp[name] = nc.dram_tensor(name, shape, F32, kind="ExternalInput").ap()
        for name, shape, dt in TABLE_SPECS:
            self.inp[name] = nc.dram_tensor(name, shape, dt, kind="ExternalInput").ap()
        self.out = nc.dram_tensor("out", [32 * 128, D], F32, kind="ExternalOutput").ap()
        self.outbuf = self.cx.buf("out")
        def kd(n):
            if n in self.scratch_in:
                return "ExternalInput"
            return "ExternalOutput" if n in self.debug else "Internal"
        self.ft = nc.dram_tensor("ft", [NFB, 128, S], BF16, kind=kd("ft")).ap()
        self.ftb = self.cx.buf("ft")
        self.va = nc.dram_tensor("va", [S, 8, VW], BF16, kind=kd("va")).ap()
        self.vab = self.cx.buf("va")
        self.vsl = nc.dram_tensor("vsl", [S, 2, VW], BF16, kind=kd("vsl")).ap()
        self.vslb = self.cx.buf("vsl")
        self.vwn = nc.dram_tensor("vwn", [S, 2, VW], BF16, kind=kd("vwn")).ap()
        self.vwnb = self.cx.buf("vwn")
        self.ot = nc.dram_tensor("ot", [16, 64, QTOK], BF16, kind=kd("ot")).ap()
        self.otb = self.cx.buf("ot")
        self.x1d = nc.dram_tensor("x1d", [QTOK, D], F32, kind=kd("x1d")).ap()
        self.wupd = nc.dram_tensor("wupd", [8, 128, 2 * DFF], BF16, kind="Internal").ap()
        self.wupdb = self.cx.buf("wupd")
        self.wdnd = nc.dram_tensor("wdnd", [128, NFC, D], BF16, kind="Internal").ap()
        self.wdndb = self.cx.buf("wdnd")
        self.wcast_done = False
        self.x1b = self.cx.buf("x1d")
        self.dbg = {}

    def dbg_out(self, name, shape, dt):
        ap = self.nc.dram_tensor("dbg_" + name, shape, dt, kind="ExternalOutput").ap()
        self.dbg[name] = (ap, self.cx.buf("dbg_" + name))
        return self.dbg[name]

    def sb(self, es, name, shape, dt):
        t = es.enter_context(self.nc.sbuf_tensor("s_" + name, shape, dt))
        return t, self.cx.buf(name)

    def ps(self, es, name, shape, dt):
        t = es.enter_context(self.nc.psum_tensor("p_" + name, shape, dt))
        b = self.cx.buf(name)
        b.psum = True
        return t, b

    def consts(self, es):
        cx = self.cx
        self.ident, self.identb = self.sb(es, "ident", [128, 128], BF16)
        cx.dma([], [self.identb], self.ident[:], self.inp["ident"][:, :])
        self.masks, self.masksb = self.sb(es, "masks", [128, 3, 128], BF16)
        cx.dma([], [self.masksb], self.masks[:], self.inp["masks"][:, :, :])
        self.keybias, self.keybiasb = self.sb(es, "keybias", [128, NT], F32)
        cx.dma([], [self.keybiasb], self.keybias[:], self.inp["keybias"][:, :])
        self.modT, self.modTb = self.sb(es, "modT", [128, 48], F32)
        self.gbc, self.gbcb = self.sb(es, "gbc", [128, 2, D], F32)
        self.ones, self.onesb = self.sb(es, "ones", [128, 128], F32)
        cx.op("pool", [], [self.onesb], lambda e: e.memset(self.ones[:], 1.0))
        self.identf, self.identfb = self.sb(es, "identf", [128, 128], F32)
        cx.op("dve", [self.identb], [self.identfb], lambda e: e.tensor_copy(out=self.identf[:], in_=self.ident[:]))
        self.onesbf, self.onesbfb = self.sb(es, "onesbf", [128, 128], BF16)
        cx.op("pool", [], [self.onesbfb], lambda e: e.memset(self.onesbf[:], 1.0))

    def phase_mod(self):
        cx, nc = self.cx, self.nc
        with contextlib.ExitStack() as es:
            cT, cTb = self.sb(es, "cT", [128, 8], F32)
            cx.dma([], [cTb], cT[:], self.inp["c"].rearrange("o (k p) -> p (o k)", p=128), allow_slow_non_contiguous=True)
            bT, bTb = self.sb(es, "bT", [128, 48], F32)
            cx.dma([], [bTb], bT[:], self.inp["b_ada"].rearrange("o (j p) -> p (o j)", p=128), allow_slow_non_contiguous=True)
            brow, browb = self.sb(es, "brow", [1, 2, D], F32)
            cx.dma([], [browb], brow[:, 0, :], self.inp["b_ada"][:, 2 * D:3 * D])
            cx.dma([], [browb], brow[:, 1, :], self.inp["b_ada"][:, 5 * D:6 * D])
            sc, scb = self.sb(es, "silc", [128, 8], BF16)
            cx.op("act", [cTb], [scb], lambda e: e.activation(out=sc[:], in_=cT[:], func=AF.Silu))
            wst = [self.sb(es, "wada_st%d" % i, [128, 6 * D], F32) for i in range(2)]
            wbf = [self.sb(es, "wada_bf%d" % i, [128, 6 * D], BF16) for i in range(2)]
            pp, ppb = self.ps(es, "modpart", [128, 48, 8], F32)
            prow = [self.ps(es, "modrow%d" % i, [1, 512], F32) for i in range(4)]
            wada = self.inp["w_ada"].rearrange("(k p) n -> k p n", p=128)
            for kc in range(8):
                st, stb = wst[kc % 2]
                wb, wbb = wbf[kc % 2]
                cx.dma([], [stb], st[:, 0:3 * D], wada[kc, :, 0:3 * D])
                cx.dma([], [stb], st[:, 3 * D:6 * D], wada[kc, :, 3 * D:6 * D])
                cx.op("dve", [stb], [wbb], lambda e: e.tensor_copy(out=wb[:, 0:5 * 512], in_=st[:, 0:5 * 512]))
                cx.op("act", [stb], [wbb], lambda e: e.copy(out=wb[:, 5 * 512:10 * 512], in_=st[:, 5 * 512:10 * 512]))
                cx.op("pool", [stb], [wbb], lambda e: e.tensor_copy(out=wb[:, 10 * 512:6 * D], in_=st[:, 10 * 512:6 * D]))
                for j in range(48):
                    cx.op("pe", [wbb, scb], [ppb], lambda e: e.matmul(pp[:, j, kc:kc + 1], lhsT=wb[:, j * 128:(j + 1) * 128],
                                                                       rhs=sc[:, kc:kc + 1], start=True, stop=True), sig=(j == 47))
                for r in range(4):
                    col = (2 * D if r < 2 else 5 * D) + (r % 2) * 512
                    pr, prb = prow[r]
                    cx.op("pe", [wbb, scb], [prb], lambda e: e.matmul(pr[:, :], lhsT=sc[:, kc:kc + 1], rhs=wb[:, col:col + 512],
                                                                       start=(kc == 0), stop=(kc == 7)))
            tmp, tmpb = self.sb(es, "modtmp", [128, 48], F32)
            cx.op("dve", [ppb], [tmpb], lambda e: e.tensor_reduce(out=tmp[:], in_=pp[:], axis=AX.X, op=ALU.add))
            cx.op("dve", [tmpb, bTb], [self.modTb], lambda e: e.tensor_tensor(out=self.modT[:], in0=tmp[:], in1=bT[:], op=ALU.add))
            cx.op("dve", [self.modTb], [self.modTb], lambda e: e.tensor_scalar_add(out=self.modT[:, 8:16], in0=self.modT[:, 8:16], scalar1=1.0))
            cx.op("dve", [self.modTb], [self.modTb], lambda e: e.tensor_scalar_add(out=self.modT[:, 32:40], in0=self.modT[:, 32:40], scalar1=1.0))
            grow, growb = self.sb(es, "grow", [1, 2, D], F32)
            for r in range(4):
                pr, prb = prow[r]
                cx.op("dve", [prb, browb], [growb], lambda e: e.tensor_tensor(out=grow[:, r // 2, (r % 2) * 512:(r % 2) * 512 + 512],
                                                                               in0=pr[:, :], in1=brow[:, r // 2, (r % 2) * 512:(r % 2) * 512 + 512], op=ALU.add))
            pb, pbb = self.ps(es, "gbps", [128, 512], F32)
            for r in range(4):
                c0 = (r % 2) * 512
                cx.op("pe", [growb, self.onesb], [pbb], lambda e: e.matmul(pb[:, :], lhsT=self.ones[0:1, :], rhs=grow[:, r // 2, c0:c0 + 512], start=True, stop=True))
                cx.op("act", [pbb], [self.gbcb], lambda e: e.copy(out=self.gbc[:, r // 2, c0:c0 + 512], in_=pb[:, :]))
            cx.barrier()
            if "mod" in self.debug:
                ap, b = self.dbg_out("modT", [128, 48], F32)
                cx.dma([self.modTb], [b], ap[:, :], self.modT[:])
                ap, b = self.dbg_out("gbc", [128, 2, D], F32)
                cx.dma([self.gbcb], [b], ap[:, :, :], self.gbc[:])

    def ln_stats(self, es, tag, nbufs=2):
        sets = []
        for i in range(nbufs):
            st, stb = self.sb(es, "%s_st%d" % (tag, i), [128, 12], F32)
            mv, mvb = self.sb(es, "%s_mv%d" % (tag, i), [128, 4], F32)
            sets.append((st, stb, mv, mvb))
        return sets

    def emit_ln_stats(self, lnset, x, xb):
        cx = self.cx
        st, stb, mv, mvb = lnset
        cx.op("dve", [xb], [stb], lambda e: e.bn_stats(out=st[:, 0:6], in_=x[:, 0:512]))
        cx.op("dve", [xb, stb], [stb], lambda e: e.bn_stats(out=st[:, 6:12], in_=x[:, 512:1024]))
        cx.op("dve", [stb], [mvb], lambda e: e.bn_aggr(out=mv[:, 0:2], in_=st[:, :]))
        cx.op("dve", [mvb], [mvb], lambda e: e.tensor_scalar_add(out=mv[:, 2:3], in0=mv[:, 1:2], scalar1=EPS))
        cx.op("act", [mvb], [mvb], lambda e: e.activation(out=mv[:, 2:3], in_=mv[:, 2:3], func=AF.Sqrt))
        cx.op("dve", [mvb], [mvb], lambda e: e.reciprocal(out=mv[:, 2:3], in_=mv[:, 2:3]))
        cx.op("dve", [mvb], [mvb], lambda e: e.tensor_scalar(out=mv[:, 3:4], in0=mv[:, 0:1], scalar1=mv[:, 2:3], scalar2=-1.0, op0=ALU.mult, op1=ALU.mult))
        return mv, mvb

    def emit_modT(self, xn, xnb, tp, tpb, uT, uTb, sc_col, sh_col, n=128):
        cx = self.cx
        for kc in range(8):
            cx.op("pe", [xnb, self.identb], [tpb[kc]], lambda e: e.transpose(out=tp[:, kc, :], in_=xn[:, kc * 128:(kc + 1) * 128], identity=self.ident[:]), sig=(kc == 7))
        for kc in range(8):
            use_act = (kc % 2 == 0)
            if "allact" in SKIP:
                use_act = True
            if "alldve" in SKIP:
                use_act = False
            if "plaincopy" in SKIP:
                cx.op("dve", [tpb[kc]], [uTb[kc]], lambda e: e.tensor_copy(out=uT[:, kc, :], in_=tp[:, kc, :]))
            elif use_act:
                cx.op("act", [tpb[kc], self.modTb], [uTb[kc]], lambda e: e.activation(out=uT[:, kc, :], in_=tp[:, kc, :], func=AF.Identity,
                                                                                      bias=self.modT[:, sh_col + kc:sh_col + kc + 1], scale=self.modT[:, sc_col + kc:sc_col + kc + 1]))
            else:
                cx.op("dve", [tpb[kc], self.modTb], [uTb[kc]], lambda e: e.tensor_scalar(out=uT[:, kc, :], in0=tp[:, kc, :], scalar1=self.modT[:, sc_col + kc:sc_col + kc + 1],
                                                                                         scalar2=self.modT[:, sh_col + kc:sh_col + kc + 1], op0=ALU.mult, op1=ALU.add))

    def phase_proj(self, es_persist, tiles=None):
        cx, nc = self.cx, self.nc
        tiles = list(range(NT)) if tiles is None else tiles
        self.gates, self.gatesb = self.sb(es_persist, "gates", [128, NQ, 24], F32)
        with contextlib.ExitStack() as es:
            wbf, wbfb = self.sb(es, "win_bf", [128, 8, INW], BF16)
            wbfk = [cx.buf("win_bf%d" % k) for k in range(8)]
            wst = [self.sb(es, "win_st%d" % i, [128, INW], F32) for i in range(2)]
            win = self.inp["w_in"].rearrange("(k p) n -> k p n", p=128)
            for kc in range(8):
                st, stb = wst[kc % 2]
                cx.dma([], [stb], st[:, :], win[kc, :, :])
                cx.op("dve", [stb], [wbfk[kc]], lambda e: e.tensor_copy(out=wbf[:, kc, 0:1420], in_=st[:, 0:1420]))
                cx.op("pool", [stb], [wbfk[kc]], lambda e: e.tensor_copy(out=wbf[:, kc, 1420:INW], in_=st[:, 1420:INW]))
            rope, ropeb = self.sb(es, "ropetab", [128, NT, 32], F32)
            cx.dma([], [ropeb], rope[:], self.inp["ropetab"][:, :, :])
            xt = [self.sb(es, "xt%d" % i, [128, D], F32) for i in range(2)]
            xn = [self.sb(es, "xn%d" % i, [128, D], BF16) for i in range(2)]
            uT = [self.sb(es, "uT%d" % i, [128, 8, 128], BF16) for i in range(2)]
            uTk = [[cx.buf() for k in range(8)] for i in range(2)]
            lns = self.ln_stats(es, "lnp")
            tp, _ = self.ps(es, "tp", [128, 8, 128], BF16)
            tpk1 = cx.buf()
            tpk1.psum = True
            tpk = [tpk1] * 8
            pg = [self.ps(es, "pg%d" % i, [128, 512], F32) for i in range(3)]
            tpo = [self.ps(es, "tpo%d" % i, [128, 8, 128], BF16) for i in range(2)]
            tokq = [self.sb(es, "tokq%d" % i, [128, NFB * 128], BF16) for i in range(2)]
            rsc = [self.sb(es, "rsc%d" % i, [128, 2, 8, 16], F32) for i in range(2)]
            ftst = [self.sb(es, "ftst%d" % i, [128, NFB, 512], BF16) for i in range(2)]
            vst = [self.sb(es, "vst%d" % i, [128, 8, VW], BF16) for i in range(2)]
            vsst = [self.sb(es, "vsst%d" % i, [128, 2, VW], BF16) for i in range(2)]
            vwst = [self.sb(es, "vwst%d" % i, [128, 2, VW], BF16) for i in range(2)]
            for i in range(2):
                for (t_, b_) in (vst[i], vsst[i], vwst[i]):
                    cx.op("pool", [], [b_], lambda e: e.memset(t_[:], 1.0))
                cx.op("pool", [], [ftst[i][1]], lambda e: e.memset(ftst[i][0][:], 0.0))
            xk = self.inp["xk"].rearrange("(t p) d -> t p d", p=128)
            ngroup = 0
            cst = [self.sb(es, "wc_st%d" % i, [128, DFF], F32) for i in range(2)]
            cbf = [self.sb(es, "wc_bf%d" % i, [128, DFF], BF16) for i in range(2)]
            wupv_ = self.inp["w_up"].rearrange("(k p) n -> k p n", p=128)
            wdnv_ = self.inp["w_down"].rearrange("(f p) n -> p f n", p=128)
            jobs = [("up", kc, hf) for kc in range(8) for hf in range(2)] + [("dn", q, 0) for q in range(11)]
            jobn = [0]

            pend_store = [None]

            def flush_store():
                if pend_store[0] is not None:
                    (bfb, dbuf, dst, src) = pend_store[0]
                    cx.dma([bfb], [dbuf], dst, src)
                    pend_store[0] = None

            def cast_job():
                flush_store()
                if jobn[0] >= len(jobs) or len(tiles) < 40:
                    return
                kind, a_, b_ = jobs[jobn[0]]
                st, stb = cst[jobn[0] % 2]
                bf_, bfb = cbf[jobn[0] % 2]
                jobn[0] += 1
                if kind == "up":
                    cx.dma([], [stb], st[:, :], wupv_[a_, :, b_ * DFF:(b_ + 1) * DFF])
                    cx.op("pool", [stb], [bfb], lambda e: e.tensor_copy(out=bf_[:, :], in_=st[:, :]))
                    pend_store[0] = (bfb, self.wupdb, self.wupd[a_, :, b_ * DFF:(b_ + 1) * DFF], bf_[:, :])
                else:
                    sv = st[:, 0:2 * D].rearrange("p (f n) -> p f n", n=D)
                    bv = bf_[:, 0:2 * D].rearrange("p (f n) -> p f n", n=D)
                    cx.dma([], [stb], sv, wdnv_[:, 2 * a_:2 * a_ + 2, :])
                    cx.op("pool", [stb], [bfb], lambda e: e.tensor_copy(out=bv, in_=sv))
                    pend_store[0] = (bfb, self.wdndb, self.wdnd[:, 2 * a_:2 * a_ + 2, :], bv)
                if jobn[0] == len(jobs):
                    self.wcast_done = True

            def load_x(i, t):
                cx.dma([], [xt[i][1]], xt[i][0][:, :], xk[t, :, :])

            load_x(0, tiles[0])
            pgi = 0
            def stage_a(ti):
                i = ti % 2
                x_, xb_ = xt[i]
                if ti + 1 < len(tiles):
                    load_x(1 - i, tiles[ti + 1])
                mv, mvb = self.emit_ln_stats(lns[i], x_, xb_)
                xn_, xnb_ = xn[i]
                cx.op("act", [xb_, mvb], [xnb_], lambda e: e.activation(out=xn_[:, :], in_=x_[:, :], func=AF.Identity, bias=mv[:, 3:4], scale=mv[:, 2:3]))
                self.emit_modT(xn_, xnb_, tp, tpk, uT[i][0], uTk[i], 8, 0)

            stage_a(0)
            for ti, t in enumerate(tiles):
                i = ti % 2
                isq = t >= Q0
                if ti + 1 < len(tiles):
                    stage_a(ti + 1)
                if ti >= 2 and ti % 2 == 0:
                    cast_job()
                uT_, _ = uT[i]
                tq, tqb = tokq[i]
                rs, rsb = rsc[i]
                g4 = ti // 4
                fst, fstb = ftst[g4 % 2]
                sub = ti % 4
                cs32 = rope[:, t, :]

                def proj(c0, c1):
                    nonlocal pgi
                    p_, pb_ = pg[pgi % 3]
                    pgi += 1
                    for kc in range(8):
                        cx.op("pe", [uTk[i][kc], wbfk[kc]], [pb_], lambda e: e.matmul(p_[:, 0:c1 - c0], lhsT=uT_[:, kc, :], rhs=wbf[:, kc, c0:c1],
                                                                                   start=(kc == 0), stop=(kc == 7)), sig=(kc == 7))
                    return p_, pb_

                def rope_fix(p_, pb_, src_view, dst_view, shp):
                    a, b = shp
                    A = rs[:, 0, 0:a * b, :].rearrange("p (a b) d -> p a b d", a=a)
                    B = rs[:, 1, 0:a * b, :].rearrange("p (a b) d -> p a b d", a=a)

                    def tb(lo, hi):
                        return bcast(bcast(cs32[:, lo:hi], 1, b), 1, a)
                    cx.op("dve", [pb_, ropeb], [rsb], lambda e: e.tensor_tensor(out=A, in0=src_view, in1=tb(0, 16), op=ALU.mult))
                    cx.op("dve", [pb_, ropeb, rsb], [rsb], lambda e: e.tensor_tensor(out=B[:, :, :, 0:8], in0=src_view[:, :, :, 8:16], in1=tb(16, 24), op=ALU.mult))
                    cx.op("dve", [pb_, ropeb, rsb], [rsb], lambda e: e.tensor_tensor(out=B[:, :, :, 8:16], in0=src_view[:, :, :, 0:8], in1=tb(24, 32), op=ALU.mult))
                    cx.op("dve", [rsb, tqb], [tqb], lambda e: e.tensor_tensor(out=dst_view, in0=A, in1=B, op=ALU.add))

                def hv(ap2d, a, b, astride_cols):
                    base = ap2d
                    dims = [list(base.ap[0]), [astride_cols, a], [64, b], [1, 16]]
                    return bass.AP(base.tensor, base.offset, dims)

                for (c0, dst0, need) in ((0, FB_QA * 128, isq), (512, FB_KA * 128, True), (1536, FB_QB * 128, isq)):
                    if not need:
                        continue
                    p_, pb_ = proj(c0, c0 + 512)
                    cx.op("act", [pb_], [tqb], lambda e: e.copy(out=tq[:, dst0:dst0 + 512], in_=p_[:, :]))
                    if STAGE >= 4:
                        rope_fix(p_, pb_, hv(p_[:, 0:512], 1, 8, 0), hv(tq[:, dst0:dst0 + 512], 1, 8, 0), (1, 8))
                if STAGE < 5:
                    continue
                p_, pb_ = proj(1024, 1536)
                v_, vb_ = vst[i]
                cx.op("act", [pb_], [vb_], lambda e: e.copy(out=v_[:, :, 0:64], in_=p_[:, :].rearrange("p (h d) -> p h d", d=64)))
                if "va" not in SKIP:
                    cx.dma([vb_], [self.vab], self.va[t * 128:(t + 1) * 128, :, :], v_[:])
                p_, pb_ = proj(2048, 2560)
                d0 = FB_KC * 128
                cx.op("act", [pb_], [tqb], lambda e: e.copy(out=tq[:, d0:d0 + 384], in_=p_[:, 0:384]))
                rope_fix(p_, pb_, hv(p_[:, 0:384], 2, 2, 256), hv(tq[:, d0:d0 + 384], 2, 2, 256), (2, 2))
                v_, vb_ = vsst[i]
                cx.op("act", [pb_], [vb_], lambda e: e.copy(out=v_[:, :, 0:64], in_=p_[:, 384:512].rearrange("p (h d) -> p h d", d=64)))
                cx.dma([vb_], [self.vslb], self.vsl[t * 128:(t + 1) * 128, :, :], v_[:])
                p_, pb_ = proj(2560, INW)
                d0 = FB_KW * 128
                cx.op("act", [pb_], [tqb], lambda e: e.copy(out=tq[:, d0:d0 + 128], in_=p_[:, 0:128]))
                rope_fix(p_, pb_, hv(p_[:, 0:128], 1, 2, 0), hv(tq[:, d0:d0 + 128], 1, 2, 0), (1, 2))
                v_, vb_ = vwst[i]
                cx.op("act", [pb_], [vb_], lambda e: e.copy(out=v_[:, :, 0:64], in_=p_[:, 128:256].rearrange("p (h d) -> p h d", d=64)))
                cx.dma([vb_], [self.vwnb], self.vwn[t * 128:(t + 1) * 128, :, :], v_[:])
                if isq:
                    cx.op("act", [pb_], [self.gatesb], lambda e: e.activation(out=self.gates[:, t - Q0, :], in_=p_[:, 256:280], func=AF.Sigmoid))
                blocks = list(range(NFB)) if isq else [b for b in range(NFB) if not (FB_QA <= b < FB_KA or FB_QB <= b < FB_KC)]
                for j0 in range(0, len(blocks), 8):
                    bl = blocks[j0:j0 + 8]
                    to_, tob_ = tpo[(j0 // 8) % 2]
                    for jj, b in enumerate(bl):
                        cx.op("pe", [tqb, self.identb], [tob_], lambda e: e.transpose(out=to_[:, jj, :], in_=tq[:, b * 128:(b + 1) * 128], identity=self.ident[:]), sig=(jj == len(bl) - 1))
                    runs = []
                    for jj, b in enumerate(bl):
                        if runs and runs[-1][1] + runs[-1][2] == b:
                            runs[-1][2] += 1
                        else:
                            runs.append([jj, b, 1])
                    for ri, (jj, b, n) in enumerate(runs):
                        if (j0 // 8) % 2 == 0:
                            cx.op("dve", [tob_], [fstb], lambda e: e.tensor_copy(out=fst[:, b:b + n, sub * 128:(sub + 1) * 128], in_=to_[:, jj:jj + n, :]))
                        else:
                            cx.op("act", [tob_], [fstb], lambda e: e.copy(out=fst[:, b:b + n, sub * 128:(sub + 1) * 128], in_=to_[:, jj:jj + n, :]))
                if sub == 3 or ti == len(tiles) - 1:
                    t0 = (t - sub) * 128
                    ntok = (sub + 1) * 128
                    if "ft" not in SKIP:
                        cx.dma([fstb], [self.ftb], self.ft[:, :, t0:t0 + ntok].rearrange("b p t -> p b t"), fst[:, :, 0:ntok])
            flush_store()
            cx.barrier()


def build(debug=(), phases=("mod", "proj", "cmp", "attnA", "attnB", "post1", "post2"), proj_tiles=None, a_hps=(0, 1, 2, 3), a_filter=None, b_chunks=None, p_chunks=None, scratch_in=()):
    k = K(debug=debug, scratch_in=scratch_in)
    cx = k.cx
    with contextlib.ExitStack() as es:
        k.consts(es)
        if "mod" in phases:
            k.phase_mod()
        with contextlib.ExitStack() as es_mid:
            if "proj" in phases:
                k.phase_proj(es_mid, tiles=proj_tiles)
            if "cmp" in phases:
                k.phase_compress(es_mid)
            if "attnA" in phases:
                k.phase_attnA(hps=a_hps, qfilter=a_filter)
            if "attnB" in phases:
                k.phase_attnB(chunks=b_chunks)
            cx.barrier()
        if "post1" in phases:
            k.phase_post1(chunks=p_chunks)
        if "post2" in phases:
            k.phase_post2(chunks=p_chunks)
        allb = [k.outbuf, k.ftb, k.vab, k.vslb, k.vwnb, k.otb, k.x1b] + [b for (_, b) in k.dbg.values()]
        cx.finish(allb)
    return k


def make_in_maps(inputs):
    x = np.asarray(inputs["x"], np.float32)
    maps = []
    tabs = {h: host_tables(h) for h in (0, 1)}
    for core in range(8):
        b, half = core // 2, core % 2
        m = {}
        if half == 1:
            m["xk"] = np.ascontiguousarray(x[b])
        else:
            m["xk"] = np.concatenate([np.zeros((4096, D), np.float32), x[b, 0:4096]], 0)
        for name, shape in WEIGHT_SPECS:
            a = np.asarray(inputs[name], np.float32)
            if name == "c":
                a = a[b:b + 1]
            else:
                a = a[0]
            m[name] = np.ascontiguousarray(a.reshape(shape))
        m.update(tabs[half])
        maps.append(m)
    return maps


def kernel(**inputs):
    k = build()
    maps = make_in_maps(inputs)
    res = run_bass_kernel_spmd(k.nc, maps, core_ids=list(range(8)))
    out = np.zeros((4, S, D), np.float32)
    for core in range(8):
        b, half = core // 2, core % 2
        out[b, half * 4096:(half + 1) * 4096] = res.results[core]["out"]
    return out


def _attnA(self, hps=(0, 1, 2, 3), qfilter=None):
    cx, nc = self.cx, self.nc
    QBASE = Q0 * 128
    pats = []
    for d, mlo, mhi in ((1, 31, 63), (4, 7, 15), (16, 1, 3)):
        for r in range(d):
            for m in range(mlo, mhi + 1):
                j0 = max(0, -(-(QBASE - r) // d) - 128 * m)
                if j0 >= 128:
                    continue
                pats.append((d, r, m, j0))
    if qfilter is not None:
        pats = [p for p in pats if qfilter(p)]
    vtiles = {1: (30, 34), 4: (6, 10), 16: (0, 4)}
    with contextlib.ExitStack() as es:
        self.m01, self.m01b = self.sb(es, "m01", [128, 3, 128], BF16)
        cx.op("dve", [self.masksb], [self.m01b], lambda e: e.tensor_scalar(out=self.m01[:], in0=self.masks[:], scalar1=0.0, scalar2=None, op0=ALU.is_equal))
        KTs = [self.sb(es, "a_KT%d" % i, [128, S], BF16) for i in range(2)]
        QTs = [self.sb(es, "a_QT%d" % i, [128, QTOK], BF16) for i in range(2)]
        Vps = [{d: self.sb(es, "a_V%d_%d" % (d, i), [128, d * vtiles[d][1], 2, VW], BF16) for d in (1, 4, 16)} for i in range(2)]
        acc, accb = self.sb(es, "a_acc", [65, 2, QTOK], F32)
        otst, otstb = self.sb(es, "a_otst", [64, 2, QTOK], BF16)
        PT = [self.sb(es, "a_PT%d" % i, [128, 128], BF16) for i in range(8)]
        STp = [self.ps(es, "a_ST%d" % i, [128, 512], F32) for i in range(3)]
        OTp = [self.ps(es, "a_OT%d" % i, [65, 512], F32) for i in range(4)]
        BCp = [self.ps(es, "a_BC%d" % i, [64, 512], F32) for i in range(1)] * 2
        cnt = 0
        ocnt = 0

        def load(hi):
            hp = hps[hi]
            KT, KTb = KTs[hi % 2]
            QT, QTb = QTs[hi % 2]
            cx.dma([self.ftb], [KTb], KT[:, :], self.ft[FB_KA + hp, :, :])
            cx.dma([self.ftb], [QTb], QT[:, :], self.ft[FB_QA + hp, :, QBASE:S])
            for d in (1, 4, 16):
                k0, nk = vtiles[d]
                V_, Vb_ = Vps[hi % 2][d]
                rows = self.va[d * 128 * k0:d * 128 * (k0 + nk), 2 * hp:2 * hp + 2, :].rearrange("(kk j dd) h w -> dd j kk h w", j=128, dd=d)
                for r in range(d):
                    cx.dma([self.vab], [Vb_], V_[:, r * nk:(r + 1) * nk, :, :], rows[r])

        load(0)
        for hi, hp in enumerate(hps):
            if hi + 1 < len(hps):
                load(hi + 1)
            KT, KTb = KTs[hi % 2]
            QT, QTb = QTs[hi % 2]
            Vp = Vps[hi % 2]
            cx.op("pool", [], [accb], lambda e: e.memset(acc[:], 0.0))
            aq = []

            def pv_emit(pend):
                (P_, Pb_, O_, Ob_, V_, Vb_, vidx, h, n, ki, av) = pend
                cx.op("pe", [Pb_, Vb_], [Ob_], lambda e: e.matmul(O_[:, 0:n], lhsT=V_[:, vidx, h, 0:65], rhs=P_[:, 0:n], start=(ki == 0), stop=(ki == 1)), sig=(ki == 1))
                if ki == 1:
                    cx.op("dve", [Ob_, accb], [accb], lambda e: e.tensor_tensor(out=av, in0=av, in1=O_[:, 0:n], op=ALU.add))

            for (d, r, m, j0) in pats:
                n = 128 - j0
                k0, nk = vtiles[d]
                V_, Vb_ = Vp[d]
                qs = d * (128 * m + j0) + r - QBASE
                for h in range(2):
                    pb = 64 * h
                    qv = QT[pb:pb + 64, qs:qs + d * (n - 1) + 1:d]
                    O_, Ob_ = OTp[ocnt % 4]
                    ocnt += 1
                    av = acc[:, h, qs:qs + d * (n - 1) + 1:d]
                    for ki, (kt, mi) in enumerate(((m - 1, 1), (m, 0))):
                        S_, Sb_ = STp[cnt % 2]
                        P_, Pb_ = PT[cnt % 8]
                        cnt += 1
                        ks = d * 128 * kt + r
                        kv = KT[pb:pb + 64, ks:ks + d * 127 + 1:d]
                        cx.op("pe", [KTb, QTb], [Sb_], lambda e: e.matmul(S_[:, 0:n], lhsT=kv, rhs=qv, start=True, stop=True))
                        cx.op("act", [Sb_, self.keybiasb], [Pb_], lambda e: e.activation(out=P_[:, 0:n], in_=S_[:, 0:n], func=AF.Exp, bias=self.keybias[:, d * kt:d * kt + 1], scale=0.125))
                        cx.op("pool" if cnt % 2 == 0 else "dve", [Pb_, self.m01b], [Pb_], lambda e: e.tensor_tensor(out=P_[:, 0:n], in0=P_[:, 0:n], in1=self.m01[:, mi, j0:128], op=ALU.mult))
                        aq.append((P_, Pb_, O_, Ob_, V_, Vb_, r * nk + (kt - k0), h, n, ki, av))
                        while len(aq) > 4:
                            pv_emit(aq.pop(0))
            while aq:
                pv_emit(aq.pop(0))
            for h in range(2):
                cx.op("dve", [accb], [accb], lambda e: e.tensor_scalar_max(out=acc[64:65, h, :], in0=acc[64:65, h, :], scalar1=1e-30))
                cx.op("dve", [accb], [accb], lambda e: e.reciprocal(out=acc[64:65, h, :], in_=acc[64:65, h, :]))
                for c0 in range(0, QTOK, 512):
                    w = min(512, QTOK - c0)
                    B_, Bb_ = BCp[(c0 // 512) % 2]
                    cx.op("pe", [accb, self.onesb], [Bb_], lambda e: e.matmul(B_[:, 0:w], lhsT=self.ones[64:65, 0:64], rhs=acc[64:65, h, c0:c0 + w], start=True, stop=True))
                    cx.op("dve", [accb, Bb_], [otstb], lambda e: e.tensor_tensor(out=otst[:, h, c0:c0 + w], in0=acc[0:64, h, c0:c0 + w], in1=B_[:, 0:w], op=ALU.mult))
            cx.dma([otstb], [self.otb], self.ot[2 * hp:2 * hp + 2, :, :].rearrange("h d t -> d h t"), otst[:, :, :])
        cx.barrier()


K.phase_attnA = _attnA


def _compress(self, es_persist):
    cx, nc = self.cx, self.nc
    self.kccT, self.kccTb = self.sb(es_persist, "kccT", [128, 512], BF16)
    self.vcc1, self.vcc1b = self.sb(es_persist, "vcc1", [128, 4, 2, VW], BF16)
    cx.op("pool", [], [self.kccTb], lambda e: e.memset(self.kccT[:], 0.0))
    cx.op("pool", [], [self.vcc1b], lambda e: e.memset(self.vcc1[:], 0.0))
    cx.op("pool", [self.vcc1b], [self.vcc1b], lambda e: e.memset(self.vcc1[:, :, :, 64:65], 1.0))
    with contextlib.ExitStack() as es:
        w1st, w1stb = self.sb(es, "c_w1st", [128, 16, 256], F32)
        w1 = [self.sb(es, "c_w1_%d" % i, [128, 16, 256], BF16) for i in range(2)]
        w2st, w2stb = self.sb(es, "c_w2st", [128, 2, 64], F32)
        w2 = [self.sb(es, "c_w2_%d" % i, [128, 2, 64], BF16) for i in range(2)]
        pest, pestb = self.sb(es, "c_pest", [128, 16], F32)
        pebf, pebfb = self.sb(es, "c_pebf", [128, 16], BF16)
        pebias, pebiasb = self.sb(es, "c_pebias", [128, 2, 2], F32)
        X2, X2b = self.sb(es, "c_X2", [128, S], BF16)
        hT, hTb = self.sb(es, "c_hT", [128, 2, 512], BF16)
        HP = [self.ps(es, "c_HP%d" % i, [128, 512], F32) for i in range(2)]
        OP, OPb = self.ps(es, "c_OP", [128, 512], F32)
        BP, BPb = self.ps(es, "c_BP", [128, 4], F32)
        cx.dma([], [pestb], pest[0:64, :], self.inp["pe_cmp"].rearrange("(c j) d -> j d c", j=2)[0], allow_slow_non_contiguous=True)
        cx.dma([], [pestb], pest[64:128, :], self.inp["pe_cmp"].rearrange("(c j) d -> j d c", j=2)[1], allow_slow_non_contiguous=True)
        cx.op("dve", [pestb], [pebfb], lambda e: e.tensor_copy(out=pebf[:], in_=pest[:]))
        for kv, (n1, n2) in enumerate((("w_ck1", "w_ck2"), ("w_cv1", "w_cv2"))):
            cx.dma([], [w1stb], w1st[:, :, :], self.inp[n1].rearrange("(c p) h -> p c h", p=128))
            cx.op("dve", [w1stb], [w1[kv][1]], lambda e: e.tensor_copy(out=w1[kv][0][:, 0:8, :], in_=w1st[:, 0:8, :]))
            cx.op("pool", [w1stb], [w1[kv][1]], lambda e: e.tensor_copy(out=w1[kv][0][:, 8:16, :], in_=w1st[:, 8:16, :]))
            cx.dma([], [w2stb], w2st[:, :, :], self.inp[n2].rearrange("(c p) h -> p c h", p=128))
            cx.op("dve", [w2stb], [w2[kv][1]], lambda e: e.tensor_copy(out=w2[kv][0][:], in_=w2st[:]))
            for hh in range(2):
                for c in range(16):
                    cx.op("pe", [w1[kv][1], pebfb], [BPb], lambda e: e.matmul(BP[:, 2 * kv + hh:2 * kv + hh + 1], lhsT=w1[kv][0][:, c, hh * 128:(hh + 1) * 128], rhs=pebf[:, c:c + 1],
                                                                          start=(c == 0), stop=(c == 15)), sig=(c == 15))
        cx.op("dve", [BPb], [pebiasb], lambda e: e.tensor_copy(out=pebias[:].rearrange("p a b -> p (a b)"), in_=BP[:, :]))
        for kv in range(2):
            blk = FB_KC if kv == 0 else FB_VC
            for g in range(2):
                cx.dma([self.ftb], [X2b], X2[0:64, :], self.ft[blk, g * 64:(g + 1) * 64, :])
                cx.dma([self.ftb], [X2b], X2[64:128, 0:S - 1], self.ft[blk, g * 64:(g + 1) * 64, 1:S])
                if kv == 0 and g == 0:
                    cx.op("pool", [X2b], [X2b], lambda e: e.memset(X2[64:128, S - 1:S], 0.0))
                for hh in range(2):
                    H_, Hb_ = HP[hh]
                    for c in range(16):
                        cx.op("pe", [w1[kv][1], X2b], [Hb_], lambda e: e.matmul(H_[:, 0:511], lhsT=w1[kv][0][:, c, hh * 128:(hh + 1) * 128], rhs=X2[:, 2 * c:2 * c + 16 * 510 + 1:16],
                                                                              start=(c == 0), stop=(c == 15)), sig=(c == 15))
                    cx.op("act", [Hb_, pebiasb], [hTb], lambda e: e.activation(out=hT[:, hh, 0:511], in_=H_[:, 0:511], func=AF.Gelu_apprx_tanh, bias=pebias[:, kv, hh:hh + 1]))
                if kv == 0:
                    for hh in range(2):
                        cx.op("pe", [hTb, w2[0][1]], [OPb], lambda e: e.matmul(OP[64 * g:64 * g + 64, 0:511], lhsT=w2[0][0][:, hh, :], rhs=hT[:, hh, 0:511], start=(hh == 0), stop=(hh == 1)), sig=(hh == 1))
                    cx.op("dve", [OPb], [self.kccTb], lambda e: e.tensor_copy(out=self.kccT[64 * g:64 * g + 64, 0:511], in_=OP[64 * g:64 * g + 64, 0:511]))
                else:
                    for nt in range(4):
                        w = 128 if nt < 3 else 127
                        for hh in range(2):
                            cx.op("pe", [hTb, w2[1][1]], [OPb], lambda e: e.matmul(OP[0:w, nt * 64:(nt + 1) * 64], lhsT=hT[:, hh, nt * 128:nt * 128 + w], rhs=w2[1][0][:, hh, :],
                                                                               start=(hh == 0), stop=(hh == 1)), sig=(hh == 1 and nt == 3))
                    for nt in range(4):
                        w = 128 if nt < 3 else 127
                        cx.op("dve", [OPb], [self.vcc1b], lambda e: e.tensor_copy(out=self.vcc1[0:w, nt, g, 0:64], in_=OP[0:w, nt * 64:(nt + 1) * 64]))
        cx.barrier()
        if "cmp" in self.debug:
            ap, b = self.dbg_out("kccT", [128, 512], BF16)
            cx.dma([self.kccTb], [b], ap[:, :], self.kccT[:])
            ap, b = self.dbg_out("vcc1", [128, 4, 2, VW], BF16)
            cx.dma([self.vcc1b], [b], ap[:, :, :, :], self.vcc1[:])


K.phase_compress = _compress


def _attnB(self, chunks=None):
    cx, nc = self.cx, self.nc
    chunks = list(range(NQ)) if chunks is None else chunks
    QBASE = Q0 * 128
    with contextlib.ExitStack() as es:
        if not hasattr(self, "gates"):
            gin = nc.dram_tensor("gates_in", [128, NQ, 24], F32, kind="ExternalInput").ap()
            self.gates, self.gatesb = self.sb(es, "gates", [128, NQ, 24], F32)
            cx.dma([], [self.gatesb], self.gates[:], gin[:, :, :])
        kslT2 = [self.sb(es, "b_kslT%d" % g, [128, NT // 2, 128], BF16) for g in range(2)]
        kwT, kwTb = self.sb(es, "b_kwT", [128, S], BF16)
        vsl1, vsl1b = self.sb(es, "b_vsl1", [128, NT, 2, VW], BF16)
        vw1, vw1b = self.sb(es, "b_vw1", [128, NT, 2, VW], BF16)
        QB2 = [self.sb(es, "b_QB%d" % g, [128, 4, QTOK], BF16) for g in range(2)]
        eall, eallb = self.sb(es, "b_eall", [128, NT * 128], BF16)
        ovl, ovlb = self.sb(es, "b_ovl", [128, 4, 128], BF16)
        realb, realbb = self.sb(es, "b_realb", [128, 128], F32)
        cmpm = [self.sb(es, "b_cmpm%d" % i, [128, 4, 128], BF16) for i in range(2)]
        selb = [self.sb(es, "b_selb%d" % i, [128, 128], F32) for i in range(2)]
        PT = [self.sb(es, "b_PT%d" % i, [128, 2, 512], BF16) for i in range(8)]
        small = [self.sb(es, "b_small%d" % i, [128, 64], F32) for i in range(2)]
        score = [self.sb(es, "b_score%d" % i, [128, 2, 128], F32) for i in range(2)]
        selbias = [self.sb(es, "b_selbias%d" % i, [128, 128], F32) for i in range(2)]
        selT = [self.sb(es, "b_selT%d" % i, [128, 128], BF16) for i in range(2)]
        OCs = [self.sb(es, "b_OCs%d" % i, [128, 4, VW], F32) for i in range(2)]
        tmpo = [self.sb(es, "b_tmpo%d" % i, [128, 3, 4, 64], F32) for i in range(2)]
        ob = [self.sb(es, "b_ob%d" % i, [128, 256], F32) for i in range(2)]
        obT = [self.sb(es, "b_obT%d" % i, [128, 2, 128], BF16) for i in range(2)]
        STq = es.enter_context(nc.psum_tensor("p_b_STq", [128, 4, 512], F32))
        stb = [cx.buf("stq0"), cx.buf("stq1")]
        for b_ in stb:
            b_.psum = True
        Mp = [self.ps(es, "b_M%d" % i, [128, 4, 128], F32) for i in range(2)]
        OS, OSb = self.ps(es, "b_OS", [128, 4, VW], F32)
        OCW, OCWb = self.ps(es, "b_OCW", [128, 4, VW], F32)

        for g in range(2):
            src = self.ft[FB_KSL, g * 64:(g + 1) * 64, :].rearrange("d (pr par k) -> par d pr k", par=2, k=128)
            for par in range(2):
                cx.dma([self.ftb], [kslT2[g][1]], kslT2[g][0][par * 64:(par + 1) * 64, :, :], src[par])
        cx.dma([self.ftb], [kwTb], kwT[:, :], self.ft[FB_KW, :, :])
        for q4 in range(4):
            r0, r1 = q4 * 16 * 128, (q4 + 1) * 16 * 128
            cx.dma([self.vslb], [vsl1b], vsl1[:, q4 * 16:(q4 + 1) * 16, :, :], self.vsl[r0:r1, :, :].rearrange("(t p) g w -> p t g w", p=128))
            cx.dma([self.vwnb], [vw1b], vw1[:, q4 * 16:(q4 + 1) * 16, :, :], self.vwn[r0:r1, :, :].rearrange("(t p) g w -> p t g w", p=128))
        for g in range(2):
            for hq in range(4):
                blk = FB_QB + (g * 4 + hq) // 2
                prow = ((g * 4 + hq) % 2) * 64
                for half in range(2):
                    cx.dma([self.ftb], [QB2[g][1]], QB2[g][0][64 * half:64 * half + 64, hq, :], self.ft[blk, prow:prow + 64, QBASE:S])
        cx.dma([], [eallb], eall[:, :], self.inp["eall"][:, :])
        cx.dma([], [ovlb], ovl[:, :, :], self.inp["overlap"][:, :, :])
        cx.dma([], [realbb], realb[:, :], self.inp["realb"][:, :])
        Msb = [self.sb(es, "b_Msb%d" % i, [128, 4, 128], BF16) for i in range(3)]
        cnt = 0
        mcnt = 0
        mscnt = 0
        units = [(ci, c, g) for ci, c in enumerate(chunks) for g in range(2)]
        state = {}

        def nxt():
            nonlocal cnt
            i = cnt % 2
            P_, Pb_ = PT[cnt % 8]
            cnt += 1
            return STq[:, 2 * i:2 * i + 2, :], stb[i], P_, Pb_

        def load_tables(ci):
            c = chunks[ci]
            cx.dma([], [cmpm[ci % 2][1]], cmpm[ci % 2][0][:, :, :], self.inp["cmpm"][:, c, :, :])
            cx.dma([], [selb[ci % 2][1]], selb[ci % 2][0][:, :], self.inp["selb"][:, c, :])

        def cmp_stage(idx):
            nonlocal mcnt
            ci, c, g = units[idx]
            sc = Q0 + c
            if g == 0 and ci + 1 < len(chunks):
                load_tables(ci + 1)
            cm_, cmb_ = cmpm[ci % 2]
            pb = 64 * g
            QB, QBb = QB2[g]
            qv = QB[pb:pb + 64, :, c * 128:(c + 1) * 128]
            nts = (8 * sc + 6) // 128 + 1
            pts = []
            for nt in range(nts):
                S2, Sb_, P_, Pb_ = nxt()
                cx.op("pe", [self.kccTb, QBb], [Sb_], lambda e: e.matmul(S2[:, 0, :], lhsT=self.kccT[pb:pb + 64, nt * 128:(nt + 1) * 128], rhs=qv, start=True, stop=True))
                cx.op("act", [Sb_], [Pb_], lambda e: e.activation(out=P_[:, 0, :], in_=S2[:, 0, :], func=AF.Exp, scale=0.125))
                pv4 = P_[:, 0, :].rearrange("p (h q) -> p h q", h=4)
                cx.op("dve", [Pb_, cmb_], [Pb_], lambda e: e.tensor_tensor(out=pv4, in0=pv4, in1=bcast(cm_[:, nt, :], 1, 4), op=ALU.mult))
                pts.append((P_, Pb_))
            IMP, IMPb = Mp[mcnt % 2]
            mcnt += 1
            for h in range(4):
                for nt in range(nts):
                    P_, Pb_ = pts[nt]
                    cx.op("pe", [Pb_, self.vcc1b], [OCWb], lambda e: e.matmul(OCW[:, h, 0:65], lhsT=P_[:, 0, h * 128:(h + 1) * 128], rhs=self.vcc1[:, nt, g, 0:65], start=(nt == 0 and h == 0), stop=(nt == nts - 1 and h == 3)), sig=(nt == nts - 1 and h == 3))
            for h in range(4):
                for nt in range(nts):
                    P_, Pb_ = pts[nt]
                    cx.op("pe", [Pb_, ovlb], [IMPb], lambda e: e.matmul(IMP[:, h, :], lhsT=P_[:, 0, h * 128:(h + 1) * 128], rhs=ovl[:, nt, :], start=(nt == 0 and h == 0), stop=(nt == nts - 1 and h == 3)), sig=(h == 3 and nt == nts - 1))
            state[idx] = (IMP, IMPb)

        def main_stage(idx):
            nonlocal mcnt, mscnt
            ci, c, g = units[idx]
            sc = Q0 + c
            sb_, sbb_ = selb[ci % 2]
            u = idx % 2
            pb = 64 * g
            QB, QBb = QB2[g]
            kT, kTb = kslT2[g]
            qv = QB[pb:pb + 64, :, c * 128:(c + 1) * 128]
            qlo = QB[0:64, :, c * 128:(c + 1) * 128]
            qhi = QB[64:128, :, c * 128:(c + 1) * 128]
            sm_, smb_ = small[u]
            IMP, IMPb = state.pop(idx)
            oc_, ocb_ = OCs[u]
            cx.op("dve", [OCWb], [ocb_], lambda e: e.tensor_copy(out=oc_[:, :, 0:65], in_=OCW[:, :, 0:65]))
            cx.op("dve", [ocb_], [smb_], lambda e: e.tensor_scalar_max(out=sm_[:, 0:4], in0=oc_[:, :, 64], scalar1=1e-30))
            cx.op("dve", [smb_], [smb_], lambda e: e.reciprocal(out=sm_[:, 0:4], in_=sm_[:, 0:4]))
            sco, scob = score[u]
            cx.op("dve", [IMPb, smb_], [scob], lambda e: e.tensor_scalar(out=sco[:, 0, :], in0=IMP[:, 0, :], scalar1=sm_[:, 0:1], scalar2=None, op0=ALU.mult))
            for h in range(1, 4):
                cx.op("dve", [IMPb, smb_, scob], [scob], lambda e: e.scalar_tensor_tensor(out=sco[:, 0, :], in0=IMP[:, h, :], scalar=sm_[:, h:h + 1], in1=sco[:, 0, :], op0=ALU.mult, op1=ALU.add))
            cx.op("dve", [scob, sbb_], [scob], lambda e: e.tensor_tensor(out=sco[:, 0, :], in0=sco[:, 0, :], in1=sb_[:, :], op=ALU.add))
            cx.op("dve", [scob], [smb_], lambda e: e.max(out=sm_[:, 32:40], in_=sco[:, 0, :]))
            cx.op("dve", [scob, smb_], [scob], lambda e: e.match_replace(out=sco[:, 1, :], in_to_replace=sm_[:, 32:40], in_values=sco[:, 0, :], imm_value=-3e38))
            cx.op("dve", [scob], [smb_], lambda e: e.max(out=sm_[:, 40:48], in_=sco[:, 1, :]))
            cx.op("dve", [scob, smb_], [scob], lambda e: e.tensor_scalar(out=sco[:, 1, :], in0=sco[:, 0, :], scalar1=sm_[:, 47:48], scalar2=None, op0=ALU.is_ge))
            sbi, sbib = selbias[u]
            cx.op("dve", [scob, realbb], [sbib], lambda e: e.tensor_tensor(out=sbi[:, :], in0=sco[:, 1, :], in1=realb[:, :], op=ALU.mult))
            queue = []

            def pv_emit(pend):
                (P_, Pb_, j, O_, Ob_, V_, Vb_, kt, first, last) = pend
                for h in range(4):
                    cx.op("pe", [Pb_, Vb_], [Ob_], lambda e: e.matmul(O_[:, h, 0:65], lhsT=P_[:, j, h * 128:(h + 1) * 128], rhs=V_[:, kt, g, 0:65], start=(first and h == 0), stop=(last and h == 3)), sig=(h == 3))

            def push(pend, lag=2):
                queue.append(pend)
                while len(queue) > lag:
                    pv_emit(queue.pop(0))

            kts = list(range(sc - 4, sc + 1))
            for ki, kt in enumerate(kts):
                S2, Sb_, P_, Pb_ = nxt()
                mi = 2 if ki == 0 else (0 if ki == 4 else None)
                cx.op("pe", [kwTb, QBb], [Sb_], lambda e: e.matmul(S2[:, 0, :], lhsT=kwT[pb:pb + 64, kt * 128:(kt + 1) * 128], rhs=qv, start=True, stop=(mi is None)), sig=(mi is None))
                if mi is not None:
                    cx.op("pe", [self.identb, self.masksb], [Sb_], lambda e: e.matmul(S2[:, 0, :], lhsT=self.ident[:, :], rhs=bcast(self.masks[:, mi, :], 1, 4), start=False, stop=True))
                cx.op("act", [Sb_, self.keybiasb], [Pb_], lambda e: e.activation(out=P_[:, 0, :], in_=S2[:, 0, :], func=AF.Exp, bias=self.keybias[:, kt:kt + 1], scale=0.125))
                push((P_, Pb_, 0, OCW, OCWb, vw1, vw1b, kt, ki == 0, ki == 4))
            Mt, Mtb = Mp[mcnt % 2]
            mcnt += 1
            cx.op("pe", [sbib, self.identfb], [Mtb], lambda e: e.transpose(out=Mt[:, 0, :], in_=sbi[:, :], identity=self.identf[:]))
            sT, sTb = selT[u]
            cx.op("dve", [Mtb], [sTb], lambda e: e.tensor_copy(out=sT[:, :], in_=Mt[:, 0, :]))
            npairs = (sc + 2) // 2
            for p in range(npairs):
                if p % 2 == 0:
                    M_, Mb_ = Mp[mcnt % 2]
                    mcnt += 1
                    Ms_, Msb_ = Msb[mscnt % 3]
                    mscnt += 1
                    nk4 = min(4, sc + 1 - 2 * p)
                    for j in range(nk4):
                        cx.op("pe", [eallb, sTb], [Mb_], lambda e: e.matmul(M_[:, j, :], lhsT=eall[:, (2 * p + j) * 128:(2 * p + j + 1) * 128], rhs=sT[:, :], start=True, stop=True), sig=(j == nk4 - 1))
                    cx.op("act", [Mb_], [Msb_], lambda e: e.copy(out=Ms_[:, 0:nk4, :], in_=M_[:, 0:nk4, :]))
                S2, Sb_, P_, Pb_ = nxt()
                nk = 2 if 2 * p + 1 <= sc else 1
                for j in range(nk):
                    kt = 2 * p + j
                    diag = (kt == sc)
                    cx.op("pe", [kTb, QBb], [Sb_], lambda e: e.matmul(S2[:, j, :], lhsT=kT[64 * j:64 * j + 64, p, :], rhs=(qlo if j == 0 else qhi), start=True, stop=(not diag)), sig=(j == nk - 1 and not diag))
                if 2 * p + nk - 1 == sc:
                    j = nk - 1
                    cx.op("pe", [self.identb, self.masksb], [Sb_], lambda e: e.matmul(S2[:, j, :], lhsT=self.ident[:, :], rhs=bcast(self.masks[:, 0, :], 1, 4), start=False, stop=True))
                cx.op("act", [Sb_], [Pb_], lambda e: e.activation(out=P_[:, 0:nk, :], in_=S2[:, 0:nk, :], func=AF.Exp, scale=0.125))
                pv5 = P_[:, 0:nk, :].rearrange("p k (h q) -> p k h q", h=4)
                m0 = (2 * p) % 4
                cx.op("dve", [Pb_, Msb_], [Pb_], lambda e: e.tensor_tensor(out=pv5, in0=pv5, in1=bcast(Ms_[:, m0:m0 + nk, :], 2, 4), op=ALU.mult))
                for j in range(nk):
                    kt = 2 * p + j
                    push((P_, Pb_, j, OS, OSb, vsl1, vsl1b, kt, kt == 0, kt == sc), lag=8)
            while queue:
                pv_emit(queue.pop(0))
            cx.op("dve", [OSb], [smb_], lambda e: e.tensor_scalar_max(out=sm_[:, 4:8], in0=OS[:, :, 64], scalar1=1e-30))
            cx.op("dve", [OCWb, smb_], [smb_], lambda e: e.tensor_scalar_max(out=sm_[:, 8:12], in0=OCW[:, :, 64], scalar1=1e-30))
            cx.op("dve", [smb_], [smb_], lambda e: e.reciprocal(out=sm_[:, 4:12], in_=sm_[:, 4:12]))
            for br in range(3):
                gv = self.gates[:, c, g * 12 + br:g * 12 + br + 10:3]
                cx.op("dve", [smb_, self.gatesb], [smb_], lambda e: e.tensor_tensor(out=sm_[:, 16 + 4 * br:20 + 4 * br], in0=sm_[:, 4 * br:4 * br + 4], in1=gv, op=ALU.mult))
            tm, tmb = tmpo[u]
            for br, (O_, Ob_) in enumerate(((oc_, ocb_), (OS, OSb), (OCW, OCWb))):
                cx.op("dve", [Ob_, smb_], [tmb], lambda e: e.tensor_tensor(out=tm[:, br, :, :], in0=O_[:, :, 0:64], in1=bcast(sm_[:, 16 + 4 * br:20 + 4 * br], 2, 64), op=ALU.mult))

        def tail_stage(idx):
            nonlocal mcnt
            ci, c, g = units[idx]
            u = idx % 2
            tm, tmb = tmpo[u]
            cx.op("pool", [tmb], [tmb], lambda e: e.tensor_tensor(out=tm[:, 0, :, :], in0=tm[:, 0, :, :], in1=tm[:, 1, :, :], op=ALU.add))
            o_, ob_ = ob[u]
            cx.op("pool", [tmb], [ob_], lambda e: e.tensor_tensor(out=o_[:, :].rearrange("p (h d) -> p h d", d=64), in0=tm[:, 0, :, :], in1=tm[:, 2, :, :], op=ALU.add))
            Mt, Mtb = Mp[mcnt % 2]
            mcnt += 1
            for j in range(2):
                cx.op("pe", [ob_, self.identfb], [Mtb], lambda e: e.transpose(out=Mt[:, j, :], in_=o_[:, j * 128:(j + 1) * 128], identity=self.identf[:]), sig=(j == 1))
            oT, oTb = obT[u]
            cx.op("act", [Mtb], [oTb], lambda e: e.copy(out=oT[:, :, :], in_=Mt[:, 0:2, :]))
            r0 = (8 + 4 * g) * 64
            otv = self.ot.rearrange("h d t -> (h d) t")
            for j in range(2):
                cx.dma([oTb], [self.otb], otv[r0 + j * 128:r0 + (j + 1) * 128, c * 128:(c + 1) * 128], oT[:, j, :])

        load_tables(0)
        cmp_stage(0)
        for idx in range(len(units)):
            main_stage(idx)
            if idx + 1 < len(units):
                cmp_stage(idx + 1)
            tail_stage(idx)
        cx.barrier()


K.phase_attnB = _attnB


def _bc_load(self, es, name, src_row):
    t, b = self.sb(es, name, [128, D], F32)
    src = bass.AP(src_row.tensor, src_row.offset, [[0, 128], [1, D]])
    self.cx.dma([], [b], t[:, :], src)
    return t, b


K.bc_load = _bc_load


def _post1(self, chunks=None):
    cx, nc = self.cx, self.nc
    chunks = list(range(NQ)) if chunks is None else chunks
    with contextlib.ExitStack() as es:
        wo, wob = self.sb(es, "p_wo", [128, 8, D], BF16)
        wst = [self.sb(es, "p_wost%d" % i, [128, 2, D], F32) for i in range(2)]
        wov = self.inp["w_o"].rearrange("(hp p) n -> p hp n", p=128)
        for q in range(4):
            st, stb = wst[q % 2]
            cx.dma([], [stb], st[:, :, :], wov[:, q * 2:(q + 1) * 2, :])
            cx.op("dve" if q % 2 == 0 else "pool", [stb], [wob], lambda e: e.tensor_copy(out=wo[:, q * 2:(q + 1) * 2, :], in_=st[:, :, :]))
        g1, g1b = self.bc_load(es, "p_ln1g", self.inp["ln1_g"])
        b1, b1b = self.bc_load(es, "p_ln1b", self.inp["ln1_b"])
        OTt = [self.sb(es, "p_OT%d" % i, [128, 8, 128], BF16) for i in range(2)]
        xt = [self.sb(es, "p_x%d" % i, [128, D], F32) for i in range(2)]
        tt = [self.sb(es, "p_t%d" % i, [128, D], F32) for i in range(2)]
        x1 = [self.sb(es, "p_x1%d" % i, [128, D], F32) for i in range(2)]
        lns = self.ln_stats(es, "lnq")
        Y = [self.ps(es, "p_Y%d" % i, [128, 512], F32) for i in range(4)]
        xk = self.inp["xk"].rearrange("(t p) d -> t p d", p=128)
        otv = self.ot.rearrange("(hp two) d t -> two d hp t", two=2)

        def load(ci):
            c = chunks[ci]
            i = ci % 2
            for two in range(2):
                cx.dma([self.otb], [OTt[i][1]], OTt[i][0][64 * two:64 * two + 64, :, :], otv[two][:, :, c * 128:(c + 1) * 128])
            cx.dma([], [xt[i][1]], xt[i][0][:, :], xk[Q0 + c, :, :])

        load(0)
        for ci, c in enumerate(chunks):
            i = ci % 2
            if ci + 1 < len(chunks):
                load(ci + 1)
            O_, Ob_ = OTt[i]
            x_, xb_ = xt[i]
            t_, tb_ = tt[i]
            for half in range(2):
                Y_, Yb_ = Y[(ci * 2 + half) % 4]
                for hp in range(8):
                    cx.op("pe", [Ob_, wob], [Yb_], lambda e: e.matmul(Y_[:, :], lhsT=O_[:, hp, :], rhs=wo[:, hp, half * 512:(half + 1) * 512], start=(hp == 0), stop=(hp == 7)), sig=(hp == 7))
                cx.op("dve", [Yb_, self.gbcb], [tb_], lambda e: e.tensor_tensor(out=t_[:, half * 512:(half + 1) * 512], in0=Y_[:, :], in1=self.gbc[:, 0, half * 512:(half + 1) * 512], op=ALU.mult))
            cx.op("dve", [xb_, tb_], [tb_], lambda e: e.scalar_tensor_tensor(out=t_[:, :], in0=x_[:, :], scalar=ALPHA, in1=t_[:, :], op0=ALU.mult, op1=ALU.add))
            mv, mvb = self.emit_ln_stats(lns[i], t_, tb_)
            x1_, x1b_ = x1[i]
            cx.op("act", [tb_, mvb], [x1b_], lambda e: e.activation(out=x1_[:, :], in_=t_[:, :], func=AF.Identity, bias=mv[:, 3:4], scale=mv[:, 2:3]))
            cx.op("dve", [x1b_, g1b], [x1b_], lambda e: e.tensor_tensor(out=x1_[:, :], in0=x1_[:, :], in1=g1[:, :], op=ALU.mult))
            cx.op("pool", [x1b_, b1b], [x1b_], lambda e: e.tensor_tensor(out=x1_[:, :], in0=x1_[:, :], in1=b1[:, :], op=ALU.add))
            cx.dma([x1b_], [self.x1b], self.x1d[c * 128:(c + 1) * 128, :], x1_[:, :])
        cx.barrier()


K.phase_post1 = _post1


def _post2(self, chunks=None):
    cx, nc = self.cx, self.nc
    chunks = list(range(NQ)) if chunks is None else chunks
    assert chunks[0] == 0
    body = chunks[1:]
    groups = [body[i:i + 2] for i in range(0, len(body), 2)]
    with contextlib.ExitStack() as es:
        wup, _ = self.sb(es, "f_wup", [128, 8, 2 * DFF], BF16)
        wupk = [cx.buf() for k in range(8)]
        wdn, wdnb = self.sb(es, "f_wdn", [128, NFC, D], BF16)
        if self.wcast_done:
            for kc in range(8):
                cx.dma([self.wupdb], [wupk[kc]], wup[:, kc, :], self.wupd[kc, :, :])
            for q in range(2):
                cx.dma([self.wdndb], [wdnb], wdn[:, q * 11:(q + 1) * 11, :], self.wdnd[:, q * 11:(q + 1) * 11, :])
        else:
            with contextlib.ExitStack() as es2:
                wst = [self.sb(es2, "f_wst%d" % i, [128, DFF], F32) for i in range(2)]
                wupv = self.inp["w_up"].rearrange("(k p) n -> k p n", p=128)
                n = 0
                for kc in range(8):
                    for hf in range(2):
                        st, stb = wst[n % 2]
                        cx.dma([], [stb], st[:, :], wupv[kc, :, hf * DFF:(hf + 1) * DFF])
                        cx.op("dve" if n % 2 == 0 else "pool", [stb], [wupk[kc]], lambda e: e.tensor_copy(out=wup[:, kc, hf * DFF:(hf + 1) * DFF], in_=st[:, :]))
                        n += 1
                wdnv = self.inp["w_down"].rearrange("(f p) n -> p f n", p=128)
                for q in range(11):
                    st, stb = wst[n % 2]
                    sv = st[:, 0:2 * D].rearrange("p (f n) -> p f n", n=D)
                    cx.dma([], [stb], sv, wdnv[:, 2 * q:2 * q + 2, :])
                    cx.op("dve" if n % 2 == 0 else "pool", [stb], [wdnb], lambda e: e.tensor_copy(out=wdn[:, 2 * q:2 * q + 2, :], in_=sv))
                    n += 1
                cx.barrier()
        g2, g2b = self.bc_load(es, "f_ln2g", self.inp["ln2_g"])
        b2, b2b = self.bc_load(es, "f_ln2b", self.inp["ln2_b"])
        cw, cwb = self.sb(es, "f_cw", [128, 3, NFC], F32)
        cb, cbb = self.sb(es, "f_cb", [128, NFC], F32)
        cx.dma([], [cwb], cw[:, :, :], self.inp["conv_w"].rearrange("k (f p) -> p k f", p=128), allow_slow_non_contiguous=True)
        cx.dma([], [cbb], cb[:, :], self.inp["conv_b"].rearrange("o (f p) -> p (o f)", p=128), allow_slow_non_contiguous=True)
        halo, halob = self.sb(es, "f_halo", [128, 1], F32)
        cx.dma([], [halob], halo[:, :], self.inp["halo"][:, :])
        Hs, _ = self.sb(es, "f_Hs", [128, NFC, 2], F32)
        Hk = [cx.buf() for f in range(NFC)]
        cx.op("pool", [], Hk, lambda e: e.memset(Hs[:], 0.0))
        NG = 2
        W = 128 * NG
        x1t = [self.sb(es, "f_x1%d" % i, [128, NG, D], F32) for i in range(2)]
        xn = [self.sb(es, "f_xn%d" % i, [128, D], BF16) for i in range(2)]
        uT = [self.sb(es, "f_uT%d" % i, [128, 8, W], BF16) for i in range(2)]
        uTk = [[cx.buf() for k in range(8)] for i in range(2)]
        Gf = [self.sb(es, "f_G%d" % i, [128, W + 2], F32) for i in range(3)]
        cv = [self.sb(es, "f_cv%d" % i, [128, 2, W], F32) for i in range(2)]
        gl = [self.sb(es, "f_gl%d" % i, [128, W], F32) for i in range(2)]
        hT, _ = self.sb(es, "f_hT", [128, NFC, W], BF16)
        hTk = [cx.buf() for f in range(NFC)]
        tt = [self.sb(es, "f_t%d" % i, [128, D], F32) for i in range(2)]
        lns = self.ln_stats(es, "lnf")
        lns2 = self.ln_stats(es, "lng")
        tp, _ = self.ps(es, "f_tp", [128, 8, 128], BF16)
        tpk1 = cx.buf()
        tpk1.psum = True
        tpk = [tpk1] * 8
        GV = [self.ps(es, "f_GV%d" % i, [128, 2, W], F32) for i in range(3)]
        Y = [self.ps(es, "f_Y%d" % i, [128, 512], F32) for i in range(4)]
        lcnt = [0]

        def stage_a(gi, cl):
            i = gi % 2
            x_, xb_ = x1t[i]
            for j, c in enumerate(cl):
                cx.dma([self.x1b], [xb_], x_[:, j, :], self.x1d[c * 128:(c + 1) * 128, :])
            for j, c in enumerate(cl):
                k = lcnt[0] % 2
                lcnt[0] += 1
                mv, mvb = self.emit_ln_stats(lns[k], x_[:, j, :], xb_)
                xn_, xnb_ = xn[k]
                cx.op("act", [xb_, mvb], [xnb_], lambda e: e.activation(out=xn_[:, :], in_=x_[:, j, :], func=AF.Identity, bias=mv[:, 3:4], scale=mv[:, 2:3]))
                self.emit_modT(xn_, xnb_, tp, tpk, uT[i][0][:, :, j * 128:(j + 1) * 128], uTk[i], 32, 24)

        def up_mm(gi, f, w, P_, Pb_, both):
            i = gi % 2
            uT_ = uT[i][0]
            for kc in range(8):
                cx.op("pe", [uTk[i][kc], wupk[kc]], [Pb_], lambda e: e.matmul(P_[:, 0, 0:w], lhsT=wup[:, kc, f * 128:(f + 1) * 128], rhs=uT_[:, kc, 0:w], start=(kc == 0), stop=(kc == 7)), sig=(kc == 7 and not both))
            if both:
                for kc in range(8):
                    cx.op("pe", [uTk[i][kc], wupk[kc]], [Pb_], lambda e: e.matmul(P_[:, 1, 0:w], lhsT=wup[:, kc, DFF + f * 128:DFF + (f + 1) * 128], rhs=uT_[:, kc, 0:w], start=(kc == 0), stop=(kc == 7)), sig=(kc == 7))

        stage_a(0, [0])
        if groups:
            stage_a(1, groups[0])
        cnt = 0
        for f in range(NFC):
            P_, Pb_ = GV[cnt % 3]
            G_, Gb_ = Gf[cnt % 3]
            cnt += 1
            up_mm(0, f, 128, P_, Pb_, False)
            cx.op("act", [Pb_], [Gb_], lambda e: e.copy(out=G_[:, 2:130], in_=P_[:, 0, 0:128]))
            cx.op("dve", [Gb_, halob], [Hk[f]], lambda e: e.tensor_scalar(out=Hs[:, f, :], in0=G_[:, 128:130], scalar1=halo[:, 0:1], scalar2=None, op0=ALU.mult))
        ecnt = 0
        prev = None

        def down_mm(pg, f):
            (pcl, pi) = pg
            for j, c in enumerate(pcl):
                for half in range(2):
                    Y_, Yb_ = Y[j * 2 + half]
                    cx.op("pe", [hTk[f], wdnb], [Yb_], lambda e: e.matmul(Y_[:, :], lhsT=hT[:, f, j * 128:(j + 1) * 128], rhs=wdn[:, f, half * 512:(half + 1) * 512], start=(f == 0), stop=(f == NFC - 1)), sig=(f == NFC - 1))

        def down_tail(pg, gidx):
            (pcl, pi) = pg
            x_, xb_ = x1t[pi]
            for j, c in enumerate(pcl):
                t_, tb_ = tt[j % 2]
                for half in range(2):
                    Y_, Yb_ = Y[j * 2 + half]
                    cx.op("dve", [Yb_, self.gbcb], [tb_], lambda e: e.tensor_tensor(out=t_[:, half * 512:(half + 1) * 512], in0=Y_[:, :], in1=self.gbc[:, 1, half * 512:(half + 1) * 512], op=ALU.mult))
                cx.op("dve", [xb_, tb_], [tb_], lambda e: e.scalar_tensor_tensor(out=t_[:, :], in0=x_[:, j, :], scalar=ALPHA, in1=t_[:, :], op0=ALU.mult, op1=ALU.add))
                mv2, mv2b = self.emit_ln_stats(lns2[j % 2], t_, tb_)
                cx.op("act", [tb_, mv2b], [tb_], lambda e: e.activation(out=t_[:, :], in_=t_[:, :], func=AF.Identity, bias=mv2[:, 3:4], scale=mv2[:, 2:3]))
                cx.op("pool", [tb_, g2b], [tb_], lambda e: e.tensor_tensor(out=t_[:, :], in0=t_[:, :], in1=g2[:, :], op=ALU.mult))
                cx.op("pool", [tb_, b2b], [tb_], lambda e: e.tensor_tensor(out=t_[:, :], in0=t_[:, :], in1=b2[:, :], op=ALU.add))
                cx.dma([tb_], [self.outbuf], self.out[(c - 1) * 128:c * 128, :], t_[:, :])

        for gi0, cl in enumerate(groups):
            gi = gi0 + 1
            i = gi % 2
            w = 128 * len(cl)
            pend = None

            def elem_a(pd):
                nonlocal ecnt
                (f, P_, Pb_, G_, Gb_) = pd
                cv_, cvb_ = cv[ecnt % 2]
                gl_, glb_ = gl[ecnt % 2]
                ecnt += 1
                cx.op("act", [Pb_], [Gb_], lambda e: e.copy(out=G_[:, 2:2 + w], in_=P_[:, 0, 0:w]))
                cx.op("act", [Hk[f]], [Gb_], lambda e: e.copy(out=G_[:, 0:2], in_=Hs[:, f, :]))
                cx.op("dve", [Gb_, cwb, cbb], [cvb_], lambda e: e.tensor_scalar(out=cv_[:, 0, 0:w], in0=G_[:, 2:2 + w], scalar1=cw[:, 2, f:f + 1], scalar2=cb[:, f:f + 1], op0=ALU.mult, op1=ALU.add))
                cx.op("dve", [Gb_, cwb, cvb_], [cvb_], lambda e: e.scalar_tensor_tensor(out=cv_[:, 1, 0:w], in0=G_[:, 1:1 + w], scalar=cw[:, 1, f:f + 1], in1=cv_[:, 0, 0:w], op0=ALU.mult, op1=ALU.add))
                cx.op("dve", [Gb_, cwb, cvb_], [cvb_], lambda e: e.scalar_tensor_tensor(out=cv_[:, 0, 0:w], in0=G_[:, 0:w], scalar=cw[:, 0, f:f + 1], in1=cv_[:, 1, 0:w], op0=ALU.mult, op1=ALU.add))
                cx.op("act", [Gb_], [Hk[f]], lambda e: e.copy(out=Hs[:, f, :], in_=G_[:, w:w + 2]))
                cx.op("act", [cvb_], [glb_], lambda e: e.activation(out=gl_[:, 0:w], in_=cv_[:, 0, 0:w], func=AF.Gelu_apprx_tanh))
                return (f, P_, Pb_, gl_, glb_)

            def elem_b(pd):
                (f, P_, Pb_, gl_, glb_) = pd
                cx.op("dve", [glb_, Pb_], [hTk[f]], lambda e: e.tensor_tensor(out=hT[:, f, 0:w], in0=gl_[:, 0:w], in1=P_[:, 1, 0:w], op=ALU.mult))

            pend_b = None
            for f in range(NFC):
                P_, Pb_ = GV[cnt % 3]
                G_, Gb_ = Gf[cnt % 3]
                cnt += 1
                up_mm(gi, f, w, P_, Pb_, True)
                if prev is not None:
                    down_mm(prev, f)
                nb = elem_a(pend) if pend is not None else None
                if pend_b is not None:
                    elem_b(pend_b)
                pend_b = nb
                pend = (f, P_, Pb_, G_, Gb_)
            if prev is not None:
                down_tail(prev, gi0 - 1)
            nb = elem_a(pend)
            if pend_b is not None:
                elem_b(pend_b)
            elem_b(nb)
            if gi0 + 1 < len(groups):
                stage_a(gi + 1, groups[gi0 + 1])
            prev = (cl, i)
        if prev is not None:
            for f in range(NFC):
                down_mm(prev, f)
            down_tail(prev, len(groups) - 1)
        cx.barrier()


K.phase_post2 = _post2
```

```python
import contextlib
import os
SKIP = set(os.environ.get('KSKIP', '').split(','))
STAGE = int(os.environ.get('KSTAGE', '99'))
import numpy as np
import ml_dtypes
import concourse.bass as bass
import concourse.mybir as mybir
from concourse.bass_utils import run_bass_kernel_spmd

F32 = mybir.dt.float32
BF16 = mybir.dt.bfloat16
AF = mybir.ActivationFunctionType
ALU = mybir.AluOpType
AX = mybir.AxisListType

D = 1024
S = 8192
NT = 64
Q0 = 31
NQ = 33
QTOK = NQ * 128
HD = 64
INW = 2840
DFF = 2816
NFC = 22
ALPHA = 2.0 ** 0.25
EPS = 1e-5
MASKV = -30000.0
VW = 66
FB_QA, FB_KA, FB_QB, FB_KC, FB_VC, FB_KSL, FB_KW = 0, 4, 8, 12, 13, 14, 15
NFB = 16


def bcast(ap, axis, n):
    dims = [list(d) for d in ap.ap]
    dims.insert(axis, [0, n])
    return bass.AP(ap.tensor, ap.offset, dims)


class Buf:
    __slots__ = ("name", "w", "r", "psum")

    def __init__(self, name, psum=False):
        self.name = name
        self.w = {}
        self.r = {}
        self.psum = psum


class Ctx:
    def __init__(self, nc, n_dma_sems=40):
        self.nc = nc
        self.es = contextlib.ExitStack()
        self.eng = {"pe": nc.tensor, "dve": nc.vector, "act": nc.scalar, "pool": nc.gpsimd, "sp": nc.sync}
        self.sem = {}
        self.cnt = {}
        self.semobj = {}
        for e in self.eng:
            s = self.es.enter_context(nc.semaphore("sem_" + e))
            self.sem[e] = s
            self.cnt[e] = 0
        self.dma_sems = [self.es.enter_context(nc.semaphore("dsem%d" % i)) for i in range(n_dma_sems)]
        self.dma_cnt = [0] * n_dma_sems
        self.dma_rr = 0
        self.waited = {}
        self.nbuf = 0
        self.ninst = 0

    def buf(self, name=None):
        self.nbuf += 1
        return Buf(name or ("b%d" % self.nbuf))

    def _key(self, s):
        return id(s)

    def _wait(self, e, deps):
        for k, (s, v) in deps.items():
            if e == "pe" and s is self.sem["pe"]:
                continue
            if self.waited.get((e, k), 0) < v:
                self.eng[e].wait_ge(s, v)
                self.waited[(e, k)] = v

    def _deps(self, reads, writes):
        deps = {}

        def add(dd):
            for k, (s, v) in dd.items():
                if k not in deps or deps[k][1] < v:
                    deps[k] = (s, v)
        for b in reads:
            add(b.w)
        for b in writes:
            add(b.w)
            add(b.r)
        return deps

    def _commit(self, ev, reads, writes):
        k = self._key(ev[0])
        for b in writes:
            b.w = {k: ev}
            b.r = {}
        for b in reads:
            if k not in b.r or b.r[k][1] < ev[1]:
                b.r[k] = ev

    def op(self, e, reads, writes, fn, sig=True):
        if e != "pe":
            px = [b for b in reads if b.psum]
            if px:
                reads = [b for b in reads if not b.psum]
                writes = list(writes) + px
        deps = self._deps(reads, writes)
        self._wait(e, deps)
        inst = fn(self.eng[e])
        self.ninst += 1
        if sig:
            self.cnt[e] += 1
            inst.then_inc(self.sem[e], 1)
            ev = (self.sem[e], self.cnt[e])
        else:
            ev = (self.sem[e], self.cnt[e] + 1)
        self._commit(ev, reads, writes)
        return inst

    def dma(self, reads, writes, out, in_, q="sp", **kw):
        deps = self._deps(reads, writes)
        i = self.dma_rr
        self.dma_rr = (self.dma_rr + 1) % len(self.dma_sems)
        s = self.dma_sems[i]
        if self.dma_cnt[i] > 0:
            deps[self._key(s)] = (s, self.dma_cnt[i])
        self._wait(q, deps)
        inst = self.eng[q].dma_start(out=out, in_=in_, **kw)
        self.ninst += 1
        self.dma_cnt[i] += 16
        inst.then_inc(s, 16)
        ev = (s, self.dma_cnt[i])
        self._commit(ev, reads, writes)
        return ev

    def barrier(self):
        for e in self.eng:
            deps = {}
            for e2 in self.eng:
                if e2 != e and self.cnt[e2] > 0:
                    deps[self._key(self.sem[e2])] = (self.sem[e2], self.cnt[e2])
            for i, sm in enumerate(self.dma_sems):
                if self.dma_cnt[i] > 0:
                    deps[self._key(sm)] = (sm, self.dma_cnt[i])
            self._wait(e, deps)

    def finish(self, bufs):
        deps = self._deps(bufs, [])
        self._wait("sp", deps)

    def close(self):
        self.es.close()


def host_tables(half):
    off = 0 if half == 1 else 32
    t = {}
    slots = np.arange(NT * 128)
    gpos = slots - off * 128
    real = gpos >= 0
    inv = 1.0 / (500000.0 ** (np.arange(0, 16, 2, dtype=np.float32) / 16.0))
    ang = np.where(real, gpos, 0).astype(np.float32)[:, None] * inv[None, :]
    cs = np.concatenate([np.cos(ang), np.cos(ang), -np.sin(ang), np.sin(ang)], -1).astype(np.float32)
    t["ropetab"] = np.ascontiguousarray(cs.reshape(NT, 128, 32).transpose(1, 0, 2))
    kb = np.where(real, 0.0, MASKV).astype(np.float32)
    t["keybias"] = np.ascontiguousarray(kb.reshape(NT, 128).T)
    kl = np.arange(128)[:, None]
    ql = np.arange(128)[None, :]
    m = np.zeros((3, 128, 128), np.float32)
    m[0] = np.where(kl <= ql, 0.0, MASKV)
    m[1] = np.where(kl >= ql, 0.0, MASKV)
    m[2] = np.where(kl > ql, 0.0, MASKV)
    t["masks"] = np.ascontiguousarray(m.transpose(1, 0, 2)).astype(ml_dtypes.bfloat16)
    t["ident"] = np.eye(128, dtype=np.float32).astype(ml_dtypes.bfloat16)
    E = np.zeros((128, NT * 128), np.float32)
    E[(np.arange(NT * 128) // 64), np.arange(NT * 128)] = 1.0
    t["eall"] = E.astype(ml_dtypes.bfloat16)
    cst = np.arange(512) * 16
    sst = np.arange(128) * 64
    ov = np.clip(np.minimum(cst[:, None] + 32, sst[None, :] + 64) - np.maximum(cst[:, None], sst[None, :]), 0, None) / 32.0
    ov[511] = 0.0
    t["overlap"] = np.ascontiguousarray(ov.reshape(4, 128, 128).transpose(1, 0, 2)).astype(ml_dtypes.bfloat16)
    selb = np.zeros((NQ, 128, 128), np.float32)
    cmpm = np.zeros((NQ, 512, 128), np.float32)
    for c in range(NQ):
        sl = Q0 + c
        tq = (sl - off) * 128 + np.arange(128)
        blk = np.arange(128) - off * 2
        cur = tq // 64
        sb = np.zeros((128, 128), np.float32)
        valid = (blk[None, :] >= 0) & (blk[None, :] * 64 <= tq[:, None])
        sb[:] = np.where(valid, 0.0, -1e9 - 1e4 * np.arange(128)[None, :])
        for j, bid in enumerate([cur, cur - 1, np.zeros_like(cur)]):
            hit = (blk[None, :] == bid[:, None]) & (blk[None, :] >= 0)
            sb = np.where(hit, (1.0 + j) * 1e9, sb)
        selb[c] = sb
        ci = np.arange(512)
        cstart_g = ci * 16 - off * 128
        cvalid = (cstart_g[:, None] >= 0) & (cstart_g[:, None] + 31 <= tq[None, :]) & (ci[:, None] < 511)
        cmpm[c] = np.where(cvalid, 1.0, 0.0)
    t["selb"] = np.ascontiguousarray(selb.transpose(1, 0, 2))
    t["cmpm"] = np.ascontiguousarray(cmpm.reshape(NQ, 4, 128, 128).transpose(2, 0, 1, 3)).astype(ml_dtypes.bfloat16)
    t["halo"] = np.full((128, 1), 1.0 if half == 1 else 0.0, np.float32)
    t["realb"] = np.ascontiguousarray(np.broadcast_to(((np.arange(128) - off * 2) >= 0).astype(np.float32)[None, :], (128, 128)))
    return t


TABLE_SPECS = [
    ("ropetab", [128, NT, 32], F32), ("keybias", [128, NT], F32), ("masks", [128, 3, 128], BF16),
    ("ident", [128, 128], BF16), ("eall", [128, NT * 128], BF16), ("overlap", [128, 4, 128], BF16),
    ("selb", [128, NQ, 128], F32), ("cmpm", [128, NQ, 4, 128], BF16), ("halo", [128, 1], F32), ("realb", [128, 128], F32),
]

WEIGHT_SPECS = [
    ("c", [1, D]), ("w_ada", [D, 6 * D]), ("b_ada", [1, 6 * D]), ("w_in", [D, INW]), ("pe_cmp", [32, 64]),
    ("w_ck1", [2048, 256]), ("w_ck2", [256, 64]), ("w_cv1", [2048, 256]), ("w_cv2", [256, 64]),
    ("w_o", [D, D]), ("ln1_g", [1, D]), ("ln1_b", [1, D]), ("w_up", [D, 2 * DFF]), ("conv_w", [3, DFF]),
    ("conv_b", [1, DFF]), ("w_down", [DFF, D]), ("ln2_g", [1, D]), ("ln2_b", [1, D]),
]


class K:
    def __init__(self, debug=(), scratch_in=()):
        self.debug = set(debug)
        self.scratch_in = set(scratch_in)
        nc = bass.Bass("TRN2", target_bir_lowering=False)
        self.nc = nc
        self.cx = Ctx(nc)
        self.inp = {}
        self.inp["xk"] = nc.dram_tensor("xk", [S, D], F32, kind="ExternalInput").ap()
        for name, shape in WEIGHT_SPECS:
            self.inp[name] = nc.dram_tensor(name, shape, F32, kind="ExternalInput").ap()
        for name, shape, dt in TABLE_SPECS:
            self.inp[name] = nc.dram_tensor(name, shape, dt, kind="ExternalInput").ap()
        self.out = nc.dram_tensor("out", [32 * 128, D], F32, kind="ExternalOutput").ap()
        self.outbuf = self.cx.buf("out")
        def kd(n):
            if n in self.scratch_in:
                return "ExternalInput"
            return "ExternalOutput" if n in self.debug else "Internal"
        self.ft = nc.dram_tensor("ft", [NFB, 128, S], BF16, kind=kd("ft")).ap()
        self.ftb = self.cx.buf("ft")
        self.va = nc.dram_tensor("va", [S, 8, VW], BF16, kind=kd("va")).ap()
        self.vab = self.cx.buf("va")
        self.vsl = nc.dram_tensor("vsl", [S, 2, VW], BF16, kind=kd("vsl")).ap()
        self.vslb = self.cx.buf("vsl")
        self.vwn = nc.dram_tensor("vwn", [S, 2, VW], BF16, kind=kd("vwn")).ap()
        self.vwnb = self.cx.buf("vwn")
        self.ot = nc.dram_tensor("ot", [16, 64, QTOK], BF16, kind=kd("ot")).ap()
        self.otb = self.cx.buf("ot")
        self.x1d = nc.dram_tensor("x1d", [QTOK, D], F32, kind=kd("x1d")).ap()
        self.wupd = nc.dram_tensor("wupd", [8, 128, 2 * DFF], BF16, kind="Internal").ap()
        self.wupdb = self.cx.buf("wupd")
        self.wdnd = nc.dram_tensor("wdnd", [128, NFC, D], BF16, kind="Internal").ap()
        self.wdndb = self.cx.buf("wdnd")
        self.wcast_done = False
        self.x1b = self.cx.buf("x1d")
        self.dbg = {}

    def dbg_out(self, name, shape, dt):
        ap = self.nc.dram_tensor("dbg_" + name, shape, dt, kind="ExternalOutput").ap()
        self.dbg[name] = (ap, self.cx.buf("dbg_" + name))
        return self.dbg[name]

    def sb(self, es, name, shape, dt):
        t = es.enter_context(self.nc.sbuf_tensor("s_" + name, shape, dt))
        return t, self.cx.buf(name)

    def ps(self, es, name, shape, dt):
        t = es.enter_context(self.nc.psum_tensor("p_" + name, shape, dt))
        b = self.cx.buf(name)
        b.psum = True
        return t, b

    def consts(self, es):
        cx = self.cx
        self.ident, self.identb = self.sb(es, "ident", [128, 128], BF16)
        cx.dma([], [self.identb], self.ident[:], self.inp["ident"][:, :])
        self.masks, self.masksb = self.sb(es, "masks", [128, 3, 128], BF16)
        cx.dma([], [self.masksb], self.masks[:], self.inp["masks"][:, :, :])
        self.keybias, self.keybiasb = self.sb(es, "keybias", [128, NT], F32)
        cx.dma([], [self.keybiasb], self.keybias[:], self.inp["keybias"][:, :])
        self.modT, self.modTb = self.sb(es, "modT", [128, 48], F32)
        self.gbc, self.gbcb = self.sb(es, "gbc", [128, 2, D], F32)
        self.ones, self.onesb = self.sb(es, "ones", [128, 128], F32)
        cx.op("pool", [], [self.onesb], lambda e: e.memset(self.ones[:], 1.0))
        self.identf, self.identfb = self.sb(es, "identf", [128, 128], F32)
        cx.op("dve", [self.identb], [self.identfb], lambda e: e.tensor_copy(out=self.identf[:], in_=self.ident[:]))
        self.onesbf, self.onesbfb = self.sb(es, "onesbf", [128, 128], BF16)
        cx.op("pool", [], [self.onesbfb], lambda e: e.memset(self.onesbf[:], 1.0))

    def phase_mod(self):
        cx, nc = self.cx, self.nc
        with contextlib.ExitStack() as es:
            cT, cTb = self.sb(es, "cT", [128, 8], F32)
            cx.dma([], [cTb], cT[:], self.inp["c"].rearrange("o (k p) -> p (o k)", p=128), allow_slow_non_contiguous=True)
            bT, bTb = self.sb(es, "bT", [128, 48], F32)
            cx.dma([], [bTb], bT[:], self.inp["b_ada"].rearrange("o (j p) -> p (o j)", p=128), allow_slow_non_contiguous=True)
            brow, browb = self.sb(es, "brow", [1, 2, D], F32)
            cx.dma([], [browb], brow[:, 0, :], self.inp["b_ada"][:, 2 * D:3 * D])
            cx.dma([], [browb], brow[:, 1, :], self.inp["b_ada"][:, 5 * D:6 * D])
            sc, scb = self.sb(es, "silc", [128, 8], BF16)
            cx.op("act", [cTb], [scb], lambda e: e.activation(out=sc[:], in_=cT[:], func=AF.Silu))
            wst = [self.sb(es, "wada_st%d" % i, [128, 6 * D], F32) for i in range(2)]
            wbf = [self.sb(es, "wada_bf%d" % i, [128, 6 * D], BF16) for i in range(2)]
            pp, ppb = self.ps(es, "modpart", [128, 48, 8], F32)
            prow = [self.ps(es, "modrow%d" % i, [1, 512], F32) for i in range(4)]
            wada = self.inp["w_ada"].rearrange("(k p) n -> k p n", p=128)
            for kc in range(8):
                st, stb = wst[kc % 2]
                wb, wbb = wbf[kc % 2]
                cx.dma([], [stb], st[:, 0:3 * D], wada[kc, :, 0:3 * D])
                cx.dma([], [stb], st[:, 3 * D:6 * D], wada[kc, :, 3 * D:6 * D])
                cx.op("dve", [stb], [wbb], lambda e: e.tensor_copy(out=wb[:, 0:5 * 512], in_=st[:, 0:5 * 512]))
                cx.op("act", [stb], [wbb], lambda e: e.copy(out=wb[:, 5 * 512:10 * 512], in_=st[:, 5 * 512:10 * 512]))
                cx.op("pool", [stb], [wbb], lambda e: e.tensor_copy(out=wb[:, 10 * 512:6 * D], in_=st[:, 10 * 512:6 * D]))
                for j in range(48):
                    cx.op("pe", [wbb, scb], [ppb], lambda e: e.matmul(pp[:, j, kc:kc + 1], lhsT=wb[:, j * 128:(j + 1) * 128],
                                                                       rhs=sc[:, kc:kc + 1], start=True, stop=True), sig=(j == 47))
                for r in range(4):
                    col = (2 * D if r < 2 else 5 * D) + (r % 2) * 512
                    pr, prb = prow[r]
                    cx.op("pe", [wbb, scb], [prb], lambda e: e.matmul(pr[:, :], lhsT=sc[:, kc:kc + 1], rhs=wb[:, col:col + 512],
                                                                       start=(kc == 0), stop=(kc == 7)))
            tmp, tmpb = self.sb(es, "modtmp", [128, 48], F32)
            cx.op("dve", [ppb], [tmpb], lambda e: e.tensor_reduce(out=tmp[:], in_=pp[:], axis=AX.X, op=ALU.add))
            cx.op("dve", [tmpb, bTb], [self.modTb], lambda e: e.tensor_tensor(out=self.modT[:], in0=tmp[:], in1=bT[:], op=ALU.add))
            cx.op("dve", [self.modTb], [self.modTb], lambda e: e.tensor_scalar_add(out=self.modT[:, 8:16], in0=self.modT[:, 8:16], scalar1=1.0))
            cx.op("dve", [self.modTb], [self.modTb], lambda e: e.tensor_scalar_add(out=self.modT[:, 32:40], in0=self.modT[:, 32:40], scalar1=1.0))
            grow, growb = self.sb(es, "grow", [1, 2, D], F32)
            for r in range(4):
                pr, prb = prow[r]
                cx.op("dve", [prb, browb], [growb], lambda e: e.tensor_tensor(out=grow[:, r // 2, (r % 2) * 512:(r % 2) * 512 + 512],
                                                                               in0=pr[:, :], in1=brow[:, r // 2, (r % 2) * 512:(r % 2) * 512 + 512], op=ALU.add))
            pb, pbb = self.ps(es, "gbps", [128, 512], F32)
            for r in range(4):
                c0 = (r % 2) * 512
                cx.op("pe", [growb, self.onesb], [pbb], lambda e: e.matmul(pb[:, :], lhsT=self.ones[0:1, :], rhs=grow[:, r // 2, c0:c0 + 512], start=True, stop=True))
                cx.op("act", [pbb], [self.gbcb], lambda e: e.copy(out=self.gbc[:, r // 2, c0:c0 + 512], in_=pb[:, :]))
            cx.barrier()
            if "mod" in self.debug:
                ap, b = self.dbg_out("modT", [128, 48], F32)
                cx.dma([self.modTb], [b], ap[:, :], self.modT[:])
                ap, b = self.dbg_out("gbc", [128, 2, D], F32)
                cx.dma([self.gbcb], [b], ap[:, :, :], self.gbc[:])

    def ln_stats(self, es, tag, nbufs=2):
        sets = []
        for i in range(nbufs):
            st, stb = self.sb(es, "%s_st%d" % (tag, i), [128, 12], F32)
            mv, mvb = self.sb(es, "%s_mv%d" % (tag, i), [128, 4], F32)
            sets.append((st, stb, mv, mvb))
        return sets

    def emit_ln_stats(self, lnset, x, xb):
        cx = self.cx
        st, stb, mv, mvb = lnset
        cx.op("dve", [xb], [stb], lambda e: e.bn_stats(out=st[:, 0:6], in_=x[:, 0:512]))
        cx.op("dve", [xb, stb], [stb], lambda e: e.bn_stats(out=st[:, 6:12], in_=x[:, 512:1024]))
        cx.op("dve", [stb], [mvb], lambda e: e.bn_aggr(out=mv[:, 0:2], in_=st[:, :]))
        cx.op("dve", [mvb], [mvb], lambda e: e.tensor_scalar_add(out=mv[:, 2:3], in0=mv[:, 1:2], scalar1=EPS))
        cx.op("act", [mvb], [mvb], lambda e: e.activation(out=mv[:, 2:3], in_=mv[:, 2:3], func=AF.Sqrt))
        cx.op("dve", [mvb], [mvb], lambda e: e.reciprocal(out=mv[:, 2:3], in_=mv[:, 2:3]))
        cx.op("dve", [mvb], [mvb], lambda e: e.tensor_scalar(out=mv[:, 3:4], in0=mv[:, 0:1], scalar1=mv[:, 2:3], scalar2=-1.0, op0=ALU.mult, op1=ALU.mult))
        return mv, mvb

    def emit_modT(self, xn, xnb, tp, tpb, uT, uTb, sc_col, sh_col, n=128):
        cx = self.cx
        for kc in range(8):
            cx.op("pe", [xnb, self.identb], [tpb[kc]], lambda e: e.transpose(out=tp[:, kc, :], in_=xn[:, kc * 128:(kc + 1) * 128], identity=self.ident[:]), sig=(kc == 7))
        for kc in range(8):
            use_act = (kc % 2 == 0)
            if "allact" in SKIP:
                use_act = True
            if "alldve" in SKIP:
                use_act = False
            if "plaincopy" in SKIP:
                cx.op("dve", [tpb[kc]], [uTb[kc]], lambda e: e.tensor_copy(out=uT[:, kc, :], in_=tp[:, kc, :]))
            elif use_act:
                cx.op("act", [tpb[kc], self.modTb], [uTb[kc]], lambda e: e.activation(out=uT[:, kc, :], in_=tp[:, kc, :], func=AF.Identity,
                                                                                      bias=self.modT[:, sh_col + kc:sh_col + kc + 1], scale=self.modT[:, sc_col + kc:sc_col + kc + 1]))
            else:
                cx.op("dve", [tpb[kc], self.modTb], [uTb[kc]], lambda e: e.tensor_scalar(out=uT[:, kc, :], in0=tp[:, kc, :], scalar1=self.modT[:, sc_col + kc:sc_col + kc + 1],
                                                                                         scalar2=self.modT[:, sh_col + kc:sh_col + kc + 1], op0=ALU.mult, op1=ALU.add))

    def phase_proj(self, es_persist, tiles=None):
        cx, nc = self.cx, self.nc
        tiles = list(range(NT)) if tiles is None else tiles
        self.gates, self.gatesb = self.sb(es_persist, "gates", [128, NQ, 24], F32)
        with contextlib.ExitStack() as es:
            wbf, wbfb = self.sb(es, "win_bf", [128, 8, INW], BF16)
            wbfk = [cx.buf("win_bf%d" % k) for k in range(8)]
            wst = [self.sb(es, "win_st%d" % i, [128, INW], F32) for i in range(2)]
            win = self.inp["w_in"].rearrange("(k p) n -> k p n", p=128)
            for kc in range(8):
                st, stb = wst[kc % 2]
                cx.dma([], [stb], st[:, :], win[kc, :, :])
                cx.op("dve", [stb], [wbfk[kc]], lambda e: e.tensor_copy(out=wbf[:, kc, 0:1420], in_=st[:, 0:1420]))
                cx.op("pool", [stb], [wbfk[kc]], lambda e: e.tensor_copy(out=wbf[:, kc, 1420:INW], in_=st[:, 1420:INW]))
            rope, ropeb = self.sb(es, "ropetab", [128, NT, 32], F32)
            cx.dma([], [ropeb], rope[:], self.inp["ropetab"][:, :, :])
            xt = [self.sb(es, "xt%d" % i, [128, D], F32) for i in range(2)]
            xn = [self.sb(es, "xn%d" % i, [128, D], BF16) for i in range(2)]
            uT = [self.sb(es, "uT%d" % i, [128, 8, 128], BF16) for i in range(2)]
            uTk = [[cx.buf() for k in range(8)] for i in range(2)]
            lns = self.ln_stats(es, "lnp")
            tp, _ = self.ps(es, "tp", [128, 8, 128], BF16)
            tpk1 = cx.buf()
            tpk1.psum = True
            tpk = [tpk1] * 8
            pg = [self.ps(es, "pg%d" % i, [128, 512], F32) for i in range(3)]
            tpo = [self.ps(es, "tpo%d" % i, [128, 8, 128], BF16) for i in range(2)]
            tokq = [self.sb(es, "tokq%d" % i, [128, NFB * 128], BF16) for i in range(2)]
            rsc = [self.sb(es, "rsc%d" % i, [128, 2, 8, 16], F32) for i in range(2)]
            ftst = [self.sb(es, "ftst%d" % i, [128, NFB, 512], BF16) for i in range(2)]
            vst = [self.sb(es, "vst%d" % i, [128, 8, VW], BF16) for i in range(2)]
            vsst = [self.sb(es, "vsst%d" % i, [128, 2, VW], BF16) for i in range(2)]
            vwst = [self.sb(es, "vwst%d" % i, [128, 2, VW], BF16) for i in range(2)]
            for i in range(2):
                for (t_, b_) in (vst[i], vsst[i], vwst[i]):
                    cx.op("pool", [], [b_], lambda e: e.memset(t_[:], 1.0))
                cx.op("pool", [], [ftst[i][1]], lambda e: e.memset(ftst[i][0][:], 0.0))
            xk = self.inp["xk"].rearrange("(t p) d -> t p d", p=128)
            ngroup = 0
            cst = [self.sb(es, "wc_st%d" % i, [128, DFF], F32) for i in range(2)]
            cbf = [self.sb(es, "wc_bf%d" % i, [128, DFF], BF16) for i in range(2)]
            wupv_ = self.inp["w_up"].rearrange("(k p) n -> k p n", p=128)
            wdnv_ = self.inp["w_down"].rearrange("(f p) n -> p f n", p=128)
            jobs = [("up", kc, hf) for kc in range(8) for hf in range(2)] + [("dn", q, 0) for q in range(11)]
            jobn = [0]

            pend_store = [None]

            def flush_store():
                if pend_store[0] is not None:
                    (bfb, dbuf, dst, src) = pend_store[0]
                    cx.dma([bfb], [dbuf], dst, src)
                    pend_store[0] = None

            def cast_job():
                flush_store()
                if jobn[0] >= len(jobs) or len(tiles) < 40:
                    return
                kind, a_, b_ = jobs[jobn[0]]
                st, stb = cst[jobn[0] % 2]
                bf_, bfb = cbf[jobn[0] % 2]
                jobn[0] += 1
                if kind == "up":
                    cx.dma([], [stb], st[:, :], wupv_[a_, :, b_ * DFF:(b_ + 1) * DFF])
                    cx.op("pool", [stb], [bfb], lambda e: e.tensor_copy(out=bf_[:, :], in_=st[:, :]))
                    pend_store[0] = (bfb, self.wupdb, self.wupd[a_, :, b_ * DFF:(b_ + 1) * DFF], bf_[:, :])
                else:
                    sv = st[:, 0:2 * D].rearrange("p (f n) -> p f n", n=D)
                    bv = bf_[:, 0:2 * D].rearrange("p (f n) -> p f n", n=D)
                    cx.dma([], [stb], sv, wdnv_[:, 2 * a_:2 * a_ + 2, :])
                    cx.op("pool", [stb], [bfb], lambda e: e.tensor_copy(out=bv, in_=sv))
                    pend_store[0] = (bfb, self.wdndb, self.wdnd[:, 2 * a_:2 * a_ + 2, :], bv)
                if jobn[0] == len(jobs):
                    self.wcast_done = True

            def load_x(i, t):
                cx.dma([], [xt[i][1]], xt[i][0][:, :], xk[t, :, :])

            load_x(0, tiles[0])
            pgi = 0
            def stage_a(ti):
                i = ti % 2
                x_, xb_ = xt[i]
                if ti + 1 < len(tiles):
                    load_x(1 - i, tiles[ti + 1])
                mv, mvb = self.emit_ln_stats(lns[i], x_, xb_)
                xn_, xnb_ = xn[i]
                cx.op("act", [xb_, mvb], [xnb_], lambda e: e.activation(out=xn_[:, :], in_=x_[:, :], func=AF.Identity, bias=mv[:, 3:4], scale=mv[:, 2:3]))
                self.emit_modT(xn_, xnb_, tp, tpk, uT[i][0], uTk[i], 8, 0)

            stage_a(0)
            for ti, t in enumerate(tiles):
                i = ti % 2
                isq = t >= Q0
                if ti + 1 < len(tiles):
                    stage_a(ti + 1)
                if ti >= 2 and ti % 2 == 0:
                    cast_job()
                uT_, _ = uT[i]
                tq, tqb = tokq[i]
                rs, rsb = rsc[i]
                g4 = ti // 4
                fst, fstb = ftst[g4 % 2]
                sub = ti % 4
                cs32 = rope[:, t, :]

                def proj(c0, c1):
                    nonlocal pgi
                    p_, pb_ = pg[pgi % 3]
                    pgi += 1
                    for kc in range(8):
                        cx.op("pe", [uTk[i][kc], wbfk[kc]], [pb_], lambda e: e.matmul(p_[:, 0:c1 - c0], lhsT=uT_[:, kc, :], rhs=wbf[:, kc, c0:c1],
                                                                                   start=(kc == 0), stop=(kc == 7)), sig=(kc == 7))
                    return p_, pb_

                def rope_fix(p_, pb_, src_view, dst_view, shp):
                    a, b = shp
                    A = rs[:, 0, 0:a * b, :].rearrange("p (a b) d -> p a b d", a=a)
                    B = rs[:, 1, 0:a * b, :].rearrange("p (a b) d -> p a b d", a=a)

                    def tb(lo, hi):
                        return bcast(bcast(cs32[:, lo:hi], 1, b), 1, a)
                    cx.op("dve", [pb_, ropeb], [rsb], lambda e: e.tensor_tensor(out=A, in0=src_view, in1=tb(0, 16), op=ALU.mult))
                    cx.op("dve", [pb_, ropeb, rsb], [rsb], lambda e: e.tensor_tensor(out=B[:, :, :, 0:8], in0=src_view[:, :, :, 8:16], in1=tb(16, 24), op=ALU.mult))
                    cx.op("dve", [pb_, ropeb, rsb], [rsb], lambda e: e.tensor_tensor(out=B[:, :, :, 8:16], in0=src_view[:, :, :, 0:8], in1=tb(24, 32), op=ALU.mult))
                    cx.op("dve", [rsb, tqb], [tqb], lambda e: e.tensor_tensor(out=dst_view, in0=A, in1=B, op=ALU.add))

                def hv(ap2d, a, b, astride_cols):
                    base = ap2d
                    dims = [list(base.ap[0]), [astride_cols, a], [64, b], [1, 16]]
                    return bass.AP(base.tensor, base.offset, dims)

                for (c0, dst0, need) in ((0, FB_QA * 128, isq), (512, FB_KA * 128, True), (1536, FB_QB * 128, isq)):
                    if not need:
                        continue
                    p_, pb_ = proj(c0, c0 + 512)
                    cx.op("act", [pb_], [tqb], lambda e: e.copy(out=tq[:, dst0:dst0 + 512], in_=p_[:, :]))
                    if STAGE >= 4:
                        rope_fix(p_, pb_, hv(p_[:, 0:512], 1, 8, 0), hv(tq[:, dst0:dst0 + 512], 1, 8, 0), (1, 8))
                if STAGE < 5:
                    continue
                p_, pb_ = proj(1024, 1536)
                v_, vb_ = vst[i]
                cx.op("act", [pb_], [vb_], lambda e: e.copy(out=v_[:, :, 0:64], in_=p_[:, :].rearrange("p (h d) -> p h d", d=64)))
                if "va" not in SKIP:
                    cx.dma([vb_], [self.vab], self.va[t * 128:(t + 1) * 128, :, :], v_[:])
                p_, pb_ = proj(2048, 2560)
                d0 = FB_KC * 128
                cx.op("act", [pb_], [tqb], lambda e: e.copy(out=tq[:, d0:d0 + 384], in_=p_[:, 0:384]))
                rope_fix(p_, pb_, hv(p_[:, 0:384], 2, 2, 256), hv(tq[:, d0:d0 + 384], 2, 2, 256), (2, 2))
                v_, vb_ = vsst[i]
                cx.op("act", [pb_], [vb_], lambda e: e.copy(out=v_[:, :, 0:64], in_=p_[:, 384:512].rearrange("p (h d) -> p h d", d=64)))
                cx.dma([vb_], [self.vslb], self.vsl[t * 128:(t + 1) * 128, :, :], v_[:])
                p_, pb_ = proj(2560, INW)
                d0 = FB_KW * 128
                cx.op("act", [pb_], [tqb], lambda e: e.copy(out=tq[:, d0:d0 + 128], in_=p_[:, 0:128]))
                rope_fix(p_, pb_, hv(p_[:, 0:128], 1, 2, 0), hv(tq[:, d0:d0 + 128], 1, 2, 0), (1, 2))
                v_, vb_ = vwst[i]
                cx.op("act", [pb_], [vb_], lambda e: e.copy(out=v_[:, :, 0:64], in_=p_[:, 128:256].rearrange("p (h d) -> p h d", d=64)))
                cx.dma([vb_], [self.vwnb], self.vwn[t * 128:(t + 1) * 128, :, :], v_[:])
                if isq:
                    cx.op("act", [pb_], [self.gatesb], lambda e: e.activation(out=self.gates[:, t - Q0, :], in_=p_[:, 256:280], func=AF.Sigmoid))
                blocks = list(range(NFB)) if isq else [b for b in range(NFB) if not (FB_QA <= b < FB_KA or FB_QB <= b < FB_KC)]
                for j0 in range(0, len(blocks), 8):
                    bl = blocks[j0:j0 + 8]
                    to_, tob_ = tpo[(j0 // 8) % 2]
                    for jj, b in enumerate(bl):
                        cx.op("pe", [tqb, self.identb], [tob_], lambda e: e.transpose(out=to_[:, jj, :], in_=tq[:, b * 128:(b + 1) * 128], identity=self.ident[:]), sig=(jj == len(bl) - 1))
                    runs = []
                    for jj, b in enumerate(bl):
                        if runs and runs[-1][1] + runs[-1][2] == b:
                            runs[-1][2] += 1
                        else:
                            runs.append([jj, b, 1])
                    for ri, (jj, b, n) in enumerate(runs):
                        eng = "dve" if ri % 2 == 0 else "pool"
                        if eng == "pool":
                            eng = "dve"
                        cx.op(eng, [tob_], [fstb], lambda e: e.tensor_copy(out=fst[:, b:b + n, sub * 128:(sub + 1) * 128], in_=to_[:, jj:jj + n, :]))
                if sub == 3 or ti == len(tiles) - 1:
                    t0 = (t - sub) * 128
                    ntok = (sub + 1) * 128
                    if "ft" not in SKIP:
                        cx.dma([fstb], [self.ftb], self.ft[:, :, t0:t0 + ntok].rearrange("b p t -> p b t"), fst[:, :, 0:ntok])
            flush_store()
            cx.barrier()


def build(debug=(), phases=("mod", "proj", "cmp", "attnA", "attnB", "post1", "post2"), proj_tiles=None, a_hps=(0, 1, 2, 3), a_filter=None, b_chunks=None, p_chunks=None, scratch_in=()):
    k = K(debug=debug, scratch_in=scratch_in)
    cx = k.cx
    with contextlib.ExitStack() as es:
        k.consts(es)
        if "mod" in phases:
            k.phase_mod()
        with contextlib.ExitStack() as es_mid:
            if "proj" in phases:
                k.phase_proj(es_mid, tiles=proj_tiles)
            if "cmp" in phases:
                k.phase_compress(es_mid)
            if "attnA" in phases:
                k.phase_attnA(hps=a_hps, qfilter=a_filter)
            if "attnB" in phases:
                k.phase_attnB(chunks=b_chunks)
            cx.barrier()
        if "post1" in phases:
            k.phase_post1(chunks=p_chunks)
        if "post2" in phases:
            k.phase_post2(chunks=p_chunks)
        allb = [k.outbuf, k.ftb, k.vab, k.vslb, k.vwnb, k.otb, k.x1b] + [b for (_, b) in k.dbg.values()]
        cx.finish(allb)
    return k


def make_in_maps(inputs):
    x = np.asarray(inputs["x"], np.float32)
    maps = []
    tabs = {h: host_tables(h) for h in (0, 1)}
    for core in range(8):
        b, half = core // 2, core % 2
        m = {}
        if half == 1:
            m["xk"] = np.ascontiguousarray(x[b])
        else:
            m["xk"] = np.concatenate([np.zeros((4096, D), np.float32), x[b, 0:4096]], 0)
        for name, shape in WEIGHT_SPECS:
            a = np.asarray(inputs[name], np.float32)
            if name == "c":
                a = a[b:b + 1]
            else:
                a = a[0]
            m[name] = np.ascontiguousarray(a.reshape(shape))
        m.update(tabs[half])
        maps.append(m)
    return maps


def kernel(**inputs):
    k = build()
    maps = make_in_maps(inputs)
    res = run_bass_kernel_spmd(k.nc, maps, core_ids=list(range(8)))
    out = np.zeros((4, S, D), np.float32)
    for core in range(8):
        b, half = core // 2, core % 2
        out[b, half * 4096:(half + 1) * 4096] = res.results[core]["out"]
    return out


def _attnA(self, hps=(0, 1, 2, 3), qfilter=None):
    cx, nc = self.cx, self.nc
    QBASE = Q0 * 128
    pats = []
    for d, mlo, mhi in ((1, 31, 63), (4, 7, 15), (16, 1, 3)):
        for r in range(d):
            for m in range(mlo, mhi + 1):
                j0 = max(0, -(-(QBASE - r) // d) - 128 * m)
                if j0 >= 128:
                    continue
                pats.append((d, r, m, j0))
    if qfilter is not None:
        pats = [p for p in pats if qfilter(p)]
    vtiles = {1: (30, 34), 4: (6, 10), 16: (0, 4)}
    with contextlib.ExitStack() as es:
        self.m01, self.m01b = self.sb(es, "m01", [128, 3, 128], BF16)
        cx.op("dve", [self.masksb], [self.m01b], lambda e: e.tensor_scalar(out=self.m01[:], in0=self.masks[:], scalar1=0.0, scalar2=None, op0=ALU.is_equal))
        KTs = [self.sb(es, "a_KT%d" % i, [128, S], BF16) for i in range(2)]
        QTs = [self.sb(es, "a_QT%d" % i, [128, QTOK], BF16) for i in range(2)]
        Vps = [{d: self.sb(es, "a_V%d_%d" % (d, i), [128, d * vtiles[d][1], 2, VW], BF16) for d in (1, 4, 16)} for i in range(2)]
        acc, accb = self.sb(es, "a_acc", [65, 2, QTOK], F32)
        otst, otstb = self.sb(es, "a_otst", [64, 2, QTOK], BF16)
        PT = [self.sb(es, "a_PT%d" % i, [128, 128], BF16) for i in range(8)]
        STp = [self.ps(es, "a_ST%d" % i, [128, 512], F32) for i in range(3)]
        OTp = [self.ps(es, "a_OT%d" % i, [65, 512], F32) for i in range(4)]
        BCp = [self.ps(es, "a_BC%d" % i, [64, 512], F32) for i in range(1)] * 2
        cnt = 0
        ocnt = 0

        def load(hi):
            hp = hps[hi]
            KT, KTb = KTs[hi % 2]
            QT, QTb = QTs[hi % 2]
            cx.dma([self.ftb], [KTb], KT[:, :], self.ft[FB_KA + hp, :, :])
            cx.dma([self.ftb], [QTb], QT[:, :], self.ft[FB_QA + hp, :, QBASE:S])
            for d in (1, 4, 16):
                k0, nk = vtiles[d]
                V_, Vb_ = Vps[hi % 2][d]
                rows = self.va[d * 128 * k0:d * 128 * (k0 + nk), 2 * hp:2 * hp + 2, :].rearrange("(kk j dd) h w -> dd j kk h w", j=128, dd=d)
                for r in range(d):
                    cx.dma([self.vab], [Vb_], V_[:, r * nk:(r + 1) * nk, :, :], rows[r])

        load(0)
        for hi, hp in enumerate(hps):
            if hi + 1 < len(hps):
                load(hi + 1)
            KT, KTb = KTs[hi % 2]
            QT, QTb = QTs[hi % 2]
            Vp = Vps[hi % 2]
            cx.op("pool", [], [accb], lambda e: e.memset(acc[:], 0.0))
            aq = []

            def pv_emit(pend):
                (P_, Pb_, O_, Ob_, V_, Vb_, vidx, h, n, ki, av) = pend
                cx.op("pe", [Pb_, Vb_], [Ob_], lambda e: e.matmul(O_[:, 0:n], lhsT=V_[:, vidx, h, 0:65], rhs=P_[:, 0:n], start=(ki == 0), stop=(ki == 1)), sig=(ki == 1))
                if ki == 1:
                    cx.op("dve", [Ob_, accb], [accb], lambda e: e.tensor_tensor(out=av, in0=av, in1=O_[:, 0:n], op=ALU.add))

            for (d, r, m, j0) in pats:
                n = 128 - j0
                k0, nk = vtiles[d]
                V_, Vb_ = Vp[d]
                qs = d * (128 * m + j0) + r - QBASE
                for h in range(2):
                    pb = 64 * h
                    qv = QT[pb:pb + 64, qs:qs + d * (n - 1) + 1:d]
                    O_, Ob_ = OTp[ocnt % 4]
                    ocnt += 1
                    av = acc[:, h, qs:qs + d * (n - 1) + 1:d]
                    for ki, (kt, mi) in enumerate(((m - 1, 1), (m, 0))):
                        S_, Sb_ = STp[cnt % 2]
                        P_, Pb_ = PT[cnt % 8]
                        cnt += 1
                        ks = d * 128 * kt + r
                        kv = KT[pb:pb + 64, ks:ks + d * 127 + 1:d]
                        cx.op("pe", [KTb, QTb], [Sb_], lambda e: e.matmul(S_[:, 0:n], lhsT=kv, rhs=qv, start=True, stop=True))
                        cx.op("act", [Sb_, self.keybiasb], [Pb_], lambda e: e.activation(out=P_[:, 0:n], in_=S_[:, 0:n], func=AF.Exp, bias=self.keybias[:, d * kt:d * kt + 1], scale=0.125))
                        cx.op("pool" if cnt % 2 == 0 else "dve", [Pb_, self.m01b], [Pb_], lambda e: e.tensor_tensor(out=P_[:, 0:n], in0=P_[:, 0:n], in1=self.m01[:, mi, j0:128], op=ALU.mult))
                        aq.append((P_, Pb_, O_, Ob_, V_, Vb_, r * nk + (kt - k0), h, n, ki, av))
                        while len(aq) > 4:
                            pv_emit(aq.pop(0))
            while aq:
                pv_emit(aq.pop(0))
            for h in range(2):
                cx.op("dve", [accb], [accb], lambda e: e.tensor_scalar_max(out=acc[64:65, h, :], in0=acc[64:65, h, :], scalar1=1e-30))
                cx.op("dve", [accb], [accb], lambda e: e.reciprocal(out=acc[64:65, h, :], in_=acc[64:65, h, :]))
                for c0 in range(0, QTOK, 512):
                    w = min(512, QTOK - c0)
                    B_, Bb_ = BCp[(c0 // 512) % 2]
                    cx.op("pe", [accb, self.onesb], [Bb_], lambda e: e.matmul(B_[:, 0:w], lhsT=self.ones[64:65, 0:64], rhs=acc[64:65, h, c0:c0 + w], start=True, stop=True))
                    cx.op("dve", [accb, Bb_], [otstb], lambda e: e.tensor_tensor(out=otst[:, h, c0:c0 + w], in0=acc[0:64, h, c0:c0 + w], in1=B_[:, 0:w], op=ALU.mult))
            cx.dma([otstb], [self.otb], self.ot[2 * hp:2 * hp + 2, :, :].rearrange("h d t -> d h t"), otst[:, :, :])
        cx.barrier()


K.phase_attnA = _attnA


def _compress(self, es_persist):
    cx, nc = self.cx, self.nc
    self.kccT, self.kccTb = self.sb(es_persist, "kccT", [128, 512], BF16)
    self.vcc1, self.vcc1b = self.sb(es_persist, "vcc1", [128, 4, 2, VW], BF16)
    cx.op("pool", [], [self.kccTb], lambda e: e.memset(self.kccT[:], 0.0))
    cx.op("pool", [], [self.vcc1b], lambda e: e.memset(self.vcc1[:], 0.0))
    cx.op("pool", [self.vcc1b], [self.vcc1b], lambda e: e.memset(self.vcc1[:, :, :, 64:65], 1.0))
    with contextlib.ExitStack() as es:
        w1st, w1stb = self.sb(es, "c_w1st", [128, 16, 256], F32)
        w1 = [self.sb(es, "c_w1_%d" % i, [128, 16, 256], BF16) for i in range(2)]
        w2st, w2stb = self.sb(es, "c_w2st", [128, 2, 64], F32)
        w2 = [self.sb(es, "c_w2_%d" % i, [128, 2, 64], BF16) for i in range(2)]
        pest, pestb = self.sb(es, "c_pest", [128, 16], F32)
        pebf, pebfb = self.sb(es, "c_pebf", [128, 16], BF16)
        pebias, pebiasb = self.sb(es, "c_pebias", [128, 2, 2], F32)
        X2, X2b = self.sb(es, "c_X2", [128, S], BF16)
        hT, hTb = self.sb(es, "c_hT", [128, 2, 512], BF16)
        HP = [self.ps(es, "c_HP%d" % i, [128, 512], F32) for i in range(2)]
        OP, OPb = self.ps(es, "c_OP", [128, 512], F32)
        BP, BPb = self.ps(es, "c_BP", [128, 4], F32)
        cx.dma([], [pestb], pest[0:64, :], self.inp["pe_cmp"].rearrange("(c j) d -> j d c", j=2)[0], allow_slow_non_contiguous=True)
        cx.dma([], [pestb], pest[64:128, :], self.inp["pe_cmp"].rearrange("(c j) d -> j d c", j=2)[1], allow_slow_non_contiguous=True)
        cx.op("dve", [pestb], [pebfb], lambda e: e.tensor_copy(out=pebf[:], in_=pest[:]))
        for kv, (n1, n2) in enumerate((("w_ck1", "w_ck2"), ("w_cv1", "w_cv2"))):
            cx.dma([], [w1stb], w1st[:, :, :], self.inp[n1].rearrange("(c p) h -> p c h", p=128))
            cx.op("dve", [w1stb], [w1[kv][1]], lambda e: e.tensor_copy(out=w1[kv][0][:, 0:8, :], in_=w1st[:, 0:8, :]))
            cx.op("pool", [w1stb], [w1[kv][1]], lambda e: e.tensor_copy(out=w1[kv][0][:, 8:16, :], in_=w1st[:, 8:16, :]))
            cx.dma([], [w2stb], w2st[:, :, :], self.inp[n2].rearrange("(c p) h -> p c h", p=128))
            cx.op("dve", [w2stb], [w2[kv][1]], lambda e: e.tensor_copy(out=w2[kv][0][:], in_=w2st[:]))
            for hh in range(2):
                for c in range(16):
                    cx.op("pe", [w1[kv][1], pebfb], [BPb], lambda e: e.matmul(BP[:, 2 * kv + hh:2 * kv + hh + 1], lhsT=w1[kv][0][:, c, hh * 128:(hh + 1) * 128], rhs=pebf[:, c:c + 1],
                                                                          start=(c == 0), stop=(c == 15)), sig=(c == 15))
        cx.op("dve", [BPb], [pebiasb], lambda e: e.tensor_copy(out=pebias[:].rearrange("p a b -> p (a b)"), in_=BP[:, :]))
        for kv in range(2):
            blk = FB_KC if kv == 0 else FB_VC
            for g in range(2):
                cx.dma([self.ftb], [X2b], X2[0:64, :], self.ft[blk, g * 64:(g + 1) * 64, :])
                cx.dma([self.ftb], [X2b], X2[64:128, 0:S - 1], self.ft[blk, g * 64:(g + 1) * 64, 1:S])
                if kv == 0 and g == 0:
                    cx.op("pool", [X2b], [X2b], lambda e: e.memset(X2[64:128, S - 1:S], 0.0))
                for hh in range(2):
                    H_, Hb_ = HP[hh]
                    for c in range(16):
                        cx.op("pe", [w1[kv][1], X2b], [Hb_], lambda e: e.matmul(H_[:, 0:511], lhsT=w1[kv][0][:, c, hh * 128:(hh + 1) * 128], rhs=X2[:, 2 * c:2 * c + 16 * 510 + 1:16],
                                                                              start=(c == 0), stop=(c == 15)), sig=(c == 15))
                    cx.op("act", [Hb_, pebiasb], [hTb], lambda e: e.activation(out=hT[:, hh, 0:511], in_=H_[:, 0:511], func=AF.Gelu_apprx_tanh, bias=pebias[:, kv, hh:hh + 1]))
                if kv == 0:
                    for hh in range(2):
                        cx.op("pe", [hTb, w2[0][1]], [OPb], lambda e: e.matmul(OP[64 * g:64 * g + 64, 0:511], lhsT=w2[0][0][:, hh, :], rhs=hT[:, hh, 0:511], start=(hh == 0), stop=(hh == 1)), sig=(hh == 1))
                    cx.op("dve", [OPb], [self.kccTb], lambda e: e.tensor_copy(out=self.kccT[64 * g:64 * g + 64, 0:511], in_=OP[64 * g:64 * g + 64, 0:511]))
                else:
                    for nt in range(4):
                        w = 128 if nt < 3 else 127
                        for hh in range(2):
                            cx.op("pe", [hTb, w2[1][1]], [OPb], lambda e: e.matmul(OP[0:w, nt * 64:(nt + 1) * 64], lhsT=hT[:, hh, nt * 128:nt * 128 + w], rhs=w2[1][0][:, hh, :],
                                                                               start=(hh == 0), stop=(hh == 1)), sig=(hh == 1 and nt == 3))
                    for nt in range(4):
                        w = 128 if nt < 3 else 127
                        cx.op("dve", [OPb], [self.vcc1b], lambda e: e.tensor_copy(out=self.vcc1[0:w, nt, g, 0:64], in_=OP[0:w, nt * 64:(nt + 1) * 64]))
        cx.barrier()
        if "cmp" in self.debug:
            ap, b = self.dbg_out("kccT", [128, 512], BF16)
            cx.dma([self.kccTb], [b], ap[:, :], self.kccT[:])
            ap, b = self.dbg_out("vcc1", [128, 4, 2, VW], BF16)
            cx.dma([self.vcc1b], [b], ap[:, :, :, :], self.vcc1[:])


K.phase_compress = _compress


def _attnB(self, chunks=None):
    cx, nc = self.cx, self.nc
    chunks = list(range(NQ)) if chunks is None else chunks
    QBASE = Q0 * 128
    with contextlib.ExitStack() as es:
        if not hasattr(self, "gates"):
            gin = nc.dram_tensor("gates_in", [128, NQ, 24], F32, kind="ExternalInput").ap()
            self.gates, self.gatesb = self.sb(es, "gates", [128, NQ, 24], F32)
            cx.dma([], [self.gatesb], self.gates[:], gin[:, :, :])
        kslT2 = [self.sb(es, "b_kslT%d" % g, [128, NT // 2, 128], BF16) for g in range(2)]
        kwT, kwTb = self.sb(es, "b_kwT", [128, S], BF16)
        vsl1, vsl1b = self.sb(es, "b_vsl1", [128, NT, 2, VW], BF16)
        vw1, vw1b = self.sb(es, "b_vw1", [128, NT, 2, VW], BF16)
        QB2 = [self.sb(es, "b_QB%d" % g, [128, 4, QTOK], BF16) for g in range(2)]
        eall, eallb = self.sb(es, "b_eall", [128, NT * 128], BF16)
        ovl, ovlb = self.sb(es, "b_ovl", [128, 4, 128], BF16)
        realb, realbb = self.sb(es, "b_realb", [128, 128], F32)
        cmpm = [self.sb(es, "b_cmpm%d" % i, [128, 4, 128], BF16) for i in range(2)]
        selb = [self.sb(es, "b_selb%d" % i, [128, 128], F32) for i in range(2)]
        PT = [self.sb(es, "b_PT%d" % i, [128, 2, 512], BF16) for i in range(8)]
        small = [self.sb(es, "b_small%d" % i, [128, 64], F32) for i in range(2)]
        score = [self.sb(es, "b_score%d" % i, [128, 2, 128], F32) for i in range(2)]
        selbias = [self.sb(es, "b_selbias%d" % i, [128, 128], F32) for i in range(2)]
        selT = [self.sb(es, "b_selT%d" % i, [128, 128], BF16) for i in range(2)]
        OCs = [self.sb(es, "b_OCs%d" % i, [128, 4, VW], F32) for i in range(2)]
        tmpo = [self.sb(es, "b_tmpo%d" % i, [128, 3, 4, 64], F32) for i in range(2)]
        ob = [self.sb(es, "b_ob%d" % i, [128, 256], F32) for i in range(2)]
        obT = [self.sb(es, "b_obT%d" % i, [128, 2, 128], BF16) for i in range(2)]
        STq = es.enter_context(nc.psum_tensor("p_b_STq", [128, 4, 512], F32))
        stb = [cx.buf("stq0"), cx.buf("stq1")]
        for b_ in stb:
            b_.psum = True
        Mp = [self.ps(es, "b_M%d" % i, [128, 4, 128], F32) for i in range(2)]
        OS, OSb = self.ps(es, "b_OS", [128, 4, VW], F32)
        OCW, OCWb = self.ps(es, "b_OCW", [128, 4, VW], F32)

        for g in range(2):
            src = self.ft[FB_KSL, g * 64:(g + 1) * 64, :].rearrange("d (pr par k) -> par d pr k", par=2, k=128)
            for par in range(2):
                cx.dma([self.ftb], [kslT2[g][1]], kslT2[g][0][par * 64:(par + 1) * 64, :, :], src[par])
        cx.dma([self.ftb], [kwTb], kwT[:, :], self.ft[FB_KW, :, :])
        for q4 in range(4):
            r0, r1 = q4 * 16 * 128, (q4 + 1) * 16 * 128
            cx.dma([self.vslb], [vsl1b], vsl1[:, q4 * 16:(q4 + 1) * 16, :, :], self.vsl[r0:r1, :, :].rearrange("(t p) g w -> p t g w", p=128))
            cx.dma([self.vwnb], [vw1b], vw1[:, q4 * 16:(q4 + 1) * 16, :, :], self.vwn[r0:r1, :, :].rearrange("(t p) g w -> p t g w", p=128))
        for g in range(2):
            for hq in range(4):
                blk = FB_QB + (g * 4 + hq) // 2
                prow = ((g * 4 + hq) % 2) * 64
                for half in range(2):
                    cx.dma([self.ftb], [QB2[g][1]], QB2[g][0][64 * half:64 * half + 64, hq, :], self.ft[blk, prow:prow + 64, QBASE:S])
        cx.dma([], [eallb], eall[:, :], self.inp["eall"][:, :])
        cx.dma([], [ovlb], ovl[:, :, :], self.inp["overlap"][:, :, :])
        cx.dma([], [realbb], realb[:, :], self.inp["realb"][:, :])
        Msb = [self.sb(es, "b_Msb%d" % i, [128, 4, 128], BF16) for i in range(3)]
        cnt = 0
        mcnt = 0
        mscnt = 0
        units = [(ci, c, g) for ci, c in enumerate(chunks) for g in range(2)]
        state = {}

        def nxt():
            nonlocal cnt
            i = cnt % 2
            P_, Pb_ = PT[cnt % 8]
            cnt += 1
            return STq[:, 2 * i:2 * i + 2, :], stb[i], P_, Pb_

        def load_tables(ci):
            c = chunks[ci]
            cx.dma([], [cmpm[ci % 2][1]], cmpm[ci % 2][0][:, :, :], self.inp["cmpm"][:, c, :, :])
            cx.dma([], [selb[ci % 2][1]], selb[ci % 2][0][:, :], self.inp["selb"][:, c, :])

        def cmp_stage(idx):
            nonlocal mcnt
            ci, c, g = units[idx]
            sc = Q0 + c
            if g == 0 and ci + 1 < len(chunks):
                load_tables(ci + 1)
            cm_, cmb_ = cmpm[ci % 2]
            pb = 64 * g
            QB, QBb = QB2[g]
            qv = QB[pb:pb + 64, :, c * 128:(c + 1) * 128]
            nts = (8 * sc + 6) // 128 + 1
            pts = []
            for nt in range(nts):
                S2, Sb_, P_, Pb_ = nxt()
                cx.op("pe", [self.kccTb, QBb], [Sb_], lambda e: e.matmul(S2[:, 0, :], lhsT=self.kccT[pb:pb + 64, nt * 128:(nt + 1) * 128], rhs=qv, start=True, stop=True))
                cx.op("act", [Sb_], [Pb_], lambda e: e.activation(out=P_[:, 0, :], in_=S2[:, 0, :], func=AF.Exp, scale=0.125))
                pv4 = P_[:, 0, :].rearrange("p (h q) -> p h q", h=4)
                cx.op("dve", [Pb_, cmb_], [Pb_], lambda e: e.tensor_tensor(out=pv4, in0=pv4, in1=bcast(cm_[:, nt, :], 1, 4), op=ALU.mult))
                pts.append((P_, Pb_))
            IMP, IMPb = Mp[mcnt % 2]
            mcnt += 1
            for h in range(4):
                for nt in range(nts):
                    P_, Pb_ = pts[nt]
                    cx.op("pe", [Pb_, self.vcc1b], [OCWb], lambda e: e.matmul(OCW[:, h, 0:65], lhsT=P_[:, 0, h * 128:(h + 1) * 128], rhs=self.vcc1[:, nt, g, 0:65], start=(nt == 0 and h == 0), stop=(nt == nts - 1 and h == 3)), sig=(nt == nts - 1 and h == 3))
            for h in range(4):
                for nt in range(nts):
                    P_, Pb_ = pts[nt]
                    cx.op("pe", [Pb_, ovlb], [IMPb], lambda e: e.matmul(IMP[:, h, :], lhsT=P_[:, 0, h * 128:(h + 1) * 128], rhs=ovl[:, nt, :], start=(nt == 0 and h == 0), stop=(nt == nts - 1 and h == 3)), sig=(h == 3 and nt == nts - 1))
            state[idx] = (IMP, IMPb)

        def main_stage(idx):
            nonlocal mcnt, mscnt
            ci, c, g = units[idx]
            sc = Q0 + c
            sb_, sbb_ = selb[ci % 2]
            u = idx % 2
            pb = 64 * g
            QB, QBb = QB2[g]
            kT, kTb = kslT2[g]
            qv = QB[pb:pb + 64, :, c * 128:(c + 1) * 128]
            qlo = QB[0:64, :, c * 128:(c + 1) * 128]
            qhi = QB[64:128, :, c * 128:(c + 1) * 128]
            sm_, smb_ = small[u]
            IMP, IMPb = state.pop(idx)
            oc_, ocb_ = OCs[u]
            cx.op("dve", [OCWb], [ocb_], lambda e: e.tensor_copy(out=oc_[:, :, 0:65], in_=OCW[:, :, 0:65]))
            cx.op("dve", [ocb_], [smb_], lambda e: e.tensor_scalar_max(out=sm_[:, 0:4], in0=oc_[:, :, 64], scalar1=1e-30))
            cx.op("dve", [smb_], [smb_], lambda e: e.reciprocal(out=sm_[:, 0:4], in_=sm_[:, 0:4]))
            sco, scob = score[u]
            cx.op("dve", [IMPb, smb_], [scob], lambda e: e.tensor_scalar(out=sco[:, 0, :], in0=IMP[:, 0, :], scalar1=sm_[:, 0:1], scalar2=None, op0=ALU.mult))
            for h in range(1, 4):
                cx.op("dve", [IMPb, smb_, scob], [scob], lambda e: e.scalar_tensor_tensor(out=sco[:, 0, :], in0=IMP[:, h, :], scalar=sm_[:, h:h + 1], in1=sco[:, 0, :], op0=ALU.mult, op1=ALU.add))
            cx.op("dve", [scob, sbb_], [scob], lambda e: e.tensor_tensor(out=sco[:, 0, :], in0=sco[:, 0, :], in1=sb_[:, :], op=ALU.add))
            cx.op("dve", [scob], [smb_], lambda e: e.max(out=sm_[:, 32:40], in_=sco[:, 0, :]))
            cx.op("dve", [scob, smb_], [scob], lambda e: e.match_replace(out=sco[:, 1, :], in_to_replace=sm_[:, 32:40], in_values=sco[:, 0, :], imm_value=-3e38))
            cx.op("dve", [scob], [smb_], lambda e: e.max(out=sm_[:, 40:48], in_=sco[:, 1, :]))
            cx.op("dve", [scob, smb_], [scob], lambda e: e.tensor_scalar(out=sco[:, 1, :], in0=sco[:, 0, :], scalar1=sm_[:, 47:48], scalar2=None, op0=ALU.is_ge))
            sbi, sbib = selbias[u]
            cx.op("dve", [scob, realbb], [sbib], lambda e: e.tensor_tensor(out=sbi[:, :], in0=sco[:, 1, :], in1=realb[:, :], op=ALU.mult))
            queue = []

            def pv_emit(pend):
                (P_, Pb_, j, O_, Ob_, V_, Vb_, kt, first, last) = pend
                for h in range(4):
                    cx.op("pe", [Pb_, Vb_], [Ob_], lambda e: e.matmul(O_[:, h, 0:65], lhsT=P_[:, j, h * 128:(h + 1) * 128], rhs=V_[:, kt, g, 0:65], start=(first and h == 0), stop=(last and h == 3)), sig=(h == 3))

            def push(pend, lag=2):
                queue.append(pend)
                while len(queue) > lag:
                    pv_emit(queue.pop(0))

            kts = list(range(sc - 4, sc + 1))
            for ki, kt in enumerate(kts):
                S2, Sb_, P_, Pb_ = nxt()
                mi = 2 if ki == 0 else (0 if ki == 4 else None)
                cx.op("pe", [kwTb, QBb], [Sb_], lambda e: e.matmul(S2[:, 0, :], lhsT=kwT[pb:pb + 64, kt * 128:(kt + 1) * 128], rhs=qv, start=True, stop=(mi is None)), sig=(mi is None))
                if mi is not None:
                    cx.op("pe", [self.identb, self.masksb], [Sb_], lambda e: e.matmul(S2[:, 0, :], lhsT=self.ident[:, :], rhs=bcast(self.masks[:, mi, :], 1, 4), start=False, stop=True))
                cx.op("act", [Sb_, self.keybiasb], [Pb_], lambda e: e.activation(out=P_[:, 0, :], in_=S2[:, 0, :], func=AF.Exp, bias=self.keybias[:, kt:kt + 1], scale=0.125))
                push((P_, Pb_, 0, OCW, OCWb, vw1, vw1b, kt, ki == 0, ki == 4))
            Mt, Mtb = Mp[mcnt % 2]
            mcnt += 1
            cx.op("pe", [sbib, self.identfb], [Mtb], lambda e: e.transpose(out=Mt[:, 0, :], in_=sbi[:, :], identity=self.identf[:]))
            sT, sTb = selT[u]
            cx.op("dve", [Mtb], [sTb], lambda e: e.tensor_copy(out=sT[:, :], in_=Mt[:, 0, :]))
            npairs = (sc + 2) // 2
            for p in range(npairs):
                if p % 2 == 0:
                    M_, Mb_ = Mp[mcnt % 2]
                    mcnt += 1
                    Ms_, Msb_ = Msb[mscnt % 3]
                    mscnt += 1
                    nk4 = min(4, sc + 1 - 2 * p)
                    for j in range(nk4):
                        cx.op("pe", [eallb, sTb], [Mb_], lambda e: e.matmul(M_[:, j, :], lhsT=eall[:, (2 * p + j) * 128:(2 * p + j + 1) * 128], rhs=sT[:, :], start=True, stop=True), sig=(j == nk4 - 1))
                    if mscnt % 2 == 0:
                        cx.op("act", [Mb_], [Msb_], lambda e: e.copy(out=Ms_[:, 0:nk4, :], in_=M_[:, 0:nk4, :]))
                    else:
                        cx.op("dve", [Mb_], [Msb_], lambda e: e.tensor_copy(out=Ms_[:, 0:nk4, :], in_=M_[:, 0:nk4, :]))
                S2, Sb_, P_, Pb_ = nxt()
                nk = 2 if 2 * p + 1 <= sc else 1
                for j in range(nk):
                    kt = 2 * p + j
                    diag = (kt == sc)
                    cx.op("pe", [kTb, QBb], [Sb_], lambda e: e.matmul(S2[:, j, :], lhsT=kT[64 * j:64 * j + 64, p, :], rhs=(qlo if j == 0 else qhi), start=True, stop=(not diag)), sig=(j == nk - 1 and not diag))
                if 2 * p + nk - 1 == sc:
                    j = nk - 1
                    cx.op("pe", [self.identb, self.masksb], [Sb_], lambda e: e.matmul(S2[:, j, :], lhsT=self.ident[:, :], rhs=bcast(self.masks[:, 0, :], 1, 4), start=False, stop=True))
                cx.op("act", [Sb_], [Pb_], lambda e: e.activation(out=P_[:, 0:nk, :], in_=S2[:, 0:nk, :], func=AF.Exp, scale=0.125))
                pv5 = P_[:, 0:nk, :].rearrange("p k (h q) -> p k h q", h=4)
                m0 = (2 * p) % 4
                cx.op("dve", [Pb_, Msb_], [Pb_], lambda e: e.tensor_tensor(out=pv5, in0=pv5, in1=bcast(Ms_[:, m0:m0 + nk, :], 2, 4), op=ALU.mult))
                for j in range(nk):
                    kt = 2 * p + j
                    push((P_, Pb_, j, OS, OSb, vsl1, vsl1b, kt, kt == 0, kt == sc), lag=8)
            while queue:
                pv_emit(queue.pop(0))
            cx.op("dve", [OSb], [smb_], lambda e: e.tensor_scalar_max(out=sm_[:, 4:8], in0=OS[:, :, 64], scalar1=1e-30))
            cx.op("dve", [OCWb, smb_], [smb_], lambda e: e.tensor_scalar_max(out=sm_[:, 8:12], in0=OCW[:, :, 64], scalar1=1e-30))
            cx.op("dve", [smb_], [smb_], lambda e: e.reciprocal(out=sm_[:, 4:12], in_=sm_[:, 4:12]))
            for br in range(3):
                gv = self.gates[:, c, g * 12 + br:g * 12 + br + 10:3]
                cx.op("dve", [smb_, self.gatesb], [smb_], lambda e: e.tensor_tensor(out=sm_[:, 16 + 4 * br:20 + 4 * br], in0=sm_[:, 4 * br:4 * br + 4], in1=gv, op=ALU.mult))
            tm, tmb = tmpo[u]
            for br, (O_, Ob_) in enumerate(((oc_, ocb_), (OS, OSb), (OCW, OCWb))):
                cx.op("dve", [Ob_, smb_], [tmb], lambda e: e.tensor_tensor(out=tm[:, br, :, :], in0=O_[:, :, 0:64], in1=bcast(sm_[:, 16 + 4 * br:20 + 4 * br], 2, 64), op=ALU.mult))

        def tail_stage(idx):
            nonlocal mcnt
            ci, c, g = units[idx]
            u = idx % 2
            tm, tmb = tmpo[u]
            cx.op("pool", [tmb], [tmb], lambda e: e.tensor_tensor(out=tm[:, 0, :, :], in0=tm[:, 0, :, :], in1=tm[:, 1, :, :], op=ALU.add))
            o_, ob_ = ob[u]
            cx.op("pool", [tmb], [ob_], lambda e: e.tensor_tensor(out=o_[:, :].rearrange("p (h d) -> p h d", d=64), in0=tm[:, 0, :, :], in1=tm[:, 2, :, :], op=ALU.add))
            Mt, Mtb = Mp[mcnt % 2]
            mcnt += 1
            for j in range(2):
                cx.op("pe", [ob_, self.identfb], [Mtb], lambda e: e.transpose(out=Mt[:, j, :], in_=o_[:, j * 128:(j + 1) * 128], identity=self.identf[:]), sig=(j == 1))
            oT, oTb = obT[u]
            cx.op("act", [Mtb], [oTb], lambda e: e.copy(out=oT[:, :, :], in_=Mt[:, 0:2, :]))
            r0 = (8 + 4 * g) * 64
            otv = self.ot.rearrange("h d t -> (h d) t")
            for j in range(2):
                cx.dma([oTb], [self.otb], otv[r0 + j * 128:r0 + (j + 1) * 128, c * 128:(c + 1) * 128], oT[:, j, :])

        load_tables(0)
        cmp_stage(0)
        for idx in range(len(units)):
            main_stage(idx)
            if idx + 1 < len(units):
                cmp_stage(idx + 1)
            tail_stage(idx)
        cx.barrier()


K.phase_attnB = _attnB


def _bc_load(self, es, name, src_row):
    t, b = self.sb(es, name, [128, D], F32)
    src = bass.AP(src_row.tensor, src_row.offset, [[0, 128], [1, D]])
    self.cx.dma([], [b], t[:, :], src)
    return t, b


K.bc_load = _bc_load


def _post1(self, chunks=None):
    cx, nc = self.cx, self.nc
    chunks = list(range(NQ)) if chunks is None else chunks
    with contextlib.ExitStack() as es:
        wo, wob = self.sb(es, "p_wo", [128, 8, D], BF16)
        wst = [self.sb(es, "p_wost%d" % i, [128, 2, D], F32) for i in range(2)]
        wov = self.inp["w_o"].rearrange("(hp p) n -> p hp n", p=128)
        for q in range(4):
            st, stb = wst[q % 2]
            cx.dma([], [stb], st[:, :, :], wov[:, q * 2:(q + 1) * 2, :])
            cx.op("dve" if q % 2 == 0 else "pool", [stb], [wob], lambda e: e.tensor_copy(out=wo[:, q * 2:(q + 1) * 2, :], in_=st[:, :, :]))
        g1, g1b = self.bc_load(es, "p_ln1g", self.inp["ln1_g"])
        b1, b1b = self.bc_load(es, "p_ln1b", self.inp["ln1_b"])
        OTt = [self.sb(es, "p_OT%d" % i, [128, 8, 128], BF16) for i in range(2)]
        xt = [self.sb(es, "p_x%d" % i, [128, D], F32) for i in range(2)]
        tt = [self.sb(es, "p_t%d" % i, [128, D], F32) for i in range(2)]
        x1 = [self.sb(es, "p_x1%d" % i, [128, D], F32) for i in range(2)]
        lns = self.ln_stats(es, "lnq")
        Y = [self.ps(es, "p_Y%d" % i, [128, 512], F32) for i in range(4)]
        xk = self.inp["xk"].rearrange("(t p) d -> t p d", p=128)
        otv = self.ot.rearrange("(hp two) d t -> two d hp t", two=2)

        def load(ci):
            c = chunks[ci]
            i = ci % 2
            for two in range(2):
                cx.dma([self.otb], [OTt[i][1]], OTt[i][0][64 * two:64 * two + 64, :, :], otv[two][:, :, c * 128:(c + 1) * 128])
            cx.dma([], [xt[i][1]], xt[i][0][:, :], xk[Q0 + c, :, :])

        load(0)
        for ci, c in enumerate(chunks):
            i = ci % 2
            if ci + 1 < len(chunks):
                load(ci + 1)
            O_, Ob_ = OTt[i]
            x_, xb_ = xt[i]
            t_, tb_ = tt[i]
            for half in range(2):
                Y_, Yb_ = Y[(ci * 2 + half) % 4]
                for hp in range(8):
                    cx.op("pe", [Ob_, wob], [Yb_], lambda e: e.matmul(Y_[:, :], lhsT=O_[:, hp, :], rhs=wo[:, hp, half * 512:(half + 1) * 512], start=(hp == 0), stop=(hp == 7)), sig=(hp == 7))
                cx.op("dve", [Yb_, self.gbcb], [tb_], lambda e: e.tensor_tensor(out=t_[:, half * 512:(half + 1) * 512], in0=Y_[:, :], in1=self.gbc[:, 0, half * 512:(half + 1) * 512], op=ALU.mult))
            cx.op("dve", [xb_, tb_], [tb_], lambda e: e.scalar_tensor_tensor(out=t_[:, :], in0=x_[:, :], scalar=ALPHA, in1=t_[:, :], op0=ALU.mult, op1=ALU.add))
            mv, mvb = self.emit_ln_stats(lns[i], t_, tb_)
            x1_, x1b_ = x1[i]
            cx.op("act", [tb_, mvb], [x1b_], lambda e: e.activation(out=x1_[:, :], in_=t_[:, :], func=AF.Identity, bias=mv[:, 3:4], scale=mv[:, 2:3]))
            cx.op("dve", [x1b_, g1b], [x1b_], lambda e: e.tensor_tensor(out=x1_[:, :], in0=x1_[:, :], in1=g1[:, :], op=ALU.mult))
            cx.op("pool", [x1b_, b1b], [x1b_], lambda e: e.tensor_tensor(out=x1_[:, :], in0=x1_[:, :], in1=b1[:, :], op=ALU.add))
            cx.dma([x1b_], [self.x1b], self.x1d[c * 128:(c + 1) * 128, :], x1_[:, :])
        cx.barrier()


K.phase_post1 = _post1


def _post2(self, chunks=None):
    cx, nc = self.cx, self.nc
    chunks = list(range(NQ)) if chunks is None else chunks
    assert chunks[0] == 0
    body = chunks[1:]
    groups = [body[i:i + 2] for i in range(0, len(body), 2)]
    with contextlib.ExitStack() as es:
        wup, _ = self.sb(es, "f_wup", [128, 8, 2 * DFF], BF16)
        wupk = [cx.buf() for k in range(8)]
        wdn, wdnb = self.sb(es, "f_wdn", [128, NFC, D], BF16)
        if self.wcast_done:
            for kc in range(8):
                cx.dma([self.wupdb], [wupk[kc]], wup[:, kc, :], self.wupd[kc, :, :])
            for q in range(2):
                cx.dma([self.wdndb], [wdnb], wdn[:, q * 11:(q + 1) * 11, :], self.wdnd[:, q * 11:(q + 1) * 11, :])
        else:
            with contextlib.ExitStack() as es2:
                wst = [self.sb(es2, "f_wst%d" % i, [128, DFF], F32) for i in range(2)]
                wupv = self.inp["w_up"].rearrange("(k p) n -> k p n", p=128)
                n = 0
                for kc in range(8):
                    for hf in range(2):
                        st, stb = wst[n % 2]
                        cx.dma([], [stb], st[:, :], wupv[kc, :, hf * DFF:(hf + 1) * DFF])
                        cx.op("dve" if n % 2 == 0 else "pool", [stb], [wupk[kc]], lambda e: e.tensor_copy(out=wup[:, kc, hf * DFF:(hf + 1) * DFF], in_=st[:, :]))
                        n += 1
                wdnv = self.inp["w_down"].rearrange("(f p) n -> p f n", p=128)
                for q in range(11):
                    st, stb = wst[n % 2]
                    sv = st[:, 0:2 * D].rearrange("p (f n) -> p f n", n=D)
                    cx.dma([], [stb], sv, wdnv[:, 2 * q:2 * q + 2, :])
                    cx.op("dve" if n % 2 == 0 else "pool", [stb], [wdnb], lambda e: e.tensor_copy(out=wdn[:, 2 * q:2 * q + 2, :], in_=sv))
                    n += 1
                cx.barrier()
        g2, g2b = self.bc_load(es, "f_ln2g", self.inp["ln2_g"])
        b2, b2b = self.bc_load(es, "f_ln2b", self.inp["ln2_b"])
        cw, cwb = self.sb(es, "f_cw", [128, 3, NFC], F32)
        cb, cbb = self.sb(es, "f_cb", [128, NFC], F32)
        cx.dma([], [cwb], cw[:, :, :], self.inp["conv_w"].rearrange("k (f p) -> p k f", p=128), allow_slow_non_contiguous=True)
        cx.dma([], [cbb], cb[:, :], self.inp["conv_b"].rearrange("o (f p) -> p (o f)", p=128), allow_slow_non_contiguous=True)
        halo, halob = self.sb(es, "f_halo", [128, 1], F32)
        cx.dma([], [halob], halo[:, :], self.inp["halo"][:, :])
        Hs, _ = self.sb(es, "f_Hs", [128, NFC, 2], F32)
        Hk = [cx.buf() for f in range(NFC)]
        cx.op("pool", [], Hk, lambda e: e.memset(Hs[:], 0.0))
        NG = 2
        W = 128 * NG
        x1t = [self.sb(es, "f_x1%d" % i, [128, NG, D], F32) for i in range(2)]
        xn = [self.sb(es, "f_xn%d" % i, [128, D], BF16) for i in range(2)]
        uT = [self.sb(es, "f_uT%d" % i, [128, 8, W], BF16) for i in range(2)]
        uTk = [[cx.buf() for k in range(8)] for i in range(2)]
        Gf = [self.sb(es, "f_G%d" % i, [128, W + 2], F32) for i in range(3)]
        cv = [self.sb(es, "f_cv%d" % i, [128, 2, W], F32) for i in range(2)]
        gl = [self.sb(es, "f_gl%d" % i, [128, W], F32) for i in range(2)]
        hT, _ = self.sb(es, "f_hT", [128, NFC, W], BF16)
        hTk = [cx.buf() for f in range(NFC)]
        tt = [self.sb(es, "f_t%d" % i, [128, D], F32) for i in range(2)]
        lns = self.ln_stats(es, "lnf")
        lns2 = self.ln_stats(es, "lng")
        tp, _ = self.ps(es, "f_tp", [128, 8, 128], BF16)
        tpk1 = cx.buf()
        tpk1.psum = True
        tpk = [tpk1] * 8
        GV = [self.ps(es, "f_GV%d" % i, [128, 2, W], F32) for i in range(3)]
        Y = [self.ps(es, "f_Y%d" % i, [128, 512], F32) for i in range(4)]
        lcnt = [0]

        def stage_a(gi, cl):
            i = gi % 2
            x_, xb_ = x1t[i]
            for j, c in enumerate(cl):
                cx.dma([self.x1b], [xb_], x_[:, j, :], self.x1d[c * 128:(c + 1) * 128, :])
            for j, c in enumerate(cl):
                k = lcnt[0] % 2
                lcnt[0] += 1
                mv, mvb = self.emit_ln_stats(lns[k], x_[:, j, :], xb_)
                xn_, xnb_ = xn[k]
                cx.op("act", [xb_, mvb], [xnb_], lambda e: e.activation(out=xn_[:, :], in_=x_[:, j, :], func=AF.Identity, bias=mv[:, 3:4], scale=mv[:, 2:3]))
                self.emit_modT(xn_, xnb_, tp, tpk, uT[i][0][:, :, j * 128:(j + 1) * 128], uTk[i], 32, 24)

        def up_mm(gi, f, w, P_, Pb_, both):
            i = gi % 2
            uT_ = uT[i][0]
            for kc in range(8):
                cx.op("pe", [uTk[i][kc], wupk[kc]], [Pb_], lambda e: e.matmul(P_[:, 0, 0:w], lhsT=wup[:, kc, f * 128:(f + 1) * 128], rhs=uT_[:, kc, 0:w], start=(kc == 0), stop=(kc == 7)), sig=(kc == 7 and not both))
            if both:
                for kc in range(8):
                    cx.op("pe", [uTk[i][kc], wupk[kc]], [Pb_], lambda e: e.matmul(P_[:, 1, 0:w], lhsT=wup[:, kc, DFF + f * 128:DFF + (f + 1) * 128], rhs=uT_[:, kc, 0:w], start=(kc == 0), stop=(kc == 7)), sig=(kc == 7))

        stage_a(0, [0])
        if groups:
            stage_a(1, groups[0])
        cnt = 0
        for f in range(NFC):
            P_, Pb_ = GV[cnt % 3]
            G_, Gb_ = Gf[cnt % 3]
            cnt += 1
            up_mm(0, f, 128, P_, Pb_, False)
            cx.op("act", [Pb_], [Gb_], lambda e: e.copy(out=G_[:, 2:130], in_=P_[:, 0, 0:128]))
            cx.op("dve", [Gb_, halob], [Hk[f]], lambda e: e.tensor_scalar(out=Hs[:, f, :], in0=G_[:, 128:130], scalar1=halo[:, 0:1], scalar2=None, op0=ALU.mult))
        ecnt = 0
        prev = None

        def down_mm(pg, f):
            (pcl, pi) = pg
            for j, c in enumerate(pcl):
                for half in range(2):
                    Y_, Yb_ = Y[j * 2 + half]
                    cx.op("pe", [hTk[f], wdnb], [Yb_], lambda e: e.matmul(Y_[:, :], lhsT=hT[:, f, j * 128:(j + 1) * 128], rhs=wdn[:, f, half * 512:(half + 1) * 512], start=(f == 0), stop=(f == NFC - 1)), sig=(f == NFC - 1))

        def down_tail(pg, gidx):
            (pcl, pi) = pg
            x_, xb_ = x1t[pi]
            for j, c in enumerate(pcl):
                t_, tb_ = tt[j % 2]
                for half in range(2):
                    Y_, Yb_ = Y[j * 2 + half]
                    cx.op("dve", [Yb_, self.gbcb], [tb_], lambda e: e.tensor_tensor(out=t_[:, half * 512:(half + 1) * 512], in0=Y_[:, :], in1=self.gbc[:, 1, half * 512:(half + 1) * 512], op=ALU.mult))
                cx.op("dve", [xb_, tb_], [tb_], lambda e: e.scalar_tensor_tensor(out=t_[:, :], in0=x_[:, j, :], scalar=ALPHA, in1=t_[:, :], op0=ALU.mult, op1=ALU.add))
                mv2, mv2b = self.emit_ln_stats(lns2[j % 2], t_, tb_)
                cx.op("act", [tb_, mv2b], [tb_], lambda e: e.activation(out=t_[:, :], in_=t_[:, :], func=AF.Identity, bias=mv2[:, 3:4], scale=mv2[:, 2:3]))
                cx.op("pool", [tb_, g2b], [tb_], lambda e: e.tensor_tensor(out=t_[:, :], in0=t_[:, :], in1=g2[:, :], op=ALU.mult))
                cx.op("pool", [tb_, b2b], [tb_], lambda e: e.tensor_tensor(out=t_[:, :], in0=t_[:, :], in1=b2[:, :], op=ALU.add))
                cx.dma([tb_], [self.outbuf], self.out[(c - 1) * 128:c * 128, :], t_[:, :])

        for gi0, cl in enumerate(groups):
            gi = gi0 + 1
            i = gi % 2
            w = 128 * len(cl)
            pend = None

            def elem_a(pd):
                nonlocal ecnt
                (f, P_, Pb_, G_, Gb_) = pd
                cv_, cvb_ = cv[ecnt % 2]
                gl_, glb_ = gl[ecnt % 2]
                ecnt += 1
                cx.op("act", [Pb_], [Gb_], lambda e: e.copy(out=G_[:, 2:2 + w], in_=P_[:, 0, 0:w]))
                cx.op("act", [Hk[f]], [Gb_], lambda e: e.copy(out=G_[:, 0:2], in_=Hs[:, f, :]))
                cx.op("dve", [Gb_, cwb, cbb], [cvb_], lambda e: e.tensor_scalar(out=cv_[:, 0, 0:w], in0=G_[:, 2:2 + w], scalar1=cw[:, 2, f:f + 1], scalar2=cb[:, f:f + 1], op0=ALU.mult, op1=ALU.add))
                cx.op("dve", [Gb_, cwb, cvb_], [cvb_], lambda e: e.scalar_tensor_tensor(out=cv_[:, 1, 0:w], in0=G_[:, 1:1 + w], scalar=cw[:, 1, f:f + 1], in1=cv_[:, 0, 0:w], op0=ALU.mult, op1=ALU.add))
                cx.op("dve", [Gb_, cwb, cvb_], [cvb_], lambda e: e.scalar_tensor_tensor(out=cv_[:, 0, 0:w], in0=G_[:, 0:w], scalar=cw[:, 0, f:f + 1], in1=cv_[:, 1, 0:w], op0=ALU.mult, op1=ALU.add))
                cx.op("act", [Gb_], [Hk[f]], lambda e: e.copy(out=Hs[:, f, :], in_=G_[:, w:w + 2]))
                cx.op("act", [cvb_], [glb_], lambda e: e.activation(out=gl_[:, 0:w], in_=cv_[:, 0, 0:w], func=AF.Gelu_apprx_tanh))
                return (f, P_, Pb_, gl_, glb_)

            def elem_b(pd):
                (f, P_, Pb_, gl_, glb_) = pd
                cx.op("dve", [glb_, Pb_], [hTk[f]], lambda e: e.tensor_tensor(out=hT[:, f, 0:w], in0=gl_[:, 0:w], in1=P_[:, 1, 0:w], op=ALU.mult))

            pend_b = None
            for f in range(NFC):
                P_, Pb_ = GV[cnt % 3]
                G_, Gb_ = Gf[cnt % 3]
                cnt += 1
                up_mm(gi, f, w, P_, Pb_, True)
                if prev is not None:
                    down_mm(prev, f)
                nb = elem_a(pend) if pend is not None else None
                if pend_b is not None:
                    elem_b(pend_b)
                pend_b = nb
                pend = (f, P_, Pb_, G_, Gb_)
            if prev is not None:
                down_tail(prev, gi0 - 1)
            nb = elem_a(pend)
            if pend_b is not None:
                elem_b(pend_b)
            elem_b(nb)
            if gi0 + 1 < len(groups):
                stage_a(gi + 1, groups[gi0 + 1])
            prev = (cl, i)
        if prev is not None:
            for f in range(NFC):
                down_mm(prev, f)
            down_tail(prev, len(groups) - 1)
        cx.barrier()


K.phase_post2 = _post2
```

```python
import contextlib
import os
SKIP = set(os.environ.get('KSKIP', '').split(','))
STAGE = int(os.environ.get('KSTAGE', '99'))
import numpy as np
import ml_dtypes
import concourse.bass as bass
import concourse.mybir as mybir
from concourse.bass_utils import run_bass_kernel_spmd

F32 = mybir.dt.float32
BF16 = mybir.dt.bfloat16
AF = mybir.ActivationFunctionType
ALU = mybir.AluOpType
AX = mybir.AxisListType

D = 1024
S = 8192
NT = 64
Q0 = 31
NQ = 33
QTOK = NQ * 128
HD = 64
INW = 2840
DFF = 2816
NFC = 22
ALPHA = 2.0 ** 0.25
EPS = 1e-5
MASKV = -30000.0
VW = 66
FB_QA, FB_KA, FB_QB, FB_KC, FB_VC, FB_KSL, FB_KW = 0, 4, 8, 12, 13, 14, 15
NFB = 16


def bcast(ap, axis, n):
    dims = [list(d) for d in ap.ap]
    dims.insert(axis, [0, n])
    return bass.AP(ap.tensor, ap.offset, dims)


class Buf:
    __slots__ = ("name", "w", "r", "psum")

    def __init__(self, name, psum=False):
        self.name = name
        self.w = {}
        self.r = {}
        self.psum = psum


class Ctx:
    def __init__(self, nc, n_dma_sems=40):
        self.nc = nc
        self.es = contextlib.ExitStack()
        self.eng = {"pe": nc.tensor, "dve": nc.vector, "act": nc.scalar, "pool": nc.gpsimd, "sp": nc.sync}
        self.sem = {}
        self.cnt = {}
        self.semobj = {}
        for e in self.eng:
            s = self.es.enter_context(nc.semaphore("sem_" + e))
            self.sem[e] = s
            self.cnt[e] = 0
        self.dma_sems = [self.es.enter_context(nc.semaphore("dsem%d" % i)) for i in range(n_dma_sems)]
        self.dma_cnt = [0] * n_dma_sems
        self.dma_rr = 0
        self.waited = {}
        self.nbuf = 0
        self.ninst = 0

    def buf(self, name=None):
        self.nbuf += 1
        return Buf(name or ("b%d" % self.nbuf))

    def _key(self, s):
        return id(s)

    def _wait(self, e, deps):
        for k, (s, v) in deps.items():
            if e == "pe" and s is self.sem["pe"]:
                continue
            if self.waited.get((e, k), 0) < v:
                self.eng[e].wait_ge(s, v)
                self.waited[(e, k)] = v

    def _deps(self, reads, writes):
        deps = {}

        def add(dd):
            for k, (s, v) in dd.items():
                if k not in deps or deps[k][1] < v:
                    deps[k] = (s, v)
        for b in reads:
            add(b.w)
        for b in writes:
            add(b.w)
            add(b.r)
        return deps

    def _commit(self, ev, reads, writes):
        k = self._key(ev[0])
        for b in writes:
            b.w = {k: ev}
            b.r = {}
        for b in reads:
            if k not in b.r or b.r[k][1] < ev[1]:
                b.r[k] = ev

    def op(self, e, reads, writes, fn, sig=True):
        if e != "pe":
            px = [b for b in reads if b.psum]
            if px:
                reads = [b for b in reads if not b.psum]
                writes = list(writes) + px
        deps = self._deps(reads, writes)
        self._wait(e, deps)
        inst = fn(self.eng[e])
        self.ninst += 1
        if sig:
            self.cnt[e] += 1
            inst.then_inc(self.sem[e], 1)
            ev = (self.sem[e], self.cnt[e])
        else:
            ev = (self.sem[e], self.cnt[e] + 1)
        self._commit(ev, reads, writes)
        return inst

    def dma(self, reads, writes, out, in_, q="sp", **kw):
        deps = self._deps(reads, writes)
        i = self.dma_rr
        self.dma_rr = (self.dma_rr + 1) % len(self.dma_sems)
        s = self.dma_sems[i]
        if self.dma_cnt[i] > 0:
            deps[self._key(s)] = (s, self.dma_cnt[i])
        self._wait(q, deps)
        inst = self.eng[q].dma_start(out=out, in_=in_, **kw)
        self.ninst += 1
        self.dma_cnt[i] += 16
        inst.then_inc(s, 16)
        ev = (s, self.dma_cnt[i])
        self._commit(ev, reads, writes)
        return ev

    def barrier(self):
        for e in self.eng:
            deps = {}
            for e2 in self.eng:
                if e2 != e and self.cnt[e2] > 0:
                    deps[self._key(self.sem[e2])] = (self.sem[e2], self.cnt[e2])
            for i, sm in enumerate(self.dma_sems):
                if self.dma_cnt[i] > 0:
                    deps[self._key(sm)] = (sm, self.dma_cnt[i])
            self._wait(e, deps)

    def finish(self, bufs):
        deps = self._deps(bufs, [])
        self._wait("sp", deps)

    def close(self):
        self.es.close()


def host_tables(half):
    off = 0 if half == 1 else 32
    t = {}
    slots = np.arange(NT * 128)
    gpos = slots - off * 128
    real = gpos >= 0
    inv = 1.0 / (500000.0 ** (np.arange(0, 16, 2, dtype=np.float32) / 16.0))
    ang = np.where(real, gpos, 0).astype(np.float32)[:, None] * inv[None, :]
    cs = np.concatenate([np.cos(ang), np.cos(ang), -np.sin(ang), np.sin(ang)], -1).astype(np.float32)
    t["ropetab"] = np.ascontiguousarray(cs.reshape(NT, 128, 32).transpose(1, 0, 2))
    kb = np.where(real, 0.0, MASKV).astype(np.float32)
    t["keybias"] = np.ascontiguousarray(kb.reshape(NT, 128).T)
    kl = np.arange(128)[:, None]
    ql = np.arange(128)[None, :]
    m = np.zeros((3, 128, 128), np.float32)
    m[0] = np.where(kl <= ql, 0.0, MASKV)
    m[1] = np.where(kl >= ql, 0.0, MASKV)
    m[2] = np.where(kl > ql, 0.0, MASKV)
    t["masks"] = np.ascontiguousarray(m.transpose(1, 0, 2)).astype(ml_dtypes.bfloat16)
    t["ident"] = np.eye(128, dtype=np.float32).astype(ml_dtypes.bfloat16)
    E = np.zeros((128, NT * 128), np.float32)
    E[(np.arange(NT * 128) // 64), np.arange(NT * 128)] = 1.0
    t["eall"] = E.astype(ml_dtypes.bfloat16)
    cst = np.arange(512) * 16
    sst = np.arange(128) * 64
    ov = np.clip(np.minimum(cst[:, None] + 32, sst[None, :] + 64) - np.maximum(cst[:, None], sst[None, :]), 0, None) / 32.0
    ov[511] = 0.0
    t["overlap"] = np.ascontiguousarray(ov.reshape(4, 128, 128).transpose(1, 0, 2)).astype(ml_dtypes.bfloat16)
    selb = np.zeros((NQ, 128, 128), np.float32)
    cmpm = np.zeros((NQ, 512, 128), np.float32)
    for c in range(NQ):
        sl = Q0 + c
        tq = (sl - off) * 128 + np.arange(128)
        blk = np.arange(128) - off * 2
        cur = tq // 64
        sb = np.zeros((128, 128), np.float32)
        valid = (blk[None, :] >= 0) & (blk[None, :] * 64 <= tq[:, None])
        sb[:] = np.where(valid, 0.0, -1e9 - 1e4 * np.arange(128)[None, :])
        for j, bid in enumerate([cur, cur - 1, np.zeros_like(cur)]):
            hit = (blk[None, :] == bid[:, None]) & (blk[None, :] >= 0)
            sb = np.where(hit, (1.0 + j) * 1e9, sb)
        selb[c] = sb
        ci = np.arange(512)
        cstart_g = ci * 16 - off * 128
        cvalid = (cstart_g[:, None] >= 0) & (cstart_g[:, None] + 31 <= tq[None, :]) & (ci[:, None] < 511)
        cmpm[c] = np.where(cvalid, 1.0, 0.0)
    t["selb"] = np.ascontiguousarray(selb.transpose(1, 0, 2))
    t["cmpm"] = np.ascontiguousarray(cmpm.reshape(NQ, 4, 128, 128).transpose(2, 0, 1, 3)).astype(ml_dtypes.bfloat16)
    t["halo"] = np.full((128, 1), 1.0 if half == 1 else 0.0, np.float32)
    t["realb"] = np.ascontiguousarray(np.broadcast_to(((np.arange(128) - off * 2) >= 0).astype(np.float32)[None, :], (128, 128)))
    return t


TABLE_SPECS = [
    ("ropetab", [128, NT, 32], F32), ("keybias", [128, NT], F32), ("masks", [128, 3, 128], BF16),
    ("ident", [128, 128], BF16), ("eall", [128, NT * 128], BF16), ("overlap", [128, 4, 128], BF16),
    ("selb", [128, NQ, 128], F32), ("cmpm", [128, NQ, 4, 128], BF16), ("halo", [128, 1], F32), ("realb", [128, 128], F32),
]

WEIGHT_SPECS = [
    ("c", [1, D]), ("w_ada", [D, 6 * D]), ("b_ada", [1, 6 * D]), ("w_in", [D, INW]), ("pe_cmp", [32, 64]),
    ("w_ck1", [2048, 256]), ("w_ck2", [256, 64]), ("w_cv1", [2048, 256]), ("w_cv2", [256, 64]),
    ("w_o", [D, D]), ("ln1_g", [1, D]), ("ln1_b", [1, D]), ("w_up", [D, 2 * DFF]), ("conv_w", [3, DFF]),
    ("conv_b", [1, DFF]), ("w_down", [DFF, D]), ("ln2_g", [1, D]), ("ln2_b", [1, D]),
]


class K:
    def __init__(self, debug=(), scratch_in=()):
        self.debug = set(debug)
        self.scratch_in = set(scratch_in)
        nc = bass.Bass("TRN2", target_bir_lowering=False)
        self.nc = nc
        self.cx = Ctx(nc)
        self.inp = {}
        self.inp["xk"] = nc.dram_tensor("xk", [S, D], F32, kind="ExternalInput").ap()
        for name, shape in WEIGHT_SPECS:
            self.inp[name] = nc.dram_tensor(name, shape, F32, kind="ExternalInput").ap()
        for name, shape, dt in TABLE_SPECS:
            self.inp[name] = nc.dram_tensor(name, shape, dt, kind="ExternalInput").ap()
        self.out = nc.dram_tensor("out", [32 * 128, D], F32, kind="ExternalOutput").ap()
        self.outbuf = self.cx.buf("out")
        def kd(n):
            if n in self.scratch_in:
                return "ExternalInput"
            return "ExternalOutput" if n in self.debug else "Internal"
        self.ft = nc.dram_tensor("ft", [NFB, 128, S], BF16, kind=kd("ft")).ap()
        self.ftb = self.cx.buf("ft")
        self.va = nc.dram_tensor("va", [S, 8, VW], BF16, kind=kd("va")).ap()
        self.vab = self.cx.buf("va")
        self.vsl = nc.dram_tensor("vsl", [S, 2, VW], BF16, kind=kd("vsl")).ap()
        self.vslb = self.cx.buf("vsl")
        self.vwn = nc.dram_tensor("vwn", [S, 2, VW], BF16, kind=kd("vwn")).ap()
        self.vwnb = self.cx.buf("vwn")
        self.ot = nc.dram_tensor("ot", [16, 64, QTOK], BF16, kind=kd("ot")).ap()
        self.otb = self.cx.buf("ot")
        self.x1d = nc.dram_tensor("x1d", [QTOK, D], F32, kind=kd("x1d")).ap()
        self.wupd = nc.dram_tensor("wupd", [8, 128, 2 * DFF], BF16, kind="Internal").ap()
        self.wupdb = self.cx.buf("wupd")
        self.wdnd = nc.dram_tensor("wdnd", [128, NFC, D], BF16, kind="Internal").ap()
        self.wdndb = self.cx.buf("wdnd")
        self.wcast_done = False
        self.x1b = self.cx.buf("x1d")
        self.dbg = {}

    def dbg_out(self, name, shape, dt):
        ap = self.nc.dram_tensor("dbg_" + name, shape, dt, kind="ExternalOutput").ap()
        self.dbg[name] = (ap, self.cx.buf("dbg_" + name))
        return self.dbg[name]

    def sb(self, es, name, shape, dt):
        t = es.enter_context(self.nc.sbuf_tensor("s_" + name, shape, dt))
        return t, self.cx.buf(name)

    def ps(self, es, name, shape, dt):
        t = es.enter_context(self.nc.psum_tensor("p_" + name, shape, dt))
        b = self.cx.buf(name)
        b.psum = True
        return t, b

    def consts(self, es):
        cx = self.cx
        self.ident, self.identb = self.sb(es, "ident", [128, 128], BF16)
        cx.dma([], [self.identb], self.ident[:], self.inp["ident"][:, :])
        self.masks, self.masksb = self.sb(es, "masks", [128, 3, 128], BF16)
        cx.dma([], [self.masksb], self.masks[:], self.inp["masks"][:, :, :])
        self.keybias, self.keybiasb = self.sb(es, "keybias", [128, NT], F32)
        cx.dma([], [self.keybiasb], self.keybias[:], self.inp["keybias"][:, :])
        self.modT, self.modTb = self.sb(es, "modT", [128, 48], F32)
        self.gbc, self.gbcb = self.sb(es, "gbc", [128, 2, D], F32)
        self.ones, self.onesb = self.sb(es, "ones", [128, 128], F32)
        cx.op("pool", [], [self.onesb], lambda e: e.memset(self.ones[:], 1.0))
        self.identf, self.identfb = self.sb(es, "identf", [128, 128], F32)
        cx.op("dve", [self.identb], [self.identfb], lambda e: e.tensor_copy(out=self.identf[:], in_=self.ident[:]))
        self.onesbf, self.onesbfb = self.sb(es, "onesbf", [128, 128], BF16)
        cx.op("pool", [], [self.onesbfb], lambda e: e.memset(self.onesbf[:], 1.0))

    def phase_mod(self):
        cx, nc = self.cx, self.nc
        with contextlib.ExitStack() as es:
            cT, cTb = self.sb(es, "cT", [128, 8], F32)
            cx.dma([], [cTb], cT[:], self.inp["c"].rearrange("o (k p) -> p (o k)", p=128), allow_slow_non_contiguous=True)
            bT, bTb = self.sb(es, "bT", [128, 48], F32)
            cx.dma([], [bTb], bT[:], self.inp["b_ada"].rearrange("o (j p) -> p (o j)", p=128), allow_slow_non_contiguous=True)
            brow, browb = self.sb(es, "brow", [1, 2, D], F32)
            cx.dma([], [browb], brow[:, 0, :], self.inp["b_ada"][:, 2 * D:3 * D])
            cx.dma([], [browb], brow[:, 1, :], self.inp["b_ada"][:, 5 * D:6 * D])
            sc, scb = self.sb(es, "silc", [128, 8], BF16)
            cx.op("act", [cTb], [scb], lambda e: e.activation(out=sc[:], in_=cT[:], func=AF.Silu))
            wst = [self.sb(es, "wada_st%d" % i, [128, 6 * D], F32) for i in range(2)]
            wbf = [self.sb(es, "wada_bf%d" % i, [128, 6 * D], BF16) for i in range(2)]
            pp, ppb = self.ps(es, "modpart", [128, 48, 8], F32)
            prow = [self.ps(es, "modrow%d" % i, [1, 512], F32) for i in range(4)]
            wada = self.inp["w_ada"].rearrange("(k p) n -> k p n", p=128)
            for kc in range(8):
                st, stb = wst[kc % 2]
                wb, wbb = wbf[kc % 2]
                cx.dma([], [stb], st[:, 0:3 * D], wada[kc, :, 0:3 * D])
                cx.dma([], [stb], st[:, 3 * D:6 * D], wada[kc, :, 3 * D:6 * D])
                cx.op("dve", [stb], [wbb], lambda e: e.tensor_copy(out=wb[:, 0:5 * 512], in_=st[:, 0:5 * 512]))
                cx.op("act", [stb], [wbb], lambda e: e.copy(out=wb[:, 5 * 512:10 * 512], in_=st[:, 5 * 512:10 * 512]))
                cx.op("pool", [stb], [wbb], lambda e: e.tensor_copy(out=wb[:, 10 * 512:6 * D], in_=st[:, 10 * 512:6 * D]))
                for j in range(48):
                    cx.op("pe", [wbb, scb], [ppb], lambda e: e.matmul(pp[:, j, kc:kc + 1], lhsT=wb[:, j * 128:(j + 1) * 128],
                                                                       rhs=sc[:, kc:kc + 1], start=True, stop=True), sig=(j == 47))
                for r in range(4):
                    col = (2 * D if r < 2 else 5 * D) + (r % 2) * 512
                    pr, prb = prow[r]
                    cx.op("pe", [wbb, scb], [prb], lambda e: e.matmul(pr[:, :], lhsT=sc[:, kc:kc + 1], rhs=wb[:, col:col + 512],
                                                                       start=(kc == 0), stop=(kc == 7)))
            tmp, tmpb = self.sb(es, "modtmp", [128, 48], F32)
            cx.op("dve", [ppb], [tmpb], lambda e: e.tensor_reduce(out=tmp[:], in_=pp[:], axis=AX.X, op=ALU.add))
            cx.op("dve", [tmpb, bTb], [self.modTb], lambda e: e.tensor_tensor(out=self.modT[:], in0=tmp[:], in1=bT[:], op=ALU.add))
            cx.op("dve", [self.modTb], [self.modTb], lambda e: e.tensor_scalar_add(out=self.modT[:, 8:16], in0=self.modT[:, 8:16], scalar1=1.0))
            cx.op("dve", [self.modTb], [self.modTb], lambda e: e.tensor_scalar_add(out=self.modT[:, 32:40], in0=self.modT[:, 32:40], scalar1=1.0))
            grow, growb = self.sb(es, "grow", [1, 2, D], F32)
            for r in range(4):
                pr, prb = prow[r]
                cx.op("dve", [prb, browb], [growb], lambda e: e.tensor_tensor(out=grow[:, r // 2, (r % 2) * 512:(r % 2) * 512 + 512],
                                                                               in0=pr[:, :], in1=brow[:, r // 2, (r % 2) * 512:(r % 2) * 512 + 512], op=ALU.add))
            pb, pbb = self.ps(es, "gbps", [128, 512], F32)
            for r in range(4):
                c0 = (r % 2) * 512
                cx.op("pe", [growb, self.onesb], [pbb], lambda e: e.matmul(pb[:, :], lhsT=self.ones[0:1, :], rhs=grow[:, r // 2, c0:c0 + 512], start=True, stop=True))
                cx.op("act", [pbb], [self.gbcb], lambda e: e.copy(out=self.gbc[:, r // 2, c0:c0 + 512], in_=pb[:, :]))
            cx.barrier()
            if "mod" in self.debug:
                ap, b = self.dbg_out("modT", [128, 48], F32)
                cx.dma([self.modTb], [b], ap[:, :], self.modT[:])
                ap, b = self.dbg_out("gbc", [128, 2, D], F32)
                cx.dma([self.gbcb], [b], ap[:, :, :], self.gbc[:])

    def ln_stats(self, es, tag, nbufs=2):
        sets = []
        for i in range(nbufs):
            st, stb = self.sb(es, "%s_st%d" % (tag, i), [128, 12], F32)
            mv, mvb = self.sb(es, "%s_mv%d" % (tag, i), [128, 4], F32)
            sets.append((st, stb, mv, mvb))
        return sets

    def emit_ln_stats(self, lnset, x, xb):
        cx = self.cx
        st, stb, mv, mvb = lnset
        cx.op("dve", [xb], [stb], lambda e: e.bn_stats(out=st[:, 0:6], in_=x[:, 0:512]))
        cx.op("dve", [xb, stb], [stb], lambda e: e.bn_stats(out=st[:, 6:12], in_=x[:, 512:1024]))
        cx.op("dve", [stb], [mvb], lambda e: e.bn_aggr(out=mv[:, 0:2], in_=st[:, :]))
        cx.op("dve", [mvb], [mvb], lambda e: e.tensor_scalar_add(out=mv[:, 2:3], in0=mv[:, 1:2], scalar1=EPS))
        cx.op("act", [mvb], [mvb], lambda e: e.activation(out=mv[:, 2:3], in_=mv[:, 2:3], func=AF.Sqrt))
        cx.op("dve", [mvb], [mvb], lambda e: e.reciprocal(out=mv[:, 2:3], in_=mv[:, 2:3]))
        cx.op("dve", [mvb], [mvb], lambda e: e.tensor_scalar(out=mv[:, 3:4], in0=mv[:, 0:1], scalar1=mv[:, 2:3], scalar2=-1.0, op0=ALU.mult, op1=ALU.mult))
        return mv, mvb

    def emit_modT(self, xn, xnb, tp, tpb, uT, uTb, sc_col, sh_col, n=128):
        cx = self.cx
        for kc in range(8):
            cx.op("pe", [xnb, self.identb], [tpb[kc]], lambda e: e.transpose(out=tp[:, kc, :], in_=xn[:, kc * 128:(kc + 1) * 128], identity=self.ident[:]), sig=(kc == 7))
        for kc in range(8):
            use_act = (kc % 2 == 0)
            if "allact" in SKIP:
                use_act = True
            if "alldve" in SKIP:
                use_act = False
            if "plaincopy" in SKIP:
                cx.op("dve", [tpb[kc]], [uTb[kc]], lambda e: e.tensor_copy(out=uT[:, kc, :], in_=tp[:, kc, :]))
            elif use_act:
                cx.op("act", [tpb[kc], self.modTb], [uTb[kc]], lambda e: e.activation(out=uT[:, kc, :], in_=tp[:, kc, :], func=AF.Identity,
                                                                                      bias=self.modT[:, sh_col + kc:sh_col + kc + 1], scale=self.modT[:, sc_col + kc:sc_col + kc + 1]))
            else:
                cx.op("dve", [tpb[kc], self.modTb], [uTb[kc]], lambda e: e.tensor_scalar(out=uT[:, kc, :], in0=tp[:, kc, :], scalar1=self.modT[:, sc_col + kc:sc_col + kc + 1],
                                                                                         scalar2=self.modT[:, sh_col + kc:sh_col + kc + 1], op0=ALU.mult, op1=ALU.add))

    def phase_proj(self, es_persist, tiles=None):
        cx, nc = self.cx, self.nc
        tiles = list(range(NT)) if tiles is None else tiles
        self.gates, self.gatesb = self.sb(es_persist, "gates", [128, NQ, 24], F32)
        with contextlib.ExitStack() as es:
            wbf, wbfb = self.sb(es, "win_bf", [128, 8, INW], BF16)
            wbfk = [cx.buf("win_bf%d" % k) for k in range(8)]
            wst = [self.sb(es, "win_st%d" % i, [128, INW], F32) for i in range(2)]
            win = self.inp["w_in"].rearrange("(k p) n -> k p n", p=128)
            for kc in range(8):
                st, stb = wst[kc % 2]
                cx.dma([], [stb], st[:, :], win[kc, :, :])
                cx.op("dve", [stb], [wbfk[kc]], lambda e: e.tensor_copy(out=wbf[:, kc, 0:1420], in_=st[:, 0:1420]))
                cx.op("pool", [stb], [wbfk[kc]], lambda e: e.tensor_copy(out=wbf[:, kc, 1420:INW], in_=st[:, 1420:INW]))
            rope, ropeb = self.sb(es, "ropetab", [128, NT, 32], F32)
            cx.dma([], [ropeb], rope[:], self.inp["ropetab"][:, :, :])
            xt = [self.sb(es, "xt%d" % i, [128, D], F32) for i in range(2)]
            xn = [self.sb(es, "xn%d" % i, [128, D], BF16) for i in range(2)]
            uT = [self.sb(es, "uT%d" % i, [128, 8, 128], BF16) for i in range(2)]
            uTk = [[cx.buf() for k in range(8)] for i in range(2)]
            lns = self.ln_stats(es, "lnp")
            tp, _ = self.ps(es, "tp", [128, 8, 128], BF16)
            tpk1 = cx.buf()
            tpk1.psum = True
            tpk = [tpk1] * 8
            pg = [self.ps(es, "pg%d" % i, [128, 512], F32) for i in range(3)]
            tpo = [self.ps(es, "tpo%d" % i, [128, 8, 128], BF16) for i in range(2)]
            tokq = [self.sb(es, "tokq%d" % i, [128, NFB * 128], BF16) for i in range(2)]
            rsc = [self.sb(es, "rsc%d" % i, [128, 2, 8, 16], F32) for i in range(2)]
            ftst = [self.sb(es, "ftst%d" % i, [128, NFB, 512], BF16) for i in range(2)]
            vst = [self.sb(es, "vst%d" % i, [128, 8, VW], BF16) for i in range(2)]
            vsst = [self.sb(es, "vsst%d" % i, [128, 2, VW], BF16) for i in range(2)]
            vwst = [self.sb(es, "vwst%d" % i, [128, 2, VW], BF16) for i in range(2)]
            for i in range(2):
                for (t_, b_) in (vst[i], vsst[i], vwst[i]):
                    cx.op("pool", [], [b_], lambda e: e.memset(t_[:], 1.0))
                cx.op("pool", [], [ftst[i][1]], lambda e: e.memset(ftst[i][0][:], 0.0))
            xk = self.inp["xk"].rearrange("(t p) d -> t p d", p=128)
            ngroup = 0
            cst = [self.sb(es, "wc_st%d" % i, [128, DFF], F32) for i in range(2)]
            cbf = [self.sb(es, "wc_bf%d" % i, [128, DFF], BF16) for i in range(2)]
            wupv_ = self.inp["w_up"].rearrange("(k p) n -> k p n", p=128)
            wdnv_ = self.inp["w_down"].rearrange("(f p) n -> p f n", p=128)
            jobs = [("up", kc, hf) for kc in range(8) for hf in range(2)] + [("dn", q, 0) for q in range(11)]
            jobn = [0]

            pend_store = [None]

            def flush_store():
                if pend_store[0] is not None:
                    (bfb, dbuf, dst, src) = pend_store[0]
                    cx.dma([bfb], [dbuf], dst, src)
                    pend_store[0] = None

            def cast_job():
                flush_store()
                if jobn[0] >= len(jobs) or len(tiles) < 40:
                    return
                kind, a_, b_ = jobs[jobn[0]]
                st, stb = cst[jobn[0] % 2]
                bf_, bfb = cbf[jobn[0] % 2]
                jobn[0] += 1
                if kind == "up":
                    cx.dma([], [stb], st[:, :], wupv_[a_, :, b_ * DFF:(b_ + 1) * DFF])
                    cx.op("pool", [stb], [bfb], lambda e: e.tensor_copy(out=bf_[:, :], in_=st[:, :]))
                    pend_store[0] = (bfb, self.wupdb, self.wupd[a_, :, b_ * DFF:(b_ + 1) * DFF], bf_[:, :])
                else:
                    sv = st[:, 0:2 * D].rearrange("p (f n) -> p f n", n=D)
                    bv = bf_[:, 0:2 * D].rearrange("p (f n) -> p f n", n=D)
                    cx.dma([], [stb], sv, wdnv_[:, 2 * a_:2 * a_ + 2, :])
                    cx.op("pool", [stb], [bfb], lambda e: e.tensor_copy(out=bv, in_=sv))
                    pend_store[0] = (bfb, self.wdndb, self.wdnd[:, 2 * a_:2 * a_ + 2, :], bv)
                if jobn[0] == len(jobs):
                    self.wcast_done = True

            def load_x(i, t):
                cx.dma([], [xt[i][1]], xt[i][0][:, :], xk[t, :, :])

            load_x(0, tiles[0])
            pgi = 0
            def stage_a(ti):
                i = ti % 2
                x_, xb_ = xt[i]
                if ti + 1 < len(tiles):
                    load_x(1 - i, tiles[ti + 1])
                mv, mvb = self.emit_ln_stats(lns[i], x_, xb_)
                xn_, xnb_ = xn[i]
                cx.op("act", [xb_, mvb], [xnb_], lambda e: e.activation(out=xn_[:, :], in_=x_[:, :], func=AF.Identity, bias=mv[:, 3:4], scale=mv[:, 2:3]))
                self.emit_modT(xn_, xnb_, tp, tpk, uT[i][0], uTk[i], 8, 0)

            stage_a(0)
            for ti, t in enumerate(tiles):
                i = ti % 2
                isq = t >= Q0
                if ti + 1 < len(tiles):
                    stage_a(ti + 1)
                if ti >= 2 and ti % 2 == 0:
                    cast_job()
                uT_, _ = uT[i]
                tq, tqb = tokq[i]
                rs, rsb = rsc[i]
                g4 = ti // 4
                fst, fstb = ftst[g4 % 2]
                sub = ti % 4
                cs32 = rope[:, t, :]

                def proj(c0, c1):
                    nonlocal pgi
                    p_, pb_ = pg[pgi % 3]
                    pgi += 1
                    for kc in range(8):
                        cx.op("pe", [uTk[i][kc], wbfk[kc]], [pb_], lambda e: e.matmul(p_[:, 0:c1 - c0], lhsT=uT_[:, kc, :], rhs=wbf[:, kc, c0:c1],
                                                                                   start=(kc == 0), stop=(kc == 7)), sig=(kc == 7))
                    return p_, pb_

                def rope_fix(p_, pb_, src_view, dst_view, shp):
                    a, b = shp
                    A = rs[:, 0, 0:a * b, :].rearrange("p (a b) d -> p a b d", a=a)
                    B = rs[:, 1, 0:a * b, :].rearrange("p (a b) d -> p a b d", a=a)

                    def tb(lo, hi):
                        return bcast(bcast(cs32[:, lo:hi], 1, b), 1, a)
                    cx.op("dve", [pb_, ropeb], [rsb], lambda e: e.tensor_tensor(out=A, in0=src_view, in1=tb(0, 16), op=ALU.mult))
                    cx.op("dve", [pb_, ropeb, rsb], [rsb], lambda e: e.tensor_tensor(out=B[:, :, :, 0:8], in0=src_view[:, :, :, 8:16], in1=tb(16, 24), op=ALU.mult))
                    cx.op("dve", [pb_, ropeb, rsb], [rsb], lambda e: e.tensor_tensor(out=B[:, :, :, 8:16], in0=src_view[:, :, :, 0:8], in1=tb(24, 32), op=ALU.mult))
                    cx.op("dve", [rsb, tqb], [tqb], lambda e: e.tensor_tensor(out=dst_view, in0=A, in1=B, op=ALU.add))

                def hv(ap2d, a, b, astride_cols):
                    base = ap2d
                    dims = [list(base.ap[0]), [astride_cols, a], [64, b], [1, 16]]
                    return bass.AP(base.tensor, base.offset, dims)

                for (c0, dst0, need) in ((0, FB_QA * 128, isq), (512, FB_KA * 128, True), (1536, FB_QB * 128, isq)):
                    if not need:
                        continue
                    p_, pb_ = proj(c0, c0 + 512)
                    cx.op("act", [pb_], [tqb], lambda e: e.copy(out=tq[:, dst0:dst0 + 512], in_=p_[:, :]))
                    if STAGE >= 4:
                        rope_fix(p_, pb_, hv(p_[:, 0:512], 1, 8, 0), hv(tq[:, dst0:dst0 + 512], 1, 8, 0), (1, 8))
                if STAGE < 5:
                    continue
                p_, pb_ = proj(1024, 1536)
                v_, vb_ = vst[i]
                cx.op("act", [pb_], [vb_], lambda e: e.copy(out=v_[:, :, 0:64], in_=p_[:, :].rearrange("p (h d) -> p h d", d=64)))
                if "va" not in SKIP:
                    cx.dma([vb_], [self.vab], self.va[t * 128:(t + 1) * 128, :, :], v_[:])
                p_, pb_ = proj(2048, 2560)
                d0 = FB_KC * 128
                cx.op("act", [pb_], [tqb], lambda e: e.copy(out=tq[:, d0:d0 + 384], in_=p_[:, 0:384]))
                rope_fix(p_, pb_, hv(p_[:, 0:384], 2, 2, 256), hv(tq[:, d0:d0 + 384], 2, 2, 256), (2, 2))
                v_, vb_ = vsst[i]
                cx.op("act", [pb_], [vb_], lambda e: e.copy(out=v_[:, :, 0:64], in_=p_[:, 384:512].rearrange("p (h d) -> p h d", d=64)))
                cx.dma([vb_], [self.vslb], self.vsl[t * 128:(t + 1) * 128, :, :], v_[:])
                p_, pb_ = proj(2560, INW)
                d0 = FB_KW * 128
                cx.op("act", [pb_], [tqb], lambda e: e.copy(out=tq[:, d0:d0 + 128], in_=p_[:, 0:128]))
                rope_fix(p_, pb_, hv(p_[:, 0:128], 1, 2, 0), hv(tq[:, d0:d0 + 128], 1, 2, 0), (1, 2))
                v_, vb_ = vwst[i]
                cx.op("act", [pb_], [vb_], lambda e: e.copy(out=v_[:, :, 0:64], in_=p_[:, 128:256].rearrange("p (h d) -> p h d", d=64)))
                cx.dma([vb_], [self.vwnb], self.vwn[t * 128:(t + 1) * 128, :, :], v_[:])
                if isq:
                    cx.op("act", [pb_], [self.gatesb], lambda e: e.activation(out=self.gates[:, t - Q0, :], in_=p_[:, 256:280], func=AF.Sigmoid))
                blocks = list(range(NFB)) if isq else [b for b in range(NFB) if not (FB_QA <= b < FB_KA or FB_QB <= b < FB_KC)]
                for j0 in range(0, len(blocks), 8):
                    bl = blocks[j0:j0 + 8]
                    to_, tob_ = tpo[(j0 // 8) % 2]
                    for jj, b in enumerate(bl):
                        cx.op("pe", [tqb, self.identb], [tob_], lambda e: e.transpose(out=to_[:, jj, :], in_=tq[:, b * 128:(b + 1) * 128], identity=self.ident[:]), sig=(jj == len(bl) - 1))
                    runs = []
                    for jj, b in enumerate(bl):
                        if runs and runs[-1][1] + runs[-1][2] == b:
                            runs[-1][2] += 1
                        else:
                            runs.append([jj, b, 1])
                    for ri, (jj, b, n) in enumerate(runs):
                        eng = "dve" if ri % 2 == 0 else "pool"
                        if eng == "pool":
                            eng = "dve"
                        cx.op(eng, [tob_], [fstb], lambda e: e.tensor_copy(out=fst[:, b:b + n, sub * 128:(sub + 1) * 128], in_=to_[:, jj:jj + n, :]))
                if sub == 3 or ti == len(tiles) - 1:
                    t0 = (t - sub) * 128
                    ntok = (sub + 1) * 128
                    if "ft" not in SKIP:
                        cx.dma([fstb], [self.ftb], self.ft[:, :, t0:t0 + ntok].rearrange("b p t -> p b t"), fst[:, :, 0:ntok])
            flush_store()
            cx.barrier()


def build(debug=(), phases=("mod", "proj", "cmp", "attnA", "attnB", "post1", "post2"), proj_tiles=None, a_hps=(0, 1, 2, 3), a_filter=None, b_chunks=None, p_chunks=None, scratch_in=()):
    k = K(debug=debug, scratch_in=scratch_in)
    cx = k.cx
    with contextlib.ExitStack() as es:
        k.consts(es)
        if "mod" in phases:
            k.phase_mod()
        with contextlib.ExitStack() as es_mid:
            if "proj" in phases:
                k.phase_proj(es_mid, tiles=proj_tiles)
            if "cmp" in phases:
                k.phase_compress(es_mid)
            if "attnA" in phases:
                k.phase_attnA(hps=a_hps, qfilter=a_filter)
            if "attnB" in phases:
                k.phase_attnB(chunks=b_chunks)
            cx.barrier()
        if "post1" in phases:
            k.phase_post1(chunks=p_chunks)
        if "post2" in phases:
            k.phase_post2(chunks=p_chunks)
        allb = [k.outbuf, k.ftb, k.vab, k.vslb, k.vwnb, k.otb, k.x1b] + [b for (_, b) in k.dbg.values()]
        cx.finish(allb)
    return k


def make_in_maps(inputs):
    x = np.asarray(inputs["x"], np.float32)
    maps = []
    tabs = {h: host_tables(h) for h in (0, 1)}
    for core in range(8):
        b, half = core // 2, core % 2
        m = {}
        if half == 1:
            m["xk"] = np.ascontiguousarray(x[b])
        else:
            m["xk"] = np.concatenate([np.zeros((4096, D), np.float32), x[b, 0:4096]], 0)
        for name, shape in WEIGHT_SPECS:
            a = np.asarray(inputs[name], np.float32)
            if name == "c":
                a = a[b:b + 1]
            else:
                a = a[0]
            m[name] = np.ascontiguousarray(a.reshape(shape))
        m.update(tabs[half])
        maps.append(m)
    return maps


def kernel(**inputs):
    k = build()
    maps = make_in_maps(inputs)
    res = run_bass_kernel_spmd(k.nc, maps, core_ids=list(range(8)))
    out = np.zeros((4, S, D), np.float32)
    for core in range(8):
        b, half = core // 2, core % 2
        out[b, half * 4096:(half + 1) * 4096] = res.results[core]["out"]
    return out


def _attnA(self, hps=(0, 1, 2, 3), qfilter=None):
    cx, nc = self.cx, self.nc
    QBASE = Q0 * 128
    pats = []
    for d, mlo, mhi in ((1, 31, 63), (4, 7, 15), (16, 1, 3)):
        for r in range(d):
            for m in range(mlo, mhi + 1):
                j0 = max(0, -(-(QBASE - r) // d) - 128 * m)
                if j0 >= 128:
                    continue
                pats.append((d, r, m, j0))
    if qfilter is not None:
        pats = [p for p in pats if qfilter(p)]
    vtiles = {1: (30, 34), 4: (6, 10), 16: (0, 4)}
    with contextlib.ExitStack() as es:
        self.m01, self.m01b = self.sb(es, "m01", [128, 3, 128], BF16)
        cx.op("dve", [self.masksb], [self.m01b], lambda e: e.tensor_scalar(out=self.m01[:], in0=self.masks[:], scalar1=0.0, scalar2=None, op0=ALU.is_equal))
        KTs = [self.sb(es, "a_KT%d" % i, [128, S], BF16) for i in range(2)]
        QTs = [self.sb(es, "a_QT%d" % i, [128, QTOK], BF16) for i in range(2)]
        Vps = [{d: self.sb(es, "a_V%d_%d" % (d, i), [128, d * vtiles[d][1], 2, VW], BF16) for d in (1, 4, 16)} for i in range(2)]
        acc, accb = self.sb(es, "a_acc", [65, 2, QTOK], F32)
        otst, otstb = self.sb(es, "a_otst", [64, 2, QTOK], BF16)
        PT = [self.sb(es, "a_PT%d" % i, [128, 128], BF16) for i in range(8)]
        STp = [self.ps(es, "a_ST%d" % i, [128, 512], F32) for i in range(3)]
        OTp = [self.ps(es, "a_OT%d" % i, [65, 512], F32) for i in range(4)]
        BCp = [self.ps(es, "a_BC%d" % i, [64, 512], F32) for i in range(1)] * 2
        cnt = 0
        ocnt = 0

        def load(hi):
            hp = hps[hi]
            KT, KTb = KTs[hi % 2]
            QT, QTb = QTs[hi % 2]
            cx.dma([self.ftb], [KTb], KT[:, :], self.ft[FB_KA + hp, :, :])
            cx.dma([self.ftb], [QTb], QT[:, :], self.ft[FB_QA + hp, :, QBASE:S])
            for d in (1, 4, 16):
                k0, nk = vtiles[d]
                V_, Vb_ = Vps[hi % 2][d]
                rows = self.va[d * 128 * k0:d * 128 * (k0 + nk), 2 * hp:2 * hp + 2, :].rearrange("(kk j dd) h w -> dd j kk h w", j=128, dd=d)
                for r in range(d):
                    cx.dma([self.vab], [Vb_], V_[:, r * nk:(r + 1) * nk, :, :], rows[r])

        load(0)
        for hi, hp in enumerate(hps):
            if hi + 1 < len(hps):
                load(hi + 1)
            KT, KTb = KTs[hi % 2]
            QT, QTb = QTs[hi % 2]
            Vp = Vps[hi % 2]
            cx.op("pool", [], [accb], lambda e: e.memset(acc[:], 0.0))
            aq = []

            def pv_emit(pend):
                (P_, Pb_, O_, Ob_, V_, Vb_, vidx, h, n, ki, av) = pend
                cx.op("pe", [Pb_, Vb_], [Ob_], lambda e: e.matmul(O_[:, 0:n], lhsT=V_[:, vidx, h, 0:65], rhs=P_[:, 0:n], start=(ki == 0), stop=(ki == 1)), sig=(ki == 1))
                if ki == 1:
                    cx.op("dve", [Ob_, accb], [accb], lambda e: e.tensor_tensor(out=av, in0=av, in1=O_[:, 0:n], op=ALU.add))

            for (d, r, m, j0) in pats:
                n = 128 - j0
                k0, nk = vtiles[d]
                V_, Vb_ = Vp[d]
                qs = d * (128 * m + j0) + r - QBASE
                for h in range(2):
                    pb = 64 * h
                    qv = QT[pb:pb + 64, qs:qs + d * (n - 1) + 1:d]
                    O_, Ob_ = OTp[ocnt % 4]
                    ocnt += 1
                    av = acc[:, h, qs:qs + d * (n - 1) + 1:d]
                    for ki, (kt, mi) in enumerate(((m - 1, 1), (m, 0))):
                        S_, Sb_ = STp[cnt % 2]
                        P_, Pb_ = PT[cnt % 8]
                        cnt += 1
                        ks = d * 128 * kt + r
                        kv = KT[pb:pb + 64, ks:ks + d * 127 + 1:d]
                        cx.op("pe", [KTb, QTb], [Sb_], lambda e: e.matmul(S_[:, 0:n], lhsT=kv, rhs=qv, start=True, stop=True))
                        cx.op("act", [Sb_, self.keybiasb], [Pb_], lambda e: e.activation(out=P_[:, 0:n], in_=S_[:, 0:n], func=AF.Exp, bias=self.keybias[:, d * kt:d * kt + 1], scale=0.125))
                        cx.op("pool" if cnt % 2 == 0 else "dve", [Pb_, self.m01b], [Pb_], lambda e: e.tensor_tensor(out=P_[:, 0:n], in0=P_[:, 0:n], in1=self.m01[:, mi, j0:128], op=ALU.mult))
                        aq.append((P_, Pb_, O_, Ob_, V_, Vb_, r * nk + (kt - k0), h, n, ki, av))
                        while len(aq) > 4:
                            pv_emit(aq.pop(0))
            while aq:
                pv_emit(aq.pop(0))
            for h in range(2):
                cx.op("dve", [accb], [accb], lambda e: e.tensor_scalar_max(out=acc[64:65, h, :], in0=acc[64:65, h, :], scalar1=1e-30))
                cx.op("dve", [accb], [accb], lambda e: e.reciprocal(out=acc[64:65, h, :], in_=acc[64:65, h, :]))
                for c0 in range(0, QTOK, 512):
                    w = min(512, QTOK - c0)
                    B_, Bb_ = BCp[(c0 // 512) % 2]
                    cx.op("pe", [accb, self.onesb], [Bb_], lambda e: e.matmul(B_[:, 0:w], lhsT=self.ones[64:65, 0:64], rhs=acc[64:65, h, c0:c0 + w], start=True, stop=True))
                    cx.op("dve", [accb, Bb_], [otstb], lambda e: e.tensor_tensor(out=otst[:, h, c0:c0 + w], in0=acc[0:64, h, c0:c0 + w], in1=B_[:, 0:w], op=ALU.mult))
            cx.dma([otstb], [self.otb], self.ot[2 * hp:2 * hp + 2, :, :].rearrange("h d t -> d h t"), otst[:, :, :])
        cx.barrier()


K.phase_attnA = _attnA


def _compress(self, es_persist):
    cx, nc = self.cx, self.nc
    self.kccT, self.kccTb = self.sb(es_persist, "kccT", [128, 512], BF16)
    self.vcc1, self.vcc1b = self.sb(es_persist, "vcc1", [128, 4, 2, VW], BF16)
    cx.op("pool", [], [self.kccTb], lambda e: e.memset(self.kccT[:], 0.0))
    cx.op("pool", [], [self.vcc1b], lambda e: e.memset(self.vcc1[:], 0.0))
    cx.op("pool", [self.vcc1b], [self.vcc1b], lambda e: e.memset(self.vcc1[:, :, :, 64:65], 1.0))
    with contextlib.ExitStack() as es:
        w1st, w1stb = self.sb(es, "c_w1st", [128, 16, 256], F32)
        w1 = [self.sb(es, "c_w1_%d" % i, [128, 16, 256], BF16) for i in range(2)]
        w2st, w2stb = self.sb(es, "c_w2st", [128, 2, 64], F32)
        w2 = [self.sb(es, "c_w2_%d" % i, [128, 2, 64], BF16) for i in range(2)]
        pest, pestb = self.sb(es, "c_pest", [128, 16], F32)
        pebf, pebfb = self.sb(es, "c_pebf", [128, 16], BF16)
        pebias, pebiasb = self.sb(es, "c_pebias", [128, 2, 2], F32)
        X2, X2b = self.sb(es, "c_X2", [128, S], BF16)
        hT, hTb = self.sb(es, "c_hT", [128, 2, 512], BF16)
        HP = [self.ps(es, "c_HP%d" % i, [128, 512], F32) for i in range(2)]
        OP, OPb = self.ps(es, "c_OP", [128, 512], F32)
        BP, BPb = self.ps(es, "c_BP", [128, 4], F32)
        cx.dma([], [pestb], pest[0:64, :], self.inp["pe_cmp"].rearrange("(c j) d -> j d c", j=2)[0], allow_slow_non_contiguous=True)
        cx.dma([], [pestb], pest[64:128, :], self.inp["pe_cmp"].rearrange("(c j) d -> j d c", j=2)[1], allow_slow_non_contiguous=True)
        cx.op("dve", [pestb], [pebfb], lambda e: e.tensor_copy(out=pebf[:], in_=pest[:]))
        for kv, (n1, n2) in enumerate((("w_ck1", "w_ck2"), ("w_cv1", "w_cv2"))):
            cx.dma([], [w1stb], w1st[:, :, :], self.inp[n1].rearrange("(c p) h -> p c h", p=128))
            cx.op("dve", [w1stb], [w1[kv][1]], lambda e: e.tensor_copy(out=w1[kv][0][:, 0:8, :], in_=w1st[:, 0:8, :]))
            cx.op("pool", [w1stb], [w1[kv][1]], lambda e: e.tensor_copy(out=w1[kv][0][:, 8:16, :], in_=w1st[:, 8:16, :]))
            cx.dma([], [w2stb], w2st[:, :, :], self.inp[n2].rearrange("(c p) h -> p c h", p=128))
            cx.op("dve", [w2stb], [w2[kv][1]], lambda e: e.tensor_copy(out=w2[kv][0][:], in_=w2st[:]))
            for hh in range(2):
                for c in range(16):
                    cx.op("pe", [w1[kv][1], pebfb], [BPb], lambda e: e.matmul(BP[:, 2 * kv + hh:2 * kv + hh + 1], lhsT=w1[kv][0][:, c, hh * 128:(hh + 1) * 128], rhs=pebf[:, c:c + 1],
                                                                          start=(c == 0), stop=(c == 15)), sig=(c == 15))
        cx.op("dve", [BPb], [pebiasb], lambda e: e.tensor_copy(out=pebias[:].rearrange("p a b -> p (a b)"), in_=BP[:, :]))
        for kv in range(2):
            blk = FB_KC if kv == 0 else FB_VC
            for g in range(2):
                cx.dma([self.ftb], [X2b], X2[0:64, :], self.ft[blk, g * 64:(g + 1) * 64, :])
                cx.dma([self.ftb], [X2b], X2[64:128, 0:S - 1], self.ft[blk, g * 64:(g + 1) * 64, 1:S])
                if kv == 0 and g == 0:
                    cx.op("pool", [X2b], [X2b], lambda e: e.memset(X2[64:128, S - 1:S], 0.0))
                for hh in range(2):
                    H_, Hb_ = HP[hh]
                    for c in range(16):
                        cx.op("pe", [w1[kv][1], X2b], [Hb_], lambda e: e.matmul(H_[:, 0:511], lhsT=w1[kv][0][:, c, hh * 128:(hh + 1) * 128], rhs=X2[:, 2 * c:2 * c + 16 * 510 + 1:16],
                                                                              start=(c == 0), stop=(c == 15)), sig=(c == 15))
                    cx.op("act", [Hb_, pebiasb], [hTb], lambda e: e.activation(out=hT[:, hh, 0:511], in_=H_[:, 0:511], func=AF.Gelu_apprx_tanh, bias=pebias[:, kv, hh:hh + 1]))
                if kv == 0:
                    for hh in range(2):
                        cx.op("pe", [hTb, w2[0][1]], [OPb], lambda e: e.matmul(OP[64 * g:64 * g + 64, 0:511], lhsT=w2[0][0][:, hh, :], rhs=hT[:, hh, 0:511], start=(hh == 0), stop=(hh == 1)), sig=(hh == 1))
                    cx.op("dve", [OPb], [self.kccTb], lambda e: e.tensor_copy(out=self.kccT[64 * g:64 * g + 64, 0:511], in_=OP[64 * g:64 * g + 64, 0:511]))
                else:
                    for nt in range(4):
                        w = 128 if nt < 3 else 127
                        for hh in range(2):
                            cx.op("pe", [hTb, w2[1][1]], [OPb], lambda e: e.matmul(OP[0:w, nt * 64:(nt + 1) * 64], lhsT=hT[:, hh, nt * 128:nt * 128 + w], rhs=w2[1][0][:, hh, :],
                                                                               start=(hh == 0), stop=(hh == 1)), sig=(hh == 1 and nt == 3))
                    for nt in range(4):
                        w = 128 if nt < 3 else 127
                        cx.op("dve", [OPb], [self.vcc1b], lambda e: e.tensor_copy(out=self.vcc1[0:w, nt, g, 0:64], in_=OP[0:w, nt * 64:(nt + 1) * 64]))
        cx.barrier()
        if "cmp" in self.debug:
            ap, b = self.dbg_out("kccT", [128, 512], BF16)
            cx.dma([self.kccTb], [b], ap[:, :], self.kccT[:])
            ap, b = self.dbg_out("vcc1", [128, 4, 2, VW], BF16)
            cx.dma([self.vcc1b], [b], ap[:, :, :, :], self.vcc1[:])


K.phase_compress = _compress


def _attnB(self, chunks=None):
    cx, nc = self.cx, self.nc
    chunks = list(range(NQ)) if chunks is None else chunks
    QBASE = Q0 * 128
    with contextlib.ExitStack() as es:
        if not hasattr(self, "gates"):
            gin = nc.dram_tensor("gates_in", [128, NQ, 24], F32, kind="ExternalInput").ap()
            self.gates, self.gatesb = self.sb(es, "gates", [128, NQ, 24], F32)
            cx.dma([], [self.gatesb], self.gates[:], gin[:, :, :])
        kslT2 = [self.sb(es, "b_kslT%d" % g, [128, NT // 2, 128], BF16) for g in range(2)]
        kwT, kwTb = self.sb(es, "b_kwT", [128, S], BF16)
        vsl1, vsl1b = self.sb(es, "b_vsl1", [128, NT, 2, VW], BF16)
        vw1, vw1b = self.sb(es, "b_vw1", [128, NT, 2, VW], BF16)
        QB2 = [self.sb(es, "b_QB%d" % g, [128, 4, QTOK], BF16) for g in range(2)]
        eall, eallb = self.sb(es, "b_eall", [128, NT * 128], BF16)
        ovl, ovlb = self.sb(es, "b_ovl", [128, 4, 128], BF16)
        realb, realbb = self.sb(es, "b_realb", [128, 128], F32)
        cmpm = [self.sb(es, "b_cmpm%d" % i, [128, 4, 128], BF16) for i in range(2)]
        selb = [self.sb(es, "b_selb%d" % i, [128, 128], F32) for i in range(2)]
        PT = [self.sb(es, "b_PT%d" % i, [128, 2, 512], BF16) for i in range(8)]
        small = [self.sb(es, "b_small%d" % i, [128, 64], F32) for i in range(2)]
        score = [self.sb(es, "b_score%d" % i, [128, 2, 128], F32) for i in range(2)]
        selbias = [self.sb(es, "b_selbias%d" % i, [128, 128], F32) for i in range(2)]
        selT = [self.sb(es, "b_selT%d" % i, [128, 128], BF16) for i in range(2)]
        OCs = [self.sb(es, "b_OCs%d" % i, [128, 4, VW], F32) for i in range(2)]
        tmpo = [self.sb(es, "b_tmpo%d" % i, [128, 3, 4, 64], F32) for i in range(2)]
        ob = [self.sb(es, "b_ob%d" % i, [128, 256], F32) for i in range(2)]
        obT = [self.sb(es, "b_obT%d" % i, [128, 2, 128], BF16) for i in range(2)]
        STq = es.enter_context(nc.psum_tensor("p_b_STq", [128, 4, 512], F32))
        stb = [cx.buf("stq0"), cx.buf("stq1")]
        for b_ in stb:
            b_.psum = True
        Mp = [self.ps(es, "b_M%d" % i, [128, 4, 128], F32) for i in range(2)]
        OS, OSb = self.ps(es, "b_OS", [128, 4, VW], F32)
        OCW, OCWb = self.ps(es, "b_OCW", [128, 4, VW], F32)

        for g in range(2):
            src = self.ft[FB_KSL, g * 64:(g + 1) * 64, :].rearrange("d (pr par k) -> par d pr k", par=2, k=128)
            for par in range(2):
                cx.dma([self.ftb], [kslT2[g][1]], kslT2[g][0][par * 64:(par + 1) * 64, :, :], src[par])
        cx.dma([self.ftb], [kwTb], kwT[:, :], self.ft[FB_KW, :, :])
        for q4 in range(4):
            r0, r1 = q4 * 16 * 128, (q4 + 1) * 16 * 128
            cx.dma([self.vslb], [vsl1b], vsl1[:, q4 * 16:(q4 + 1) * 16, :, :], self.vsl[r0:r1, :, :].rearrange("(t p) g w -> p t g w", p=128))
            cx.dma([self.vwnb], [vw1b], vw1[:, q4 * 16:(q4 + 1) * 16, :, :], self.vwn[r0:r1, :, :].rearrange("(t p) g w -> p t g w", p=128))
        for g in range(2):
            for hq in range(4):
                blk = FB_QB + (g * 4 + hq) // 2
                prow = ((g * 4 + hq) % 2) * 64
                for half in range(2):
                    cx.dma([self.ftb], [QB2[g][1]], QB2[g][0][64 * half:64 * half + 64, hq, :], self.ft[blk, prow:prow + 64, QBASE:S])
        cx.dma([], [eallb], eall[:, :], self.inp["eall"][:, :])
        cx.dma([], [ovlb], ovl[:, :, :], self.inp["overlap"][:, :, :])
        cx.dma([], [realbb], realb[:, :], self.inp["realb"][:, :])
        Msb = [self.sb(es, "b_Msb%d" % i, [128, 4, 128], BF16) for i in range(3)]
        cnt = 0
        mcnt = 0
        mscnt = 0
        units = [(ci, c, g) for ci, c in enumerate(chunks) for g in range(2)]
        state = {}

        def nxt():
            nonlocal cnt
            i = cnt % 2
            P_, Pb_ = PT[cnt % 8]
            cnt += 1
            return STq[:, 2 * i:2 * i + 2, :], stb[i], P_, Pb_

        def load_tables(ci):
            c = chunks[ci]
            cx.dma([], [cmpm[ci % 2][1]], cmpm[ci % 2][0][:, :, :], self.inp["cmpm"][:, c, :, :])
            cx.dma([], [selb[ci % 2][1]], selb[ci % 2][0][:, :], self.inp["selb"][:, c, :])

        def cmp_stage(idx):
            nonlocal mcnt
            ci, c, g = units[idx]
            sc = Q0 + c
            if g == 0 and ci + 1 < len(chunks):
                load_tables(ci + 1)
            cm_, cmb_ = cmpm[ci % 2]
            pb = 64 * g
            QB, QBb = QB2[g]
            qv = QB[pb:pb + 64, :, c * 128:(c + 1) * 128]
            nts = (8 * sc + 6) // 128 + 1
            pts = []
            for nt in range(nts):
                S2, Sb_, P_, Pb_ = nxt()
                cx.op("pe", [self.kccTb, QBb], [Sb_], lambda e: e.matmul(S2[:, 0, :], lhsT=self.kccT[pb:pb + 64, nt * 128:(nt + 1) * 128], rhs=qv, start=True, stop=True))
                cx.op("act", [Sb_], [Pb_], lambda e: e.activation(out=P_[:, 0, :], in_=S2[:, 0, :], func=AF.Exp, scale=0.125))
                pv4 = P_[:, 0, :].rearrange("p (h q) -> p h q", h=4)
                cx.op("dve", [Pb_, cmb_], [Pb_], lambda e: e.tensor_tensor(out=pv4, in0=pv4, in1=bcast(cm_[:, nt, :], 1, 4), op=ALU.mult))
                pts.append((P_, Pb_))
            IMP, IMPb = Mp[mcnt % 2]
            mcnt += 1
            for h in range(4):
                for nt in range(nts):
                    P_, Pb_ = pts[nt]
                    cx.op("pe", [Pb_, self.vcc1b], [OCWb], lambda e: e.matmul(OCW[:, h, 0:65], lhsT=P_[:, 0, h * 128:(h + 1) * 128], rhs=self.vcc1[:, nt, g, 0:65], start=(nt == 0 and h == 0), stop=(nt == nts - 1 and h == 3)), sig=(nt == nts - 1 and h == 3))
            for h in range(4):
                for nt in range(nts):
                    P_, Pb_ = pts[nt]
                    cx.op("pe", [Pb_, ovlb], [IMPb], lambda e: e.matmul(IMP[:, h, :], lhsT=P_[:, 0, h * 128:(h + 1) * 128], rhs=ovl[:, nt, :], start=(nt == 0 and h == 0), stop=(nt == nts - 1 and h == 3)), sig=(h == 3 and nt == nts - 1))
            state[idx] = (IMP, IMPb)

        def main_stage(idx):
            nonlocal mcnt, mscnt
            ci, c, g = units[idx]
            sc = Q0 + c
            sb_, sbb_ = selb[ci % 2]
            u = idx % 2
            pb = 64 * g
            QB, QBb = QB2[g]
            kT, kTb = kslT2[g]
            qv = QB[pb:pb + 64, :, c * 128:(c + 1) * 128]
            qlo = QB[0:64, :, c * 128:(c + 1) * 128]
            qhi = QB[64:128, :, c * 128:(c + 1) * 128]
            sm_, smb_ = small[u]
            IMP, IMPb = state.pop(idx)
            oc_, ocb_ = OCs[u]
            cx.op("dve", [OCWb], [ocb_], lambda e: e.tensor_copy(out=oc_[:, :, 0:65], in_=OCW[:, :, 0:65]))
            cx.op("dve", [ocb_], [smb_], lambda e: e.tensor_scalar_max(out=sm_[:, 0:4], in0=oc_[:, :, 64], scalar1=1e-30))
            cx.op("dve", [smb_], [smb_], lambda e: e.reciprocal(out=sm_[:, 0:4], in_=sm_[:, 0:4]))
            sco, scob = score[u]
            cx.op("dve", [IMPb, smb_], [scob], lambda e: e.tensor_scalar(out=sco[:, 0, :], in0=IMP[:, 0, :], scalar1=sm_[:, 0:1], scalar2=None, op0=ALU.mult))
            for h in range(1, 4):
                cx.op("dve", [IMPb, smb_, scob], [scob], lambda e: e.scalar_tensor_tensor(out=sco[:, 0, :], in0=IMP[:, h, :], scalar=sm_[:, h:h + 1], in1=sco[:, 0, :], op0=ALU.mult, op1=ALU.add))
            cx.op("dve", [scob, sbb_], [scob], lambda e: e.tensor_tensor(out=sco[:, 0, :], in0=sco[:, 0, :], in1=sb_[:, :], op=ALU.add))
            cx.op("dve", [scob], [smb_], lambda e: e.max(out=sm_[:, 32:40], in_=sco[:, 0, :]))
            cx.op("dve", [scob, smb_], [scob], lambda e: e.match_replace(out=sco[:, 1, :], in_to_replace=sm_[:, 32:40], in_values=sco[:, 0, :], imm_value=-3e38))
            cx.op("dve", [scob], [smb_], lambda e: e.max(out=sm_[:, 40:48], in_=sco[:, 1, :]))
            cx.op("dve", [scob, smb_], [scob], lambda e: e.tensor_scalar(out=sco[:, 1, :], in0=sco[:, 0, :], scalar1=sm_[:, 47:48], scalar2=None, op0=ALU.is_ge))
            sbi, sbib = selbias[u]
            cx.op("dve", [scob, realbb], [sbib], lambda e: e.tensor_tensor(out=sbi[:, :], in0=sco[:, 1, :], in1=realb[:, :], op=ALU.mult))
            queue = []

            def pv_emit(pend):
                (P_, Pb_, j, O_, Ob_, V_, Vb_, kt, first, last) = pend
                for h in range(4):
                    cx.op("pe", [Pb_, Vb_], [Ob_], lambda e: e.matmul(O_[:, h, 0:65], lhsT=P_[:, j, h * 128:(h + 1) * 128], rhs=V_[:, kt, g, 0:65], start=(first and h == 0), stop=(last and h == 3)), sig=(h == 3))

            def push(pend, lag=2):
                queue.append(pend)
                while len(queue) > lag:
                    pv_emit(queue.pop(0))

            kts = list(range(sc - 4, sc + 1))
            for ki, kt in enumerate(kts):
                S2, Sb_, P_, Pb_ = nxt()
                mi = 2 if ki == 0 else (0 if ki == 4 else None)
                cx.op("pe", [kwTb, QBb], [Sb_], lambda e: e.matmul(S2[:, 0, :], lhsT=kwT[pb:pb + 64, kt * 128:(kt + 1) * 128], rhs=qv, start=True, stop=(mi is None)), sig=(mi is None))
                if mi is not None:
                    cx.op("pe", [self.identb, self.masksb], [Sb_], lambda e: e.matmul(S2[:, 0, :], lhsT=self.ident[:, :], rhs=bcast(self.masks[:, mi, :], 1, 4), start=False, stop=True))
                cx.op("act", [Sb_, self.keybiasb], [Pb_], lambda e: e.activation(out=P_[:, 0, :], in_=S2[:, 0, :], func=AF.Exp, bias=self.keybias[:, kt:kt + 1], scale=0.125))
                push((P_, Pb_, 0, OCW, OCWb, vw1, vw1b, kt, ki == 0, ki == 4))
            Mt, Mtb = Mp[mcnt % 2]
            mcnt += 1
            cx.op("pe", [sbib, self.identfb], [Mtb], lambda e: e.transpose(out=Mt[:, 0, :], in_=sbi[:, :], identity=self.identf[:]))
            sT, sTb = selT[u]
            cx.op("dve", [Mtb], [sTb], lambda e: e.tensor_copy(out=sT[:, :], in_=Mt[:, 0, :]))
            npairs = (sc + 2) // 2
            for p in range(npairs):
                if p % 2 == 0:
                    M_, Mb_ = Mp[mcnt % 2]
                    mcnt += 1
                    Ms_, Msb_ = Msb[mscnt % 3]
                    mscnt += 1
                    nk4 = min(4, sc + 1 - 2 * p)
                    for j in range(nk4):
                        cx.op("pe", [eallb, sTb], [Mb_], lambda e: e.matmul(M_[:, j, :], lhsT=eall[:, (2 * p + j) * 128:(2 * p + j + 1) * 128], rhs=sT[:, :], start=True, stop=True), sig=(j == nk4 - 1))
                    if mscnt % 4 == 0:
                        cx.op("act", [Mb_], [Msb_], lambda e: e.copy(out=Ms_[:, 0:nk4, :], in_=M_[:, 0:nk4, :]))
                    else:
                        cx.op("dve", [Mb_], [Msb_], lambda e: e.tensor_copy(out=Ms_[:, 0:nk4, :], in_=M_[:, 0:nk4, :]))
                S2, Sb_, P_, Pb_ = nxt()
                nk = 2 if 2 * p + 1 <= sc else 1
                for j in range(nk):
                    kt = 2 * p + j
                    diag = (kt == sc)
                    cx.op("pe", [kTb, QBb], [Sb_], lambda e: e.matmul(S2[:, j, :], lhsT=kT[64 * j:64 * j + 64, p, :], rhs=(qlo if j == 0 else qhi), start=True, stop=(not diag)), sig=(j == nk - 1 and not diag))
                if 2 * p + nk - 1 == sc:
                    j = nk - 1
                    cx.op("pe", [self.identb, self.masksb], [Sb_], lambda e: e.matmul(S2[:, j, :], lhsT=self.ident[:, :], rhs=bcast(self.masks[:, 0, :], 1, 4), start=False, stop=True))
                cx.op("act", [Sb_], [Pb_], lambda e: e.activation(out=P_[:, 0:nk, :], in_=S2[:, 0:nk, :], func=AF.Exp, scale=0.125))
                pv5 = P_[:, 0:nk, :].rearrange("p k (h q) -> p k h q", h=4)
                m0 = (2 * p) % 4
                cx.op("dve", [Pb_, Msb_], [Pb_], lambda e: e.tensor_tensor(out=pv5, in0=pv5, in1=bcast(Ms_[:, m0:m0 + nk, :], 2, 4), op=ALU.mult))
                for j in range(nk):
                    kt = 2 * p + j
                    push((P_, Pb_, j, OS, OSb, vsl1, vsl1b, kt, kt == 0, kt == sc), lag=8)
            while queue:
                pv_emit(queue.pop(0))
            cx.op("dve", [OSb], [smb_], lambda e: e.tensor_scalar_max(out=sm_[:, 4:8], in0=OS[:, :, 64], scalar1=1e-30))
            cx.op("dve", [OCWb, smb_], [smb_], lambda e: e.tensor_scalar_max(out=sm_[:, 8:12], in0=OCW[:, :, 64], scalar1=1e-30))
            cx.op("dve", [smb_], [smb_], lambda e: e.reciprocal(out=sm_[:, 4:12], in_=sm_[:, 4:12]))
            for br in range(3):
                gv = self.gates[:, c, g * 12 + br:g * 12 + br + 10:3]
                cx.op("dve", [smb_, self.gatesb], [smb_], lambda e: e.tensor_tensor(out=sm_[:, 16 + 4 * br:20 + 4 * br], in0=sm_[:, 4 * br:4 * br + 4], in1=gv, op=ALU.mult))
            tm, tmb = tmpo[u]
            for br, (O_, Ob_) in enumerate(((oc_, ocb_), (OS, OSb), (OCW, OCWb))):
                cx.op("dve", [Ob_, smb_], [tmb], lambda e: e.tensor_tensor(out=tm[:, br, :, :], in0=O_[:, :, 0:64], in1=bcast(sm_[:, 16 + 4 * br:20 + 4 * br], 2, 64), op=ALU.mult))

        def tail_stage(idx):
            nonlocal mcnt
            ci, c, g = units[idx]
            u = idx % 2
            tm, tmb = tmpo[u]
            cx.op("pool", [tmb], [tmb], lambda e: e.tensor_tensor(out=tm[:, 0, :, :], in0=tm[:, 0, :, :], in1=tm[:, 1, :, :], op=ALU.add))
            o_, ob_ = ob[u]
            cx.op("pool", [tmb], [ob_], lambda e: e.tensor_tensor(out=o_[:, :].rearrange("p (h d) -> p h d", d=64), in0=tm[:, 0, :, :], in1=tm[:, 2, :, :], op=ALU.add))
            Mt, Mtb = Mp[mcnt % 2]
            mcnt += 1
            for j in range(2):
                cx.op("pe", [ob_, self.identfb], [Mtb], lambda e: e.transpose(out=Mt[:, j, :], in_=o_[:, j * 128:(j + 1) * 128], identity=self.identf[:]), sig=(j == 1))
            oT, oTb = obT[u]
            cx.op("act", [Mtb], [oTb], lambda e: e.copy(out=oT[:, :, :], in_=Mt[:, 0:2, :]))
            r0 = (8 + 4 * g) * 64
            otv = self.ot.rearrange("h d t -> (h d) t")
            for j in range(2):
                cx.dma([oTb], [self.otb], otv[r0 + j * 128:r0 + (j + 1) * 128, c * 128:(c + 1) * 128], oT[:, j, :])

        load_tables(0)
        cmp_stage(0)
        for idx in range(len(units)):
            main_stage(idx)
            if idx + 1 < len(units):
                cmp_stage(idx + 1)
            tail_stage(idx)
        cx.barrier()


K.phase_attnB = _attnB


def _bc_load(self, es, name, src_row):
    t, b = self.sb(es, name, [128, D], F32)
    src = bass.AP(src_row.tensor, src_row.offset, [[0, 128], [1, D]])
    self.cx.dma([], [b], t[:, :], src)
    return t, b


K.bc_load = _bc_load


def _post1(self, chunks=None):
    cx, nc = self.cx, self.nc
    chunks = list(range(NQ)) if chunks is None else chunks
    with contextlib.ExitStack() as es:
        wo, wob = self.sb(es, "p_wo", [128, 8, D], BF16)
        wst = [self.sb(es, "p_wost%d" % i, [128, 2, D], F32) for i in range(2)]
        wov = self.inp["w_o"].rearrange("(hp p) n -> p hp n", p=128)
        for q in range(4):
            st, stb = wst[q % 2]
            cx.dma([], [stb], st[:, :, :], wov[:, q * 2:(q + 1) * 2, :])
            cx.op("dve" if q % 2 == 0 else "pool", [stb], [wob], lambda e: e.tensor_copy(out=wo[:, q * 2:(q + 1) * 2, :], in_=st[:, :, :]))
        g1, g1b = self.bc_load(es, "p_ln1g", self.inp["ln1_g"])
        b1, b1b = self.bc_load(es, "p_ln1b", self.inp["ln1_b"])
        OTt = [self.sb(es, "p_OT%d" % i, [128, 8, 128], BF16) for i in range(2)]
        xt = [self.sb(es, "p_x%d" % i, [128, D], F32) for i in range(2)]
        tt = [self.sb(es, "p_t%d" % i, [128, D], F32) for i in range(2)]
        x1 = [self.sb(es, "p_x1%d" % i, [128, D], F32) for i in range(2)]
        lns = self.ln_stats(es, "lnq")
        Y = [self.ps(es, "p_Y%d" % i, [128, 512], F32) for i in range(4)]
        xk = self.inp["xk"].rearrange("(t p) d -> t p d", p=128)
        otv = self.ot.rearrange("(hp two) d t -> two d hp t", two=2)

        def load(ci):
            c = chunks[ci]
            i = ci % 2
            for two in range(2):
                cx.dma([self.otb], [OTt[i][1]], OTt[i][0][64 * two:64 * two + 64, :, :], otv[two][:, :, c * 128:(c + 1) * 128])
            cx.dma([], [xt[i][1]], xt[i][0][:, :], xk[Q0 + c, :, :])

        load(0)
        for ci, c in enumerate(chunks):
            i = ci % 2
            if ci + 1 < len(chunks):
                load(ci + 1)
            O_, Ob_ = OTt[i]
            x_, xb_ = xt[i]
            t_, tb_ = tt[i]
            for half in range(2):
                Y_, Yb_ = Y[(ci * 2 + half) % 4]
                for hp in range(8):
                    cx.op("pe", [Ob_, wob], [Yb_], lambda e: e.matmul(Y_[:, :], lhsT=O_[:, hp, :], rhs=wo[:, hp, half * 512:(half + 1) * 512], start=(hp == 0), stop=(hp == 7)), sig=(hp == 7))
                cx.op("dve", [Yb_, self.gbcb], [tb_], lambda e: e.tensor_tensor(out=t_[:, half * 512:(half + 1) * 512], in0=Y_[:, :], in1=self.gbc[:, 0, half * 512:(half + 1) * 512], op=ALU.mult))
            cx.op("dve", [xb_, tb_], [tb_], lambda e: e.scalar_tensor_tensor(out=t_[:, :], in0=x_[:, :], scalar=ALPHA, in1=t_[:, :], op0=ALU.mult, op1=ALU.add))
            mv, mvb = self.emit_ln_stats(lns[i], t_, tb_)
            x1_, x1b_ = x1[i]
            cx.op("act", [tb_, mvb], [x1b_], lambda e: e.activation(out=x1_[:, :], in_=t_[:, :], func=AF.Identity, bias=mv[:, 3:4], scale=mv[:, 2:3]))
            cx.op("dve", [x1b_, g1b], [x1b_], lambda e: e.tensor_tensor(out=x1_[:, :], in0=x1_[:, :], in1=g1[:, :], op=ALU.mult))
            cx.op("pool", [x1b_, b1b], [x1b_], lambda e: e.tensor_tensor(out=x1_[:, :], in0=x1_[:, :], in1=b1[:, :], op=ALU.add))
            cx.dma([x1b_], [self.x1b], self.x1d[c * 128:(c + 1) * 128, :], x1_[:, :])
        cx.barrier()


K.phase_post1 = _post1


def _post2(self, chunks=None):
    cx, nc = self.cx, self.nc
    chunks = list(range(NQ)) if chunks is None else chunks
    assert chunks[0] == 0
    body = chunks[1:]
    groups = [body[i:i + 2] for i in range(0, len(body), 2)]
    with contextlib.ExitStack() as es:
        wup, _ = self.sb(es, "f_wup", [128, 8, 2 * DFF], BF16)
        wupk = [cx.buf() for k in range(8)]
        wdn, wdnb = self.sb(es, "f_wdn", [128, NFC, D], BF16)
        if self.wcast_done:
            for kc in range(8):
                cx.dma([self.wupdb], [wupk[kc]], wup[:, kc, :], self.wupd[kc, :, :])
            for q in range(2):
                cx.dma([self.wdndb], [wdnb], wdn[:, q * 11:(q + 1) * 11, :], self.wdnd[:, q * 11:(q + 1) * 11, :])
        else:
            with contextlib.ExitStack() as es2:
                wst = [self.sb(es2, "f_wst%d" % i, [128, DFF], F32) for i in range(2)]
                wupv = self.inp["w_up"].rearrange("(k p) n -> k p n", p=128)
                n = 0
                for kc in range(8):
                    for hf in range(2):
                        st, stb = wst[n % 2]
                        cx.dma([], [stb], st[:, :], wupv[kc, :, hf * DFF:(hf + 1) * DFF])
                        cx.op("dve" if n % 2 == 0 else "pool", [stb], [wupk[kc]], lambda e: e.tensor_copy(out=wup[:, kc, hf * DFF:(hf + 1) * DFF], in_=st[:, :]))
                        n += 1
                wdnv = self.inp["w_down"].rearrange("(f p) n -> p f n", p=128)
                for q in range(11):
                    st, stb = wst[n % 2]
                    sv = st[:, 0:2 * D].rearrange("p (f n) -> p f n", n=D)
                    cx.dma([], [stb], sv, wdnv[:, 2 * q:2 * q + 2, :])
                    cx.op("dve" if n % 2 == 0 else "pool", [stb], [wdnb], lambda e: e.tensor_copy(out=wdn[:, 2 * q:2 * q + 2, :], in_=sv))
                    n += 1
                cx.barrier()
        g2, g2b = self.bc_load(es, "f_ln2g", self.inp["ln2_g"])
        b2, b2b = self.bc_load(es, "f_ln2b", self.inp["ln2_b"])
        cw, cwb = self.sb(es, "f_cw", [128, 3, NFC], F32)
        cb, cbb = self.sb(es, "f_cb", [128, NFC], F32)
        cx.dma([], [cwb], cw[:, :, :], self.inp["conv_w"].rearrange("k (f p) -> p k f", p=128), allow_slow_non_contiguous=True)
        cx.dma([], [cbb], cb[:, :], self.inp["conv_b"].rearrange("o (f p) -> p (o f)", p=128), allow_slow_non_contiguous=True)
        halo, halob = self.sb(es, "f_halo", [128, 1], F32)
        cx.dma([], [halob], halo[:, :], self.inp["halo"][:, :])
        Hs, _ = self.sb(es, "f_Hs", [128, NFC, 2], F32)
        Hk = [cx.buf() for f in range(NFC)]
        cx.op("pool", [], Hk, lambda e: e.memset(Hs[:], 0.0))
        NG = 2
        W = 128 * NG
        x1t = [self.sb(es, "f_x1%d" % i, [128, NG, D], F32) for i in range(2)]
        xn = [self.sb(es, "f_xn%d" % i, [128, D], BF16) for i in range(2)]
        uT = [self.sb(es, "f_uT%d" % i, [128, 8, W], BF16) for i in range(2)]
        uTk = [[cx.buf() for k in range(8)] for i in range(2)]
        Gf = [self.sb(es, "f_G%d" % i, [128, W + 2], F32) for i in range(3)]
        cv = [self.sb(es, "f_cv%d" % i, [128, 2, W], F32) for i in range(2)]
        gl = [self.sb(es, "f_gl%d" % i, [128, W], F32) for i in range(2)]
        hT, _ = self.sb(es, "f_hT", [128, NFC, W], BF16)
        hTk = [cx.buf() for f in range(NFC)]
        tt = [self.sb(es, "f_t%d" % i, [128, D], F32) for i in range(2)]
        lns = self.ln_stats(es, "lnf")
        lns2 = self.ln_stats(es, "lng")
        tp, _ = self.ps(es, "f_tp", [128, 8, 128], BF16)
        tpk1 = cx.buf()
        tpk1.psum = True
        tpk = [tpk1] * 8
        GV = [self.ps(es, "f_GV%d" % i, [128, 2, W], F32) for i in range(3)]
        Y = [self.ps(es, "f_Y%d" % i, [128, 512], F32) for i in range(4)]
        lcnt = [0]

        def stage_a(gi, cl):
            i = gi % 2
            x_, xb_ = x1t[i]
            for j, c in enumerate(cl):
                cx.dma([self.x1b], [xb_], x_[:, j, :], self.x1d[c * 128:(c + 1) * 128, :])
            for j, c in enumerate(cl):
                k = lcnt[0] % 2
                lcnt[0] += 1
                mv, mvb = self.emit_ln_stats(lns[k], x_[:, j, :], xb_)
                xn_, xnb_ = xn[k]
                cx.op("act", [xb_, mvb], [xnb_], lambda e: e.activation(out=xn_[:, :], in_=x_[:, j, :], func=AF.Identity, bias=mv[:, 3:4], scale=mv[:, 2:3]))
                self.emit_modT(xn_, xnb_, tp, tpk, uT[i][0][:, :, j * 128:(j + 1) * 128], uTk[i], 32, 24)

        def up_mm(gi, f, w, P_, Pb_, both):
            i = gi % 2
            uT_ = uT[i][0]
            for kc in range(8):
                cx.op("pe", [uTk[i][kc], wupk[kc]], [Pb_], lambda e: e.matmul(P_[:, 0, 0:w], lhsT=wup[:, kc, f * 128:(f + 1) * 128], rhs=uT_[:, kc, 0:w], start=(kc == 0), stop=(kc == 7)), sig=(kc == 7 and not both))
            if both:
                for kc in range(8):
                    cx.op("pe", [uTk[i][kc], wupk[kc]], [Pb_], lambda e: e.matmul(P_[:, 1, 0:w], lhsT=wup[:, kc, DFF + f * 128:DFF + (f + 1) * 128], rhs=uT_[:, kc, 0:w], start=(kc == 0), stop=(kc == 7)), sig=(kc == 7))

        stage_a(0, [0])
        if groups:
            stage_a(1, groups[0])
        cnt = 0
        for f in range(NFC):
            P_, Pb_ = GV[cnt % 3]
            G_, Gb_ = Gf[cnt % 3]
            cnt += 1
            up_mm(0, f, 128, P_, Pb_, False)
            cx.op("act", [Pb_], [Gb_], lambda e: e.copy(out=G_[:, 2:130], in_=P_[:, 0, 0:128]))
            cx.op("dve", [Gb_, halob], [Hk[f]], lambda e: e.tensor_scalar(out=Hs[:, f, :], in0=G_[:, 128:130], scalar1=halo[:, 0:1], scalar2=None, op0=ALU.mult))
        ecnt = 0
        prev = None

        def down_mm(pg, f):
            (pcl, pi) = pg
            for j, c in enumerate(pcl):
                for half in range(2):
                    Y_, Yb_ = Y[j * 2 + half]
                    cx.op("pe", [hTk[f], wdnb], [Yb_], lambda e: e.matmul(Y_[:, :], lhsT=hT[:, f, j * 128:(j + 1) * 128], rhs=wdn[:, f, half * 512:(half + 1) * 512], start=(f == 0), stop=(f == NFC - 1)), sig=(f == NFC - 1))

        def down_tail(pg, gidx):
            (pcl, pi) = pg
            x_, xb_ = x1t[pi]
            for j, c in enumerate(pcl):
                t_, tb_ = tt[j % 2]
                for half in range(2):
                    Y_, Yb_ = Y[j * 2 + half]
                    cx.op("dve", [Yb_, self.gbcb], [tb_], lambda e: e.tensor_tensor(out=t_[:, half * 512:(half + 1) * 512], in0=Y_[:, :], in1=self.gbc[:, 1, half * 512:(half + 1) * 512], op=ALU.mult))
                cx.op("dve", [xb_, tb_], [tb_], lambda e: e.scalar_tensor_tensor(out=t_[:, :], in0=x_[:, j, :], scalar=ALPHA, in1=t_[:, :], op0=ALU.mult, op1=ALU.add))
                mv2, mv2b = self.emit_ln_stats(lns2[j % 2], t_, tb_)
                cx.op("act", [tb_, mv2b], [tb_], lambda e: e.activation(out=t_[:, :], in_=t_[:, :], func=AF.Identity, bias=mv2[:, 3:4], scale=mv2[:, 2:3]))
                cx.op("pool", [tb_, g2b], [tb_], lambda e: e.tensor_tensor(out=t_[:, :], in0=t_[:, :], in1=g2[:, :], op=ALU.mult))
                cx.op("pool", [tb_, b2b], [tb_], lambda e: e.tensor_tensor(out=t_[:, :], in0=t_[:, :], in1=b2[:, :], op=ALU.add))
                cx.dma([tb_], [self.outbuf], self.out[(c - 1) * 128:c * 128, :], t_[:, :])

        for gi0, cl in enumerate(groups):
            gi = gi0 + 1
            i = gi % 2
            w = 128 * len(cl)
            pend = None

            def elem_a(pd):
                nonlocal ecnt
                (f, P_, Pb_, G_, Gb_) = pd
                cv_, cvb_ = cv[ecnt % 2]
                gl_, glb_ = gl[ecnt % 2]
                ecnt += 1
                cx.op("act", [Pb_], [Gb_], lambda e: e.copy(out=G_[:, 2:2 + w], in_=P_[:, 0, 0:w]))
                cx.op("act", [Hk[f]], [Gb_], lambda e: e.copy(out=G_[:, 0:2], in_=Hs[:, f, :]))
                cx.op("dve", [Gb_, cwb, cbb], [cvb_], lambda e: e.tensor_scalar(out=cv_[:, 0, 0:w], in0=G_[:, 2:2 + w], scalar1=cw[:, 2, f:f + 1], scalar2=cb[:, f:f + 1], op0=ALU.mult, op1=ALU.add))
                cx.op("dve", [Gb_, cwb, cvb_], [cvb_], lambda e: e.scalar_tensor_tensor(out=cv_[:, 1, 0:w], in0=G_[:, 1:1 + w], scalar=cw[:, 1, f:f + 1], in1=cv_[:, 0, 0:w], op0=ALU.mult, op1=ALU.add))
                cx.op("dve", [Gb_, cwb, cvb_], [cvb_], lambda e: e.scalar_tensor_tensor(out=cv_[:, 0, 0:w], in0=G_[:, 0:w], scalar=cw[:, 0, f:f + 1], in1=cv_[:, 1, 0:w], op0=ALU.mult, op1=ALU.add))
                cx.op("act", [Gb_], [Hk[f]], lambda e: e.copy(out=Hs[:, f, :], in_=G_[:, w:w + 2]))
                cx.op("act", [cvb_], [glb_], lambda e: e.activation(out=gl_[:, 0:w], in_=cv_[:, 0, 0:w], func=AF.Gelu_apprx_tanh))
                return (f, P_, Pb_, gl_, glb_)

            def elem_b(pd):
                (f, P_, Pb_, gl_, glb_) = pd
                cx.op("dve", [glb_, Pb_], [hTk[f]], lambda e: e.tensor_tensor(out=hT[:, f, 0:w], in0=gl_[:, 0:w], in1=P_[:, 1, 0:w], op=ALU.mult))

            pend_b = None
            for f in range(NFC):
                P_, Pb_ = GV[cnt % 3]
                G_, Gb_ = Gf[cnt % 3]
                cnt += 1
                up_mm(gi, f, w, P_, Pb_, True)
                if prev is not None:
                    down_mm(prev, f)
                nb = elem_a(pend) if pend is not None else None
                if pend_b is not None:
                    elem_b(pend_b)
                pend_b = nb
                pend = (f, P_, Pb_, G_, Gb_)
            if prev is not None:
                down_tail(prev, gi0 - 1)
            nb = elem_a(pend)
            if pend_b is not None:
                elem_b(pend_b)
            elem_b(nb)
            if gi0 + 1 < len(groups):
                stage_a(gi + 1, groups[gi0 + 1])
            prev = (cl, i)
        if prev is not None:
            for f in range(NFC):
                down_mm(prev, f)
            down_tail(prev, len(groups) - 1)
        cx.barrier()


K.phase_post2 = _post2
```
